# Optimizing a Trainium2 kernel written in Bass

```python
import math
import jax, jax.numpy as jnp
from jax import lax
import numpy as np

D_MODEL = 1024
BATCH = 4
SEQ = 4096
DEPTH = 4

N_MIXERS = 3
EXPAND = 2
D_INNER = EXPAND * D_MODEL
GLA_HEADS = 4
GLA_DK = D_MODEL // GLA_HEADS
GLA_DV = D_INNER // GLA_HEADS
GLA_RANK = 16
GLA_GATE_NORM = 16.0
GLA_CHUNK = 64
DSA_HEADS = 16
DSA_HEAD_DIM = D_INNER // DSA_HEADS
IDX_HEADS = 8
IDX_DIM = 64
TOPK_MAX = 256
Q_BLOCK = 128
ATTN_SCALE = DSA_HEAD_DIM ** -0.5
INDEX_SCALE = (IDX_HEADS ** -0.5) * (IDX_DIM ** -0.5)
CONV_WIDTH = 31
ROPE_THETA = 500000.0
DSA_ROT_DIM = DSA_HEAD_DIM // 4
IDX_ROT_DIM = IDX_DIM // 4
PLE_DIM = 256
ALPHA = (2 * DEPTH) ** 0.25
BETA = (8 * DEPTH) ** -0.25
NORM_EPS = 1e-5

GLA_IN = 2 * GLA_HEADS * GLA_DK + 2 * D_INNER + GLA_RANK
DSA_IN = 2 * D_INNER + 2 * DSA_HEAD_DIM + IDX_HEADS * IDX_DIM + IDX_DIM + IDX_HEADS
CONV_IN = 3 * D_INNER

kernel_name = 'hybrid_gla_dsa_conformer_deepnorm'


def layer_norm(x, g, b):
    xf = x.astype(jnp.float32)
    mu = jnp.mean(xf, axis=-1, keepdims=True)
    var = jnp.mean(jnp.square(xf - mu), axis=-1, keepdims=True)
    return ((xf - mu) * lax.rsqrt(var + NORM_EPS) * g + b).astype(x.dtype)


def rope_tables(positions, rot_dim):
    inv_freq = ROPE_THETA ** (-jnp.arange(0, rot_dim, 2, dtype=jnp.float32) / rot_dim)
    ang = positions.astype(jnp.float32)[..., None] * inv_freq
    return jnp.cos(ang), jnp.sin(ang)


def apply_partial_rope(x, cos, sin):
    half = cos.shape[-1]
    rot = 2 * half
    bshape = cos.shape[:2] + (1,) * (x.ndim - 3) + (half,)
    c = cos.reshape(bshape)
    s = sin.reshape(bshape)
    x1 = x[..., :half].astype(jnp.float32)
    x2 = x[..., half:rot].astype(jnp.float32)
    rotated = jnp.concatenate([x1 * c - x2 * s, x2 * c + x1 * s], axis=-1).astype(x.dtype)
    return jnp.concatenate([rotated, x[..., rot:]], axis=-1)


def gla_mixer(x, w_in, w_a2, b_a, gn_g, w_out):
    B, L, _ = x.shape
    n = L // GLA_CHUNK
    qk = GLA_HEADS * GLA_DK
    h = x @ w_in
    q = h[..., :qk]
    k = h[..., qk:2 * qk]
    v = h[..., 2 * qk:2 * qk + D_INNER]
    z = h[..., 2 * qk + D_INNER:2 * qk + 2 * D_INNER]
    a = h[..., 2 * qk + 2 * D_INNER:]
    g = jax.nn.log_sigmoid((a @ w_a2 + b_a).astype(jnp.float32)) / GLA_GATE_NORM

    def chunk(t, d):
        return t.astype(jnp.float32).reshape(B, n, GLA_CHUNK, GLA_HEADS, d)

    q, k, g, v = chunk(q, GLA_DK), chunk(k, GLA_DK), chunk(g, GLA_DK), chunk(v, GLA_DV)
    G = jnp.cumsum(g, axis=2)
    G_last = G[:, :, -1:]
    q_t = q * (GLA_DK ** -0.5) * jnp.exp(G)
    k_t = k * jnp.exp(-G)
    k_s = k * jnp.exp(G_last - G)
    pos = jnp.arange(GLA_CHUNK)
    tril = pos[:, None] >= pos[None, :]
    A = jnp.where(tril, jnp.einsum('bnihd,bnjhd->bnhij', q_t, k_t), 0.0)
    o_intra = jnp.einsum('bnhij,bnjhv->bnihv', A, v)

    def step(S, inp):
        q_c, k_c, v_c, decay = inp
        o_c = jnp.einsum('bihd,bhdv->bihv', q_c, S)
        S = S * decay[..., None] + jnp.einsum('bjhd,bjhv->bhdv', k_c, v_c)
        return S, o_c

    S0 = jnp.zeros((B, GLA_HEADS, GLA_DK, GLA_DV), jnp.float32)
    xs = (jnp.moveaxis(q_t, 1, 0), jnp.moveaxis(k_s, 1, 0), jnp.moveaxis(v, 1, 0),
          jnp.moveaxis(jnp.exp(G_last[:, :, 0]), 1, 0))
    _, o_inter = lax.scan(step, S0, xs)
    o = o_intra + jnp.moveaxis(o_inter, 0, 1)
    o = o * lax.rsqrt(jnp.mean(o * o, axis=-1, keepdims=True) + NORM_EPS)
    o = o.reshape(B, L, D_INNER) * gn_g
    return (o.astype(x.dtype) * jax.nn.silu(z)) @ w_out


def dsa_mixer(x, cos_h, sin_h, cos_i, sin_i, w_in, w_out):
    B, L, _ = x.shape
    topk = min(TOPK_MAX, L // 4)
    nb = L // Q_BLOCK
    h = x @ w_in
    o1 = D_INNER
    o2 = o1 + DSA_HEAD_DIM
    o3 = o2 + DSA_HEAD_DIM
    o4 = o3 + D_INNER
    o5 = o4 + IDX_HEADS * IDX_DIM
    o6 = o5 + IDX_DIM
    q = apply_partial_rope(h[..., :o1].reshape(B, L, DSA_HEADS, DSA_HEAD_DIM), cos_h, sin_h)
    k = apply_partial_rope(h[..., o1:o2], cos_h, sin_h)
    v = h[..., o2:o3]
    z = h[..., o3:o4]
    qi = apply_partial_rope(h[..., o4:o5].reshape(B, L, IDX_HEADS, IDX_DIM), cos_i, sin_i)
    ki = apply_partial_rope(h[..., o5:o6], cos_i, sin_i)
    wi = h[..., o6:]
    gather = jax.vmap(lambda arr, ids: arr[ids])
    s_pos = jnp.arange(L, dtype=jnp.int32)

    def to_blocks(t):
        return jnp.moveaxis(t.reshape((B, nb, Q_BLOCK) + t.shape[2:]), 1, 0)

    def attend_block(args):
        q_b, qi_b, wi_b, start = args
        t_pos = start + jnp.arange(Q_BLOCK, dtype=jnp.int32)
        rel = jax.nn.relu(jnp.einsum('bthd,bsd->bths', qi_b, ki).astype(jnp.float32))
        score = jnp.einsum('bths,bth->bts', rel, wi_b.astype(jnp.float32)) * INDEX_SCALE
        causal = s_pos[None, :] <= t_pos[:, None]
        score = jnp.where(causal[None], score, -jnp.inf)
        _, idx = lax.top_k(score, topk)
        valid = idx <= t_pos[None, :, None]
        k_sel = gather(k, idx)
        v_sel = gather(v, idx)
        logits = jnp.einsum('bthd,btjd->bthj', q_b, k_sel).astype(jnp.float32) * ATTN_SCALE
        logits = jnp.where(valid[:, :, None, :], logits, -jnp.inf)
        probs = jax.nn.softmax(logits, axis=-1).astype(v.dtype)
        return jnp.einsum('bthj,btjd->bthd', probs, v_sel)

    starts = jnp.arange(nb, dtype=jnp.int32) * Q_BLOCK
    o = lax.map(attend_block, (to_blocks(q), to_blocks(qi), to_blocks(wi), starts))
    o = jnp.moveaxis(o, 0, 1).reshape(B, L, D_INNER)
    return (o * jax.nn.silu(z)) @ w_out


def conformer_conv_mixer(x, w_in, dw_w, dw_b, ln_g, ln_b, w_out):
    h = x @ w_in
    a, gate, z = jnp.split(h, 3, axis=-1)
    u = a * jax.nn.sigmoid(gate)
    u = lax.conv_general_dilated(u, dw_w[:, None, :], window_strides=(1,),
                                 padding=[(CONV_WIDTH - 1, 0)],
                                 dimension_numbers=('NWC', 'WIO', 'NWC'),
                                 feature_group_count=D_INNER) + dw_b
    u = jax.nn.silu(layer_norm(u, ln_g, ln_b))
    return (u * jax.nn.silu(z)) @ w_out


def setup_inputs(seed: int = 0) -> dict:
    key = jax.random.key(seed)
    ks = jax.random.split(key, 24)
    n_gla = len(range(0, DEPTH, N_MIXERS))
    n_dsa = len(range(1, DEPTH, N_MIXERS))
    n_conv = len(range(2, DEPTH, N_MIXERS))

    def normal(k, shape, scale):
        return jax.random.normal(k, shape, jnp.float32) * scale

    x = normal(ks[0], (BATCH, SEQ, D_MODEL), 1.0)
    p = normal(ks[1], (DEPTH, BATCH, SEQ, PLE_DIM), 1.0)
    offsets = jax.random.randint(ks[2], (BATCH, 1), 0, 1024, dtype=jnp.int32)
    positions = offsets + jnp.arange(SEQ, dtype=jnp.int32)[None, :]
    return {
        'x': x,
        'p': p,
        'positions': positions,
        'gla_w_in': normal(ks[3], (n_gla, D_MODEL, GLA_IN), D_MODEL ** -0.5),
        'gla_w_a2': normal(ks[4], (n_gla, GLA_RANK, GLA_HEADS * GLA_DK), GLA_RANK ** -0.5),
        'gla_b_a': normal(ks[5], (n_gla, GLA_HEADS * GLA_DK), 0.1),
        'gla_gn_g': 1.0 + normal(ks[6], (n_gla, D_INNER), 0.02),
        'gla_w_out': normal(ks[7], (n_gla, D_INNER, D_MODEL), BETA * D_INNER ** -0.5),
        'dsa_w_in': normal(ks[8], (n_dsa, D_MODEL, DSA_IN), D_MODEL ** -0.5),
        'dsa_w_out': normal(ks[9], (n_dsa, D_INNER, D_MODEL), BETA * D_INNER ** -0.5),
        'conv_w_in': normal(ks[10], (n_conv, D_MODEL, CONV_IN), D_MODEL ** -0.5),
        'conv_dw_w': normal(ks[11], (n_conv, CONV_WIDTH, D_INNER), CONV_WIDTH ** -0.5),
        'conv_dw_b': normal(ks[12], (n_conv, D_INNER), 0.02),
        'conv_ln_g': 1.0 + normal(ks[13], (n_conv, D_INNER), 0.02),
        'conv_ln_b': normal(ks[14], (n_conv, D_INNER), 0.02),
        'conv_w_out': normal(ks[15], (n_conv, D_INNER, D_MODEL), BETA * D_INNER ** -0.5),
        'ln_g': 1.0 + normal(ks[16], (DEPTH, D_MODEL), 0.02),
        'ln_b': normal(ks[17], (DEPTH, D_MODEL), 0.02),
        'ple_w': normal(ks[18], (DEPTH, PLE_DIM, D_MODEL), 0.5 * PLE_DIM ** -0.5),
        'ple_gate_w': normal(ks[19], (DEPTH, D_MODEL, D_MODEL), D_MODEL ** -0.5),
    }


def reference(x, p, positions, gla_w_in, gla_w_a2, gla_b_a, gla_gn_g, gla_w_out,
              dsa_w_in, dsa_w_out, conv_w_in, conv_dw_w, conv_dw_b, conv_ln_g, conv_ln_b,
              conv_w_out, ln_g, ln_b, ple_w, ple_gate_w):
    cos_h, sin_h = rope_tables(positions, DSA_ROT_DIM)
    cos_i, sin_i = rope_tables(positions, IDX_ROT_DIM)
    for i in range(DEPTH):
        kind = i % N_MIXERS
        j = i // N_MIXERS
        if kind == 0:
            y = gla_mixer(x, gla_w_in[j], gla_w_a2[j], gla_b_a[j], gla_gn_g[j], gla_w_out[j])
        elif kind == 1:
            y = dsa_mixer(x, cos_h, sin_h, cos_i, sin_i, dsa_w_in[j], dsa_w_out[j])
        else:
            y = conformer_conv_mixer(x, conv_w_in[j], conv_dw_w[j], conv_dw_b[j],
                                     conv_ln_g[j], conv_ln_b[j], conv_w_out[j])
        x = layer_norm(ALPHA * x + y, ln_g[i], ln_b[i])
        x = x + (p[i] @ ple_w[i]) * jax.nn.sigmoid(x @ ple_gate_w[i])
    return x
```

```python
import math
import os
from contextlib import ExitStack
import numpy as np
import concourse.bass as bass
import concourse.mybir as mybir
from concourse.bass_utils import run_bass_kernel_spmd

F32 = mybir.dt.float32
BF16 = mybir.dt.bfloat16
I32 = mybir.dt.int32
ALU = mybir.AluOpType
AF = mybir.ActivationFunctionType

D = 1024
SEQ = 4096
DEPTH = 4
DI = 2048
GLA_IN = 6160
DSA_IN = 4936
CONV_IN = 6144
ALPHA = (2 * DEPTH) ** 0.25
EPS = 1e-5
ROPE_THETA = 500000.0
ATTN_SCALE = 128 ** -0.5
CONV_W = 31


NOSYNC_ENGS = set(os.environ.get("NOSYNC_ENGS", "").split(","))


class Op:
    __slots__ = ("eng", "fn", "deps", "idx", "chan", "ticket", "needs_inc", "nosame")


class Prog:
    PHYS = {"pe": "tensor", "act": "scalar", "dve": "vector", "pool": "gpsimd",
            "sp": "sync", "actq": "scalar", "poolq": "gpsimd"}

    def __init__(self, nc, same_engine_raw=True):
        self.nc = nc
        self.ops = []
        self.lastw = {}
        self.readers = {}
        self.same_engine_raw = same_engine_raw
        self.last_on = {}
        self.dma_since = []
        self.barrier_deps = None
        self.barrier_seen = set()
        self.final_waits = []
        self.groups = {}

    def add(self, eng, fn, reads=(), writes=(), chan=None, group=None, nosame=False):
        op = Op()
        op.nosame = nosame
        op.eng = eng
        op.fn = fn
        op.idx = len(self.ops)
        op.chan = chan
        op.ticket = None
        op.needs_inc = False
        writes = list(writes) + [r + "_rd" for r in reads if r.startswith("pb")]
        deps = {}
        for r in reads:
            for w in self.lastw.get(r, ()):
                deps[w] = "raw"
        for w_ in writes:
            lw = self.lastw.get(w_, ())
            same_group = group is not None and lw and all(self.groups.get(w) == group for w in lw)
            if not same_group:
                for w in lw:
                    if w not in deps:
                        deps[w] = "waw"
            for rd in self.readers.get(w_, ()):
                if rd not in deps:
                    deps[rd] = "war"
        for r in reads:
            self.readers.setdefault(r, []).append(op.idx)
        for w_ in writes:
            lw = self.lastw.get(w_, ())
            if group is not None and lw and all(self.groups.get(w) == group for w in lw):
                self.lastw[w_] = list(lw) + [op.idx]
            else:
                self.lastw[w_] = [op.idx]
                self.readers[w_] = []
        if group is not None:
            self.groups[op.idx] = group
        ph = self.PHYS[eng]
        if self.barrier_deps is not None and ph not in self.barrier_seen:
            self.barrier_seen.add(ph)
            for d in self.barrier_deps:
                if d not in deps:
                    deps[d] = "raw"
        op.deps = deps
        self.ops.append(op)
        self.last_on[ph] = op.idx
        if chan is not None:
            self.dma_since.append(op.idx)
        return op

    def barrier(self):
        deps = set(self.last_on.values()) | set(self.dma_since)
        self.barrier_deps = deps
        self.barrier_seen = set()
        self.dma_since = []
        self.lastw = {}
        self.readers = {}

    def emit(self, stack):
        nc = self.nc
        ops = self.ops
        phys = [self.PHYS[o.eng] for o in ops]
        is_dma = [o.chan is not None for o in ops]
        need = [[] for _ in ops]
        for o in ops:
            for d, kind in o.deps.items():
                if is_dma[d]:
                    need[o.idx].append(d)
                elif phys[d] == phys[o.idx]:
                    if is_dma[o.idx]:
                        need[o.idx].append(d)
                    elif kind == "raw" and self.same_engine_raw and o.eng != "pe" and o.eng not in NOSYNC_ENGS and not o.nosame:
                        need[o.idx].append(d)
                else:
                    need[o.idx].append(d)
        for o in ops:
            for d in need[o.idx]:
                ops[d].needs_inc = True
            if is_dma[o.idx]:
                o.needs_inc = True
        sems = {}
        counts = {}
        for o in ops:
            if not o.needs_inc:
                continue
            key = ("d_" + o.chan) if is_dma[o.idx] else ("e_" + phys[o.idx])
            if key not in sems:
                sems[key] = stack.enter_context(nc.semaphore("s_" + key))
                counts[key] = 0
            counts[key] += 16 if is_dma[o.idx] else 1
            o.ticket = (key, counts[key])
        self.n_sems = len(sems)
        block = stack.enter_context(nc.Block())
        streams = {}
        for o in ops:
            streams.setdefault(phys[o.idx], []).append(o)
        final = self.final_waits

        def run_stream(e, lst):
            waited = {}
            for o in lst:
                req = {}
                for d in need[o.idx]:
                    key, val = ops[d].ticket
                    if req.get(key, 0) < val:
                        req[key] = val
                for key, val in req.items():
                    if waited.get(key, 0) >= val:
                        continue
                    e.wait_ge(sems[key], val)
                    waited[key] = val
                ins = o.fn(e)
                if o.needs_inc:
                    key, val = o.ticket
                    ins.then_inc(sems[key], 16 if is_dma[o.idx] else 1)

        def mk(name):
            def f(e):
                run_stream(e, streams.get(name, []))
                if name == "gpsimd":
                    for ch in final:
                        key = "d_" + ch
                        if key in sems:
                            e.wait_ge(sems[key], counts[key])
            return f

        block.sync(mk("sync"))
        block.scalar(mk("scalar"))
        block.vector(mk("vector"))
        block.gpsimd(mk("gpsimd"))
        block.tensor(mk("tensor"))


class Arena:
    def __init__(self, ap, nwords):
        self.ap = ap
        self.n = nwords
        self.off = 0

    def alloc(self, nelem, dt):
        words = nelem if dt in (F32, I32) else (nelem + 1) // 2
        words = (words + 15) // 16 * 16
        assert self.off + words <= self.n, ("SBUF arena overflow", self.off, words, self.n)
        a = self.ap[:, self.off:self.off + words]
        self.off += words
        if dt != F32:
            a = a.bitcast(dt)
        return a[:, 0:nelem]

    def mark(self):
        return self.off

    def release(self, m):
        self.off = m


def build_program(NT, layers, dbg=False):
    L = NT * 128
    nc = bass.Bass("TRN2", target_bir_lowering=False)

    def din(name, shape, dt=F32):
        return nc.dram_tensor(name, list(shape), dt, kind="ExternalInput").ap()

    x_in = din("x", [L, D])
    p_in = din("p", [DEPTH, L, 256])
    pos_in = din("pos", [L], I32)
    gla_w_in = din("gla_w_in", [2, D, GLA_IN])
    gla_w_a2 = din("gla_w_a2", [2, 16, D])
    gla_b_a = din("gla_b_a", [2, D])
    gla_gn_g = din("gla_gn_g", [2, DI])
    gla_w_out = din("gla_w_out", [2, DI, D])
    dsa_w_in = din("dsa_w_in", [1, D, DSA_IN])
    dsa_w_out = din("dsa_w_out", [1, DI, D])
    conv_w_in = din("conv_w_in", [1, D, CONV_IN])
    conv_dw_w = din("conv_dw_w", [1, CONV_W, DI])
    conv_dw_b = din("conv_dw_b", [1, DI])
    conv_ln_g = din("conv_ln_g", [1, DI])
    conv_ln_b = din("conv_ln_b", [1, DI])
    conv_w_out = din("conv_w_out", [1, DI, D])
    ln_g = din("ln_g", [DEPTH, D])
    ln_b = din("ln_b", [DEPTH, D])
    ple_w = din("ple_w", [DEPTH, 256, D])
    ple_gate_w = din("ple_gate_w", [DEPTH, D, D])
    c_ident = din("c_ident", [128, 128])
    c_umat = din("c_umat", [128, 128])
    c_lmat = din("c_lmat", [128, 128])
    c_tri = din("c_tri", [128, 128])
    c_rope = din("c_rope", [32, 8])
    c_neg = din("c_neg", [128, 128])
    c_pw2 = din("c_pw2", [128, 18])
    c_sel = din("c_sel", [128, 512])
    dsa_w_rot = din("dsa_w_rot", [D, 688])
    out = nc.dram_tensor("out", [L, D], F32, kind="ExternalOutput").ap()
    xs = nc.dram_tensor("xs", [L, D], F32).ap()
    ogs = nc.dram_tensor("ogs", [L, DI], BF16).ap()
    ogTs = nc.dram_tensor("ogTs", [DI, L], BF16).ap()

    st = ExitStack()
    ARENA_WORDS = 52000
    arena_t = st.enter_context(nc.sbuf_tensor("arena", [128, ARENA_WORDS], F32))
    A = Arena(arena_t[:], ARENA_WORDS)
    psum_t = st.enter_context(nc.psum_tensor("ps", [128, 4096], F32))
    psum = psum_t[:]
    P = Prog(nc)

    def bank(i, n=1):
        return psum[:, i * 512:(i + n) * 512]

    pstate = {"ptr": 0}
    gctr = [0]

    def palloc(n):
        if pstate["ptr"] + n > 8:
            pstate["ptr"] = 0
        b = pstate["ptr"]
        pstate["ptr"] += n
        return b, ["pb%d" % i for i in range(b, b + n)]

    ident_f = A.alloc(128, F32)
    ident_b = A.alloc(128, BF16)
    umat = A.alloc(128, F32)
    lmat = A.alloc(128, F32)
    tri = A.alloc(128, F32)
    P.add("sp", lambda e: e.dma_start(out=ident_f, in_=c_ident), writes=["ident_f"], chan="c0")
    P.add("sp", lambda e: e.dma_start(out=umat, in_=c_umat), writes=["umat"], chan="c1")
    P.add("sp", lambda e: e.dma_start(out=lmat, in_=c_lmat), writes=["lmat"], chan="c2")
    P.add("sp", lambda e: e.dma_start(out=tri, in_=c_tri), writes=["tri"], chan="c3")
    P.add("dve", lambda e: e.tensor_copy(out=ident_b, in_=ident_f), reads=["ident_f"], writes=["ident_b"])
    persist_mark = A.mark()

    def load_w(dst, src_rows, ncols, kchunks, name, c0=0):
        step = 2048
        gctr[0] += 1
        grp = "g%d" % gctr[0]
        for k in range(kchunks):
            for cs in range(0, ncols, step):
                ce = min(ncols, cs + step)
                P.add("poolq",
                      lambda e, k=k, cs=cs, ce=ce: e.dma_start(
                          out=dst[:, k, cs:ce], in_=src_rows[k * 128:(k + 1) * 128, c0 + cs:c0 + ce]),
                      writes=[name], chan="w_" + name, group=grp)

    tb_state = {"i": 0}

    def transpose_to(dst, src_bf, nblk, rname, wname, evac="dve", banks=None):
        done = 0
        while done < nblk:
            n = min(8, nblk - done)
            if banks is None:
                b, br = palloc(1)
            else:
                b = banks[tb_state["i"] % len(banks)]
                tb_state["i"] += 1
                br = ["pb%d" % b]
            pv = bank(b).bitcast(BF16)
            for k in range(n):
                P.add("pe", lambda e, k=k, done=done, pv=pv: e.transpose(
                    out=pv[:, k * 128:(k + 1) * 128], in_=src_bf[:, (done + k) * 128:(done + k + 1) * 128],
                    identity=ident_b), reads=[rname, "ident_b"], writes=br)
            P.add(evac, lambda e, n=n, done=done, pv=pv: e.tensor_copy(
                out=dst[:, done:done + n, :], in_=pv[:, 0:n * 128].rearrange("p (k t) -> p k t", k=n))
                if evac != "act" else e.activation(
                out=dst[:, done:done + n, :], in_=pv[:, 0:n * 128].rearrange("p (k t) -> p k t", k=n), func=AF.Copy),
                reads=br, writes=[wname])
            done += n

    def phase_b(li, x_src, x_dst, w_out_d, og_feature_major, zgate_w=None):
        P.barrier()
        A.release(persist_mark)
        wout = A.alloc(16 * 1024, BF16).rearrange("p (k n) -> p k n", k=16)
        wgate = A.alloc(8 * 1024, BF16).rearrange("p (k n) -> p k n", k=8)
        wple = A.alloc(2 * 1024, BF16).rearrange("p (k n) -> p k n", k=2)
        lng = A.alloc(1024, F32)
        lnb = A.alloc(1024, F32)
        load_w(wout, w_out_d, 1024, 16, "wout")
        load_w(wgate, ple_gate_w[li], 1024, 8, "wgate")
        load_w(wple, ple_w[li], 1024, 2, "wple")
        P.add("sp", lambda e: e.dma_start(out=lng, in_=ln_g[li].partition_broadcast(128)), writes=["lng"], chan="lng")
        P.add("sp", lambda e: e.dma_start(out=lnb, in_=ln_b[li].partition_broadcast(128)), writes=["lnb"], chan="lnb")
        NB = 2
        NB0 = 3
        if zgate_w is not None:
            wz = A.alloc(8 * 2048, BF16).rearrange("p (k n) -> p k n", k=8)
            load_w(wz, zgate_w, 2048, 8, "wz", 2304)
            xzb = [A.alloc(1024, BF16) for _ in range(NB0)]
            xzT = [A.alloc(1024, BF16).rearrange("p (k t) -> p k t", k=8) for _ in range(NB0)]
            szT = [A.alloc(2048, BF16).rearrange("p (c t) -> p c t", c=16) for _ in range(NB0)]
        ogt = [A.alloc(2048, BF16) for _ in range(NB0)]
        ogT = [A.alloc(2048, BF16).rearrange("p (k t) -> p k t", k=16) for _ in range(NB0)]
        xt = [A.alloc(1024, F32) for _ in range(NB0)]
        pt = [A.alloc(256, F32) for _ in range(NB)]
        s_ = [A.alloc(1024, F32) for _ in range(NB)]
        xn = [A.alloc(1024, F32) for _ in range(NB)]
        xnb = [A.alloc(1024, BF16) for _ in range(NB)]
        xnT = [A.alloc(1024, BF16).rearrange("p (k t) -> p k t", k=8) for _ in range(NB)]
        ptb = [A.alloc(256, BF16) for _ in range(NB)]
        pT = [A.alloc(256, BF16).rearrange("p (k t) -> p k t", k=2) for _ in range(NB)]
        sig = [A.alloc(1024, F32) for _ in range(NB)]
        xo = [A.alloc(1024, F32) for _ in range(NB)]
        stats = [A.alloc(16, F32) for _ in range(NB)]
        ctx = {}

        def stage0(t):
            w = t % NB0
            wfx = "w%d" % w
            r0 = t * 128
            if og_feature_major:
                P.add("sp", lambda e, w=w, r0=r0: e.dma_start(
                    out=ogT[w], in_=ogTs[:, r0:r0 + 128].rearrange("(k f) t -> f k t", f=128)),
                    reads=["ogTs"], writes=["ogT" + wfx], chan="ogT" + wfx)
            else:
                P.add("sp", lambda e, w=w, r0=r0: e.dma_start(out=ogt[w], in_=ogs[r0:r0 + 128, :]),
                      reads=["ogs"], writes=["ogt" + wfx], chan="ogt" + wfx)
                transpose_to(ogT[w], ogt[w], 16, "ogt" + wfx, "ogT" + wfx, evac="act", banks=[6, 7])
            P.add("sp", lambda e, w=w, r0=r0: e.dma_start(out=xt[w], in_=x_src[r0:r0 + 128, :]),
                  reads=["xsrc%d" % t], writes=["xt" + wfx], chan="xt" + wfx)
            if zgate_w is not None:
                P.add("act", lambda e, w=w: e.activation(out=xzb[w], in_=xt[w], func=AF.Copy), reads=["xt" + wfx], writes=["xzb" + wfx])
                transpose_to(xzT[w], xzb[w], 8, "xzb" + wfx, "xzT" + wfx, evac="dve", banks=[6, 7])
                for g8 in range(2):
                    zb = 2 + 2 * g8
                    zbr = ["pb%d" % zb, "pb%d" % (zb + 1)]
                    for hh in range(8):
                        c = g8 * 8 + hh
                        for k in range(8):
                            P.add("pe", lambda e, w=w, zb=zb, hh=hh, c=c, k=k: e.matmul(bank(zb, 2)[:, hh * 128:(hh + 1) * 128], lhsT=wz[:, k, c * 128:(c + 1) * 128],
                                                                                   rhs=xzT[w][:, k, :], start=(k == 0), stop=(k == 7)),
                                  reads=["xzT" + wfx, "wz"], writes=[zbr[hh // 4]])
                    P.add("act", lambda e, w=w, g8=g8, zb=zb: e.activation(out=szT[w][:, g8 * 8:(g8 + 1) * 8, :].rearrange("p c t -> p (c t)"), in_=bank(zb, 2), func=AF.Silu),
                          reads=zbr, writes=["szT" + wfx])
                P.add("pool", lambda e, w=w: e.tensor_tensor(out=ogT[w].rearrange("p k t -> p (k t)"), in0=ogT[w].rearrange("p k t -> p (k t)"),
                                                             in1=szT[w].rearrange("p c t -> p (c t)"), op=ALU.mult),
                      reads=["ogT" + wfx, "szT" + wfx], writes=["ogT" + wfx])

        def stage1a(t):
            u = t % NB
            sfx = "%d" % u
            r0 = t * 128
            w = t % NB0
            wfx = "w%d" % w
            P.add("sp", lambda e, u=u, r0=r0: e.dma_start(out=pt[u], in_=p_in[li, r0:r0 + 128, :]),
                  writes=["pt" + sfx], chan="pt" + sfx)
            yb, ybr = 0, ["pb0", "pb1"]
            for n in range(2):
                for k in range(16):
                    P.add("pe", lambda e, u=u, n=n, k=k, yb=yb: e.matmul(
                        bank(yb + n), lhsT=ogT[t % NB0][:, k, :], rhs=wout[:, k, n * 512:(n + 1) * 512],
                        start=(k == 0), stop=(k == 15)), reads=["ogT" + wfx, "wout"], writes=[ybr[n]])
            ctx[('y', t)] = (yb, ybr)

        def stage1b(t):
            u = t % NB
            sfx = "%d" % u
            r0 = t * 128
            yb, ybr = ctx[('y', t)]
            w = t % NB0
            wfx = "w%d" % w
            P.add("dve", lambda e, u=u, yb=yb, w=w: e.scalar_tensor_tensor(
                out=s_[u], in0=xt[w], scalar=float(ALPHA), in1=bank(yb, 2), op0=ALU.mult, op1=ALU.add),
                reads=["xt" + wfx] + ybr, writes=["s" + sfx])
            for c in range(2):
                P.add("dve", lambda e, u=u, c=c: e.bn_stats(out=stats[u][:, c * 6:(c + 1) * 6], in_=s_[u][:, c * 512:(c + 1) * 512]),
                      reads=["s" + sfx], writes=["st" + sfx])
            P.add("dve", lambda e, u=u: e.bn_aggr(out=stats[u][:, 12:14], in_=stats[u][:, 0:12]),
                  reads=["st" + sfx], writes=["st" + sfx])
            P.add("dve", lambda e, u=u: e.tensor_scalar(out=stats[u][:, 14:15], in0=stats[u][:, 13:14], scalar1=float(EPS), scalar2=None, op0=ALU.add),
                  reads=["st" + sfx], writes=["st" + sfx])
            P.add("act", lambda e, u=u: e.activation(out=stats[u][:, 14:15], in_=stats[u][:, 14:15], func=AF.Sqrt),
                  reads=["st" + sfx], writes=["st" + sfx])
            P.add("dve", lambda e, u=u: e.reciprocal(out=stats[u][:, 15:16], in_=stats[u][:, 14:15]),
                  reads=["st" + sfx], writes=["st" + sfx])
            P.add("dve", lambda e, u=u: e.tensor_scalar(out=xn[u], in0=s_[u], scalar1=stats[u][:, 12:13], scalar2=stats[u][:, 15:16],
                                                        op0=ALU.subtract, op1=ALU.mult),
                  reads=["s" + sfx, "st" + sfx], writes=["xn" + sfx])
            P.add("dve", lambda e, u=u: e.tensor_tensor(out=xn[u], in0=xn[u], in1=lng, op=ALU.mult),
                  reads=["xn" + sfx, "lng"], writes=["xn" + sfx])
            P.add("dve", lambda e, u=u: e.tensor_tensor(out=xn[u], in0=xn[u], in1=lnb, op=ALU.add),
                  reads=["xn" + sfx, "lnb"], writes=["xn" + sfx])
            P.add("act", lambda e, u=u: e.activation(out=xnb[u], in_=xn[u], func=AF.Copy),
                  reads=["xn" + sfx], writes=["xnb" + sfx])

        def stage2a(t):
            u = t % NB
            sfx = "%d" % u
            r0 = t * 128
            transpose_to(xnT[u], xnb[u], 8, "xnb" + sfx, "xnT" + sfx, evac="dve", banks=[6, 7])
            gb, gbr = 2, ["pb2", "pb3"]
            for n in range(2):
                for k in range(8):
                    P.add("pe", lambda e, u=u, n=n, k=k, gb=gb: e.matmul(
                        bank(gb + n), lhsT=xnT[u][:, k, :], rhs=wgate[:, k, n * 512:(n + 1) * 512],
                        start=(k == 0), stop=(k == 7)), reads=["xnT" + sfx, "wgate"], writes=[gbr[n]])
            P.add("act", lambda e, u=u, gb=gb: e.activation(out=sig[u], in_=bank(gb, 2), func=AF.Sigmoid),
                  reads=gbr, writes=["sig" + sfx])
            P.add("act", lambda e, u=u: e.activation(out=ptb[u], in_=pt[u], func=AF.Copy),
                  reads=["pt" + sfx], writes=["ptb" + sfx])
            transpose_to(pT[u], ptb[u], 2, "ptb" + sfx, "pT" + sfx, evac="dve", banks=[6, 7])
            lb, lbr = 4, ["pb4", "pb5"]
            for n in range(2):
                for k in range(2):
                    P.add("pe", lambda e, u=u, n=n, k=k, lb=lb: e.matmul(
                        bank(lb + n), lhsT=pT[u][:, k, :], rhs=wple[:, k, n * 512:(n + 1) * 512],
                        start=(k == 0), stop=(k == 1)), reads=["pT" + sfx, "wple"], writes=[lbr[n]])
            ctx[('l', t)] = (lb, lbr)

        def stage2b(t):
            u = t % NB
            sfx = "%d" % u
            r0 = t * 128
            lb, lbr = ctx[('l', t)]
            P.add("dve", lambda e, u=u, lb=lb: e.tensor_tensor(out=xo[u], in0=bank(lb, 2), in1=sig[u], op=ALU.mult),
                  reads=lbr + ["sig" + sfx], writes=["xo" + sfx])
            P.add("dve", lambda e, u=u: e.tensor_tensor(out=xo[u], in0=xo[u], in1=xn[u], op=ALU.add),
                  reads=["xo" + sfx, "xn" + sfx], writes=["xo" + sfx])
            P.add("poolq", lambda e, u=u, r0=r0: e.dma_start(out=x_dst[r0:r0 + 128, :], in_=xo[u]),
                  reads=["xo" + sfx], writes=["xdst%d" % t], chan="xo" + sfx)


        stage0(0)
        if NT > 1:
            stage0(1)
        stage1a(0)
        stage1b(0)
        for t in range(NT):
            if t + 2 < NT:
                stage0(t + 2)
            if t + 1 < NT:
                stage1a(t + 1)
            stage2a(t)
            if t + 1 < NT:
                stage1b(t + 1)
            stage2b(t)

    def gla_a(j, x_src):
        P.barrier()
        A.release(persist_mark)
        w_in = gla_w_in[j]
        wqk = A.alloc(8 * 2048, BF16).rearrange("p (k n) -> p k n", k=8)
        wvz = A.alloc(8 * 4096, BF16).rearrange("p (k n) -> p k n", k=8)
        wa = A.alloc(8 * 16, BF16).rearrange("p (k n) -> p k n", k=8)
        wa2 = A.alloc(1024, BF16)
        gng = A.alloc(2048, F32)
        S = A.alloc(4 * 2 * 512, F32).rearrange("p (h c n) -> p h c n", h=4, c=2)
        Sb = A.alloc(4 * 2 * 512, BF16).rearrange("p (h c n) -> p h c n", h=4, c=2)
        load_w(wqk, w_in, 2048, 8, "wqk", 0)
        load_w(wvz, w_in, 4096, 8, "wvz", 2048)
        load_w(wa, w_in, 16, 8, "wa", 6144)
        P.add("poolq", lambda e: e.dma_start(out=wa2[0:16, :], in_=gla_w_a2[j]), writes=["wa2"], chan="wa2")
        P.add("poolq", lambda e: e.dma_start(out=wa2[16:17, :], in_=gla_b_a[j].rearrange("(o n) -> o n", o=1)), writes=["wa2"], chan="wa2b")
        P.add("sp", lambda e: e.dma_start(out=gng, in_=gla_gn_g[j].partition_broadcast(128)), writes=["gng"], chan="gng")
        P.add("dve", lambda e: e.memset(S.rearrange("p h c n -> p (h c n)"), 0.0), writes=["S%d%d" % (h, ci) for h in range(4) for ci in range(2)])
        P.add("pool", lambda e: e.memset(Sb.rearrange("p h c n -> p (h c n)"), 0.0), writes=["Sb%d" % h for h in range(4)])
        NB = 2
        xt = [A.alloc(1024, F32) for _ in range(NB)]
        xb = [A.alloc(1024, BF16) for _ in range(NB)]
        xT = [A.alloc(1024, BF16).rearrange("p (k t) -> p k t", k=8) for _ in range(NB)]
        aT = A.alloc(128, BF16)
        sp_ = A.alloc(1024, F32)
        eG = A.alloc(1024, F32).rearrange("p (c t) -> p c t", c=8)
        enG = A.alloc(1024, F32).rearrange("p (c t) -> p c t", c=8)
        kdec = A.alloc(1024, F32)
        qt = A.alloc(1024, BF16).rearrange("p (c t) -> p c t", c=8)
        kt = A.alloc(1024, BF16).rearrange("p (c t) -> p c t", c=8)
        ks = A.alloc(1024, BF16)
        vb = A.alloc(2048, BF16)
        zg = A.alloc(2048, F32)
        ATb = A.alloc(128, BF16)
        sq = A.alloc(512, F32)
        og = [A.alloc(2048, BF16) for _ in range(NB)]
        nrm = A.alloc(8, F32)
        P.add("dve", lambda e: e.memset(aT[0:32, :], 1.0), writes=["aT"])
        for t in range(NT):
            u = t % NB
            sfx = "%d" % u
            r0 = t * 128
            P.add("sp", lambda e, u=u, r0=r0: e.dma_start(out=xt[u], in_=x_src[r0:r0 + 128, :]),
                  reads=["xsrc%d" % t], writes=["xt" + sfx], chan="xt" + sfx)
            P.add("act", lambda e, u=u: e.activation(out=xb[u], in_=xt[u], func=AF.Copy),
                  reads=["xt" + sfx], writes=["xb" + sfx])
            transpose_to(xT[u], xb[u], 8, "xb" + sfx, "xT" + sfx, evac="dve")
            ab, abr = palloc(1)
            for k in range(8):
                P.add("pe", lambda e, u=u, k=k, ab=ab: e.matmul(bank(ab)[0:16, 0:128], lhsT=wa[:, k, :], rhs=xT[u][:, k, :],
                                                             start=(k == 0), stop=(k == 7)),
                      reads=["xT" + sfx, "wa"], writes=abr)
            P.add("dve", lambda e, ab=ab: e.tensor_copy(out=aT[0:16, :], in_=bank(ab)[0:16, 0:128]), reads=abr, writes=["aT"])
            lb, lbr = palloc(2)
            for n in range(2):
                P.add("pe", lambda e, n=n, lb=lb: e.matmul(bank(lb + n), lhsT=aT[0:17, :], rhs=wa2[0:17, n * 512:(n + 1) * 512],
                                                         start=True, stop=True), reads=["aT", "wa2"], writes=[lbr[n]])
            P.add("act", lambda e, lb=lb: e.activation(out=sp_, in_=bank(lb, 2), func=AF.Exp, scale=-1.0), reads=lbr, writes=["sp"])
            P.add("act", lambda e: e.activation(out=sp_, in_=sp_, func=AF.Ln, bias=1.0), reads=["sp"], writes=["sp"])
            gb, gbr = palloc(2)
            for c in range(8):
                P.add("pe", lambda e, c=c, gb=gb: e.matmul(bank(gb, 2)[:, c * 128:(c + 1) * 128], lhsT=sp_[:, c * 128:(c + 1) * 128], rhs=umat,
                                                         start=True, stop=True), reads=["sp", "umat"], writes=[gbr[c // 4]])
            P.add("act", lambda e, gb=gb: e.activation(out=eG.rearrange("p c t -> p (c t)"), in_=bank(gb, 2), func=AF.Exp), reads=gbr, writes=["eG"])
            P.add("act", lambda e, gb=gb: e.activation(out=enG.rearrange("p c t -> p (c t)"), in_=bank(gb, 2), func=AF.Exp, scale=-1.0), reads=gbr, writes=["enG"])
            kb, kbr = palloc(2)
            for n in range(2):
                P.add("pe", lambda e, n=n, kb=kb: e.matmul(bank(kb + n), lhsT=lmat, rhs=sp_[:, n * 512:(n + 1) * 512], start=True, stop=True),
                      reads=["sp", "lmat"], writes=[kbr[n]])
            P.add("act", lambda e, kb=kb: e.activation(out=kdec, in_=bank(kb, 2), func=AF.Exp), reads=kbr, writes=["kdec"])
            for which, dst, gate, sc in (("q", qt, eG, 1.0 / 16.0), ("k", kt, enG, 1.0)):
                qb, qbr = palloc(2)
                c0 = 0 if which == "q" else 1024
                for c in range(8):
                    for k in range(8):
                        P.add("pe", lambda e, u=u, c=c, k=k, qb=qb, c0=c0: e.matmul(
                            bank(qb, 2)[:, c * 128:(c + 1) * 128], lhsT=wqk[:, k, c0 + c * 128:c0 + (c + 1) * 128], rhs=xT[u][:, k, :],
                            start=(k == 0), stop=(k == 7)), reads=["xT" + sfx, "wqk"], writes=[qbr[c // 4]])
                P.add("dve", lambda e, qb=qb, dst=dst, gate=gate, sc=sc: e.scalar_tensor_tensor(
                    out=dst.rearrange("p c t -> p (c t)"), in0=bank(qb, 2), scalar=float(sc), in1=gate.rearrange("p c t -> p (c t)"),
                    op0=ALU.mult, op1=ALU.mult), reads=qbr + ["eG" if which == "q" else "enG"], writes=[which + "t"])
            tb, tbr = palloc(2)
            for n in range(2):
                for k in range(8):
                    P.add("pe", lambda e, u=u, n=n, k=k, tb=tb: e.matmul(
                        bank(tb + n), lhsT=xT[u][:, k, :], rhs=wqk[:, k, 1024 + n * 512:1024 + (n + 1) * 512],
                        start=(k == 0), stop=(k == 7)), reads=["xT" + sfx, "wqk"], writes=[tbr[n]])
            P.add("dve", lambda e, tb=tb: e.tensor_tensor(out=ks, in0=bank(tb, 2), in1=kdec, op=ALU.mult), reads=tbr + ["kdec"], writes=["ks"])
            for half in range(2):
                vb_, vbr = palloc(2)
                for n in range(2):
                    for k in range(8):
                        P.add("pe", lambda e, u=u, n=n, k=k, vb_=vb_, half=half: e.matmul(
                            bank(vb_ + n), lhsT=xT[u][:, k, :], rhs=wvz[:, k, half * 1024 + n * 512:half * 1024 + (n + 1) * 512],
                            start=(k == 0), stop=(k == 7)), reads=["xT" + sfx, "wvz"], writes=[vbr[n]])
                P.add("act", lambda e, vb_=vb_, half=half: e.activation(out=vb[:, half * 1024:(half + 1) * 1024], in_=bank(vb_, 2), func=AF.Copy),
                      reads=vbr, writes=["vb%d" % half])
            for half in range(2):
                zb, zbr = palloc(2)
                for n in range(2):
                    for k in range(8):
                        P.add("pe", lambda e, u=u, n=n, k=k, zb=zb, half=half: e.matmul(
                            bank(zb + n), lhsT=xT[u][:, k, :], rhs=wvz[:, k, 2048 + half * 1024 + n * 512:2048 + half * 1024 + (n + 1) * 512],
                            start=(k == 0), stop=(k == 7)), reads=["xT" + sfx, "wvz"], writes=[zbr[n]])
                P.add("act", lambda e, zb=zb, half=half: e.activation(out=zg[:, half * 1024:(half + 1) * 1024], in_=bank(zb, 2), func=AF.Silu),
                      reads=zbr, writes=["zg%d" % half])
                P.add("pool", lambda e, half=half: e.tensor_tensor(out=zg[:, half * 1024:(half + 1) * 1024], in0=zg[:, half * 1024:(half + 1) * 1024],
                                                                 in1=gng[:, half * 1024:(half + 1) * 1024], op=ALU.mult),
                      reads=["zg%d" % half, "gng"], writes=["zg%d" % half])
            P.add("dve", lambda e: e.memset(nrm, 0.0), writes=["nrm%d" % h for h in range(4)])
            for h in range(4):
                vh = vb[:, h * 512:(h + 1) * 512]
                vres = "vb%d" % (h // 2)
                ab2, abr2 = palloc(1)
                for ci in range(2):
                    c = 2 * h + ci
                    P.add("pe", lambda e, c=c, ci=ci, ab2=ab2: e.matmul(bank(ab2)[:, 0:128], lhsT=kt[:, c, :], rhs=qt[:, c, :],
                                                                     start=(ci == 0), stop=(ci == 1)), reads=["kt", "qt"], writes=abr2)
                P.add("dve", lambda e, ab2=ab2: e.tensor_tensor(out=ATb, in0=bank(ab2)[:, 0:128], in1=tri, op=ALU.mult),
                      reads=abr2 + ["tri"], writes=["ATb"])
                ob, obr = palloc(1)
                P.add("pe", lambda e, ob=ob, vh=vh: e.matmul(bank(ob), lhsT=ATb, rhs=vh, start=True, stop=False),
                      reads=["ATb", vres], writes=obr)
                for ci in range(2):
                    c = 2 * h + ci
                    P.add("pe", lambda e, ob=ob, c=c, ci=ci, h=h: e.matmul(bank(ob), lhsT=qt[:, c, :], rhs=Sb[:, h, ci, :], start=False, stop=(ci == 1)),
                          reads=["qt", "Sb%d" % h], writes=obr)
                for ci in range(2):
                    c = 2 * h + ci
                    sb_, sbr = palloc(1)
                    P.add("pe", lambda e, sb_=sb_, c=c, vh=vh: e.matmul(bank(sb_), lhsT=ks[:, c * 128:(c + 1) * 128], rhs=vh, start=True, stop=True),
                          reads=["ks", vres], writes=sbr)
                    P.add("dve", lambda e, sb_=sb_, c=c, ci=ci, h=h: e.scalar_tensor_tensor(
                        out=S[:, h, ci, :], in0=S[:, h, ci, :], scalar=eG[:, c, 127:128], in1=bank(sb_), op0=ALU.mult, op1=ALU.add),
                        reads=["S%d%d" % (h, ci), "eG"] + sbr, writes=["S%d%d" % (h, ci)])
                P.add("act", lambda e, ob=ob, h=h: e.activation(out=sq, in_=bank(ob), func=AF.Square, accum_out=nrm[:, h:h + 1]),
                      reads=obr, writes=["sq", "nrm%d" % h])
                P.add("dve", lambda e, h=h: e.tensor_scalar(out=nrm[:, 4 + h:5 + h], in0=nrm[:, h:h + 1], scalar1=1.0 / 512, scalar2=float(EPS), op0=ALU.mult, op1=ALU.add),
                      reads=["nrm%d" % h], writes=["nrm%d" % h])
                P.add("act", lambda e, h=h: e.activation(out=nrm[:, 4 + h:5 + h], in_=nrm[:, 4 + h:5 + h], func=AF.Sqrt),
                      reads=["nrm%d" % h], writes=["nrm%d" % h])
                P.add("dve", lambda e, h=h: e.reciprocal(out=nrm[:, 4 + h:5 + h], in_=nrm[:, 4 + h:5 + h]),
                      reads=["nrm%d" % h], writes=["nrm%d" % h])
                P.add("dve", lambda e, ob=ob, h=h, u=u: e.scalar_tensor_tensor(
                    out=og[u][:, h * 512:(h + 1) * 512], in0=bank(ob), scalar=nrm[:, 4 + h:5 + h], in1=zg[:, h * 512:(h + 1) * 512],
                    op0=ALU.mult, op1=ALU.mult), reads=obr + ["nrm%d" % h, "zg%d" % (h // 2)], writes=["og" + sfx])
                for ci in range(2):
                    P.add("act", lambda e, h=h, ci=ci: e.activation(out=Sb[:, h, ci, :], in_=S[:, h, ci, :], func=AF.Copy),
                          reads=["S%d%d" % (h, ci)], writes=["Sb%d" % h])
            P.add("poolq", lambda e, u=u, r0=r0: e.dma_start(out=ogs[r0:r0 + 128, :], in_=og[u]),
                  reads=["og" + sfx], writes=["ogs"], chan="ogst" + sfx)

    def conv_a(x_src):
        P.barrier()
        A.release(persist_mark)
        TG = 256
        NG = L // TG
        w_in = conv_w_in[0]
        wc = A.alloc(8 * 6144, BF16).rearrange("p (k n) -> p k n", k=8)
        load_w(wc, w_in, 6144, 8, "wc", 0)
        prm_tm = A.alloc(2048, F32)
        P.add("sp", lambda e: e.dma_start(out=prm_tm[0:31, :], in_=conv_dw_w[0]), writes=["prm_tm"], chan="pr0")
        P.add("sp", lambda e: e.dma_start(out=prm_tm[31:32, :], in_=conv_dw_b[0].rearrange("(o n) -> o n", o=1)), writes=["prm_tm"], chan="pr1")
        P.add("sp", lambda e: e.dma_start(out=prm_tm[32:33, :], in_=conv_ln_g[0].rearrange("(o n) -> o n", o=1)), writes=["prm_tm"], chan="pr2")
        P.add("sp", lambda e: e.dma_start(out=prm_tm[33:34, :], in_=conv_ln_b[0].rearrange("(o n) -> o n", o=1)), writes=["prm_tm"], chan="pr3")
        prm = A.alloc(16 * 34, F32).rearrange("p (c k) -> p c k", c=16)
        for c in range(16):
            b, br = palloc(1)
            P.add("pe", lambda e, c=c, b=b: e.transpose(out=bank(b)[:, 0:34], in_=prm_tm[0:34, c * 128:(c + 1) * 128], identity=ident_f[0:34, 0:34]),
                  reads=["prm_tm", "ident_f"], writes=br)
            P.add("dve", lambda e, c=c, b=b: e.tensor_copy(out=prm[:, c, :], in_=bank(b)[:, 0:34]), reads=br, writes=["prm"])
        ones_f = A.alloc(128, F32)
        P.add("dve", lambda e: e.memset(ones_f, 1.0 / 2048.0), writes=["ones_f"])
        xt = [A.alloc(1024, F32) for _ in range(2)]
        xb = [A.alloc(1024, BF16) for _ in range(2)]
        xTg = A.alloc(8 * TG, BF16).rearrange("p (k t) -> p k t", k=8)
        uT = A.alloc(16 * (TG + 30), BF16).rearrange("p (c t) -> p c t", c=16)
        dg = [A.alloc(128, BF16) for _ in range(9)]
        szT = A.alloc(16 * TG, BF16).rearrange("p (c t) -> p c t", c=16)
        cT = A.alloc(16 * TG, F32).rearrange("p (c t) -> p c t", c=16)
        sgt = [A.alloc(TG, F32) for _ in range(2)]
        sqt = [A.alloc(TG, F32) for _ in range(2)]
        mean_t = A.alloc(TG, F32)
        rstd_t = A.alloc(TG, F32)
        t1 = [A.alloc(TG, F32) for _ in range(2)]
        t2 = [A.alloc(TG, F32) for _ in range(2)]
        ogTt = A.alloc(16 * TG, BF16).rearrange("p (c t) -> p c t", c=16)
        P.add("dve", lambda e: e.memset(uT.rearrange("p c t -> p (c t)"), 0.0), writes=["uT%d" % c for c in range(16)])
        for g in range(NG):
            for i in range(TG // 128):
                t = g * (TG // 128) + i
                u = t % 2
                sfx = "%d" % u
                r0 = t * 128
                P.add("sp", lambda e, u=u, r0=r0: e.dma_start(out=xt[u], in_=x_src[r0:r0 + 128, :]), writes=["xt" + sfx], chan="xt" + sfx)
                P.add("act", lambda e, u=u: e.activation(out=xb[u], in_=xt[u], func=AF.Copy), reads=["xt" + sfx], writes=["xb" + sfx])
                transpose_to(xTg[:, :, i * 128:(i + 1) * 128], xb[u], 8, "xb" + sfx, "xTg", evac="dve")
            for c in range(16):
                v = c % 2
                pa, par = palloc(1)
                pg, pgr = palloc(1)
                pz, pzr = palloc(1)
                for (pb_, pbr_, off) in ((pa, par, 0), (pg, pgr, 2048), (pz, pzr, 4096)):
                    for k in range(8):
                        P.add("pe", lambda e, pb_=pb_, off=off, c=c, k=k: e.matmul(
                            bank(pb_)[:, 0:TG], lhsT=wc[:, k, off + c * 128:off + (c + 1) * 128], rhs=xTg[:, k, :],
                            start=(k == 0), stop=(k == 7)), reads=["xTg", "wc"], writes=pbr_)
                P.add("act", lambda e, pg=pg, v=v: e.activation(out=sgt[v], in_=bank(pg)[:, 0:TG], func=AF.Sigmoid), reads=pgr, writes=["sgt%d" % v])
                if g > 0:
                    P.add("pool", lambda e, c=c: e.tensor_copy(out=uT[:, c, 0:30], in_=uT[:, c, TG:TG + 30]), reads=["uT%d" % c], writes=["uT%d" % c])
                P.add("dve", lambda e, pa=pa, v=v, c=c: e.tensor_tensor(out=uT[:, c, 30:30 + TG], in0=bank(pa)[:, 0:TG], in1=sgt[v], op=ALU.mult),
                      reads=par + ["sgt%d" % v, "uT%d" % c], writes=["uT%d" % c])
                P.add("act", lambda e, pz=pz, c=c: e.activation(out=szT[:, c, :], in_=bank(pz)[:, 0:TG], func=AF.Silu), reads=pzr, writes=["szT%d" % c])
            for c in range(16):
                cb, cbr = palloc(1)
                for k in range(CONV_W):
                    n_ = c * CONV_W + k
                    di = n_ % len(dg)
                    eng = ("dve", "pool")[n_ % 2]
                    if eng == "dve":
                        P.add("dve", lambda e, c=c, k=k, di=di: e.tensor_scalar(out=dg[di], in0=ident_b, scalar1=prm[:, c, k:k + 1], scalar2=None, op0=ALU.mult),
                              reads=["ident_b", "prm"], writes=["dg%d" % di])
                    elif eng == "pool":
                        P.add("pool", lambda e, c=c, k=k, di=di: e.tensor_scalar(out=dg[di], in0=ident_b, scalar1=prm[:, c, k:k + 1], scalar2=1.0, op0=ALU.mult, op1=ALU.mult),
                              reads=["ident_b", "prm"], writes=["dg%d" % di])
                    else:
                        P.add("act", lambda e, c=c, k=k, di=di: e.activation(out=dg[di], in_=ident_b, func=AF.Copy, scale=prm[:, c, k:k + 1]),
                              reads=["ident_b", "prm"], writes=["dg%d" % di])
                    P.add("pe", lambda e, c=c, k=k, di=di, cb=cb: e.matmul(bank(cb)[:, 0:TG], lhsT=dg[di], rhs=uT[:, c, k:k + TG], start=(k == 0), stop=(k == CONV_W - 1)),
                          reads=["dg%d" % di, "uT%d" % c], writes=cbr)
                P.add("act", lambda e, c=c, cb=cb: e.activation(out=cT[:, c, :], in_=bank(cb)[:, 0:TG], func=AF.Identity, bias=prm[:, c, 31:32]),
                      reads=cbr + ["prm"], writes=["cT%d" % c])
            ps, psr = palloc(1)
            pq, pqr = palloc(1)
            for c in range(16):
                v = c % 2
                P.add("pe", lambda e, ps=ps, c=c: e.matmul(bank(ps)[:, 0:TG], lhsT=ones_f, rhs=cT[:, c, :], start=(c == 0), stop=(c == 15)),
                      reads=["ones_f", "cT%d" % c], writes=psr)
                P.add("act", lambda e, c=c, v=v: e.activation(out=sqt[v], in_=cT[:, c, :], func=AF.Square), reads=["cT%d" % c], writes=["sqt%d" % v])
                P.add("pe", lambda e, pq=pq, c=c, v=v: e.matmul(bank(pq)[:, 0:TG], lhsT=ones_f, rhs=sqt[v], start=(c == 0), stop=(c == 15)),
                      reads=["ones_f", "sqt%d" % v], writes=pqr)
            P.add("act", lambda e, ps=ps: e.activation(out=mean_t, in_=bank(ps)[:, 0:TG], func=AF.Copy), reads=psr, writes=["mean_t"])
            P.add("dve", lambda e: e.tensor_tensor(out=rstd_t, in0=mean_t, in1=mean_t, op=ALU.mult), reads=["mean_t"], writes=["rstd_t"])
            P.add("dve", lambda e, pq=pq: e.tensor_tensor(out=rstd_t, in0=bank(pq)[:, 0:TG], in1=rstd_t, op=ALU.subtract), reads=pqr + ["rstd_t"], writes=["rstd_t"])
            P.add("dve", lambda e: e.tensor_scalar(out=rstd_t, in0=rstd_t, scalar1=float(EPS), scalar2=None, op0=ALU.add), reads=["rstd_t"], writes=["rstd_t"])
            P.add("act", lambda e: e.activation(out=rstd_t, in_=rstd_t, func=AF.Sqrt), reads=["rstd_t"], writes=["rstd_t"])
            P.add("dve", lambda e: e.reciprocal(out=rstd_t, in_=rstd_t), reads=["rstd_t"], writes=["rstd_t"])
            for c in range(16):
                v = c % 2
                P.add("dve", lambda e, c=c, v=v: e.tensor_tensor(out=t1[v], in0=cT[:, c, :], in1=mean_t, op=ALU.subtract), reads=["cT%d" % c, "mean_t"], writes=["t1%d" % v])
                P.add("dve", lambda e, c=c, v=v: e.tensor_tensor(out=t1[v], in0=t1[v], in1=rstd_t, op=ALU.mult), reads=["t1%d" % v, "rstd_t"], writes=["t1%d" % v])
                P.add("act", lambda e, c=c, v=v: e.activation(out=t2[v], in_=t1[v], func=AF.Silu, scale=prm[:, c, 32:33], bias=prm[:, c, 33:34]),
                      reads=["t1%d" % v, "prm"], writes=["t2%d" % v])
                P.add("pool", lambda e, c=c, v=v: e.tensor_tensor(out=ogTt[:, c, :], in0=t2[v], in1=szT[:, c, :], op=ALU.mult),
                      reads=["t2%d" % v, "szT%d" % c], writes=["ogTt"])
            P.add("poolq", lambda e, g=g: e.dma_start(out=ogTs[:, g * TG:(g + 1) * TG].rearrange("(c f) t -> f c t", f=128), in_=ogTt),
                  reads=["ogTt"], writes=["ogTs"], chan="ogTst")

    def dsa_a(x_src):
        P.barrier()
        A.release(persist_mark)
        NIT = 18
        NOSAME = True
        PI = math.pi
        w_in = dsa_w_in[0]
        WC = 2888
        W = A.alloc(8 * WC, BF16).rearrange("p (k n) -> p k n", k=8)
        Wr = A.alloc(8 * 688, BF16).rearrange("p (k n) -> p k n", k=8)
        load_w(W[:, :, 0:2304], w_in, 2304, 8, "W", 0)
        load_w(W[:, :, 2304:WC], w_in, 584, 8, "W", 4352)
        load_w(Wr, dsa_w_rot, 688, 8, "Wr", 0)
        ropec = A.alloc(8, F32)
        negm = A.alloc(128, F32)
        pw2 = A.alloc(NIT, F32)
        sel = A.alloc(512, F32).rearrange("p (g m) -> p g m", g=4)
        ones_b = A.alloc(32, BF16)
        P.add("sp", lambda e: e.dma_start(out=ropec[0:32, :], in_=c_rope), writes=["ropec"], chan="cr0")
        P.add("sp", lambda e: e.dma_start(out=negm, in_=c_neg), writes=["negm"], chan="cr1")
        P.add("sp", lambda e: e.dma_start(out=pw2, in_=c_pw2), writes=["pw2"], chan="cr2")
        P.add("sp", lambda e: e.dma_start(out=sel.rearrange("p g m -> p (g m)"), in_=c_sel), writes=["sel"], chan="cr3")
        P.add("dve", lambda e: e.memset(ones_b, 1.0), writes=["ones_b"])
        kT = A.alloc(L, BF16)
        vtok = A.alloc(NT * 128, BF16).rearrange("p (b d) -> p b d", b=NT)
        kiT = A.alloc(L, BF16)
        xt = A.alloc(1024, F32)
        xb = A.alloc(1024, BF16)
        xT = [A.alloc(1024, BF16).rearrange("p (k t) -> p k t", k=8) for _ in range(2)]
        posi = A.alloc(128, I32)
        posf = A.alloc(128, F32)
        ang = A.alloc(128, F32)
        arg = A.alloc(128, F32)
        argi = A.alloc(128, I32)
        argk = A.alloc(128, F32)
        C32 = A.alloc(128, F32)
        S32 = A.alloc(128, F32)
        C16 = A.alloc(128, F32)
        S16 = A.alloc(128, F32)
        tA = A.alloc(1024, F32)
        tB = A.alloc(1024, F32)
        qT = [A.alloc(2048, BF16).rearrange("p (h t) -> p h t", h=16) for _ in range(2)]
        qiT = A.alloc(1024, BF16).rearrange("p (h t) -> p h t", h=8)
        wi = A.alloc(8, F32)
        acc = A.alloc(L, F32)
        maskb = A.alloc(L, BF16)
        maskT = [A.alloc(NT * 128, BF16).rearrange("p (b t) -> p b t", b=NT) for _ in range(2)]
        rbuf = [A.alloc(512, F32) for _ in range(2)]
        probs = [A.alloc(2048, BF16) for _ in range(2)]
        den_sb = A.alloc(512, F32)
        oS = A.alloc(2048, F32)
        ogTt = A.alloc(2048, BF16).rearrange("p (c t) -> p c t", c=16)
        sm = A.alloc(8 + 2 * NIT, F32)
        lo, mid, cnt, tmp, W0 = (sm[:, i:i + 1] for i in range(5))
        wcols = sm[:, 8:8 + NIT]
        nhw = sm[:, 8 + NIT:8 + 2 * NIT]

        def bc(ap32, n):
            return ap32.unsqueeze(1).to_broadcast([ap32.shape[0], n, 128])

        def proj_fm(dst_bank_ap, col0, ncol, u, wres, Wt, brs):
            for k in range(8):
                P.add("pe", lambda e, k=k: e.matmul(dst_bank_ap, lhsT=Wt[:, k, col0:col0 + ncol], rhs=xT[u][:, k, :],
                                                     start=(k == 0), stop=(k == 7)), reads=["xT%d" % u, wres], writes=brs)

        def stP(t):
            u = t % 2
            sfx = "%d" % u
            r0 = t * 128
            qTu = qT[u]
            qres = "qT%d" % u
            P.add("sp", lambda e: e.dma_start(out=xt, in_=x_src[r0:r0 + 128, :]), writes=["xt"], chan="xt")
            P.add("sp", lambda e: e.dma_start(out=posi[0:32, :], in_=pos_in[r0:r0 + 128].partition_broadcast(32)), writes=["posi"], chan="posi")
            P.add("act", lambda e: e.activation(out=xb, in_=xt, func=AF.Copy), reads=["xt"], writes=["xb"])
            transpose_to(xT[u], xb, 8, "xb", "xT" + sfx, evac="dve")
            P.add("pool", lambda e: e.tensor_copy(out=posf[0:32, :], in_=posi[0:32, :]), reads=["posi"], writes=["posf"])
            for (n, ci, Ct, St_, nm) in ((32, 0, C32, S32, "h"), (16, 3, C16, S16, "i")):
                P.add("pool", lambda e, n=n, ci=ci: e.tensor_scalar(out=ang[0:n, :], in0=posf[0:n, :], scalar1=ropec[0:n, ci:ci + 1], scalar2=None, op0=ALU.mult),
                      reads=["posf", "ropec"], writes=["ang"])
                for (off, dst, scale_, wn) in ((0.0, St_, ropec[0:n, ci + 1:ci + 2], "S" + nm), (0.25, Ct, 2 * PI, "C" + nm)):
                    P.add("pool", lambda e, n=n, off=off: e.tensor_scalar(out=arg[0:n, :], in0=ang[0:n, :], scalar1=float(off), scalar2=None, op0=ALU.add),
                          reads=["ang"], writes=["arg"])
                    P.add("pool", lambda e, n=n: e.tensor_copy(out=argi[0:n, :], in_=arg[0:n, :]), reads=["arg"], writes=["argi"])
                    P.add("pool", lambda e, n=n: e.tensor_copy(out=argk[0:n, :], in_=argi[0:n, :]), reads=["argi"], writes=["argk"])
                    P.add("pool", lambda e, n=n: e.tensor_tensor(out=arg[0:n, :], in0=arg[0:n, :], in1=argk[0:n, :], op=ALU.subtract), reads=["arg", "argk"], writes=["arg"])
                    P.add("pool", lambda e, n=n: e.tensor_scalar(out=argk[0:n, :], in0=arg[0:n, :], scalar1=0.5, scalar2=None, op0=ALU.is_gt), reads=["arg"], writes=["argk"])
                    P.add("pool", lambda e, n=n: e.tensor_tensor(out=arg[0:n, :], in0=arg[0:n, :], in1=argk[0:n, :], op=ALU.subtract), reads=["arg", "argk"], writes=["arg"])
                    P.add("pool", lambda e, n=n: e.tensor_scalar(out=argk[0:n, :], in0=arg[0:n, :], scalar1=-0.5, scalar2=None, op0=ALU.is_lt), reads=["arg"], writes=["argk"])
                    P.add("pool", lambda e, n=n: e.tensor_tensor(out=arg[0:n, :], in0=arg[0:n, :], in1=argk[0:n, :], op=ALU.add), reads=["arg", "argk"], writes=["arg"])
                    P.add("act", lambda e, n=n, dst=dst, scale_=scale_: e.activation(out=dst[0:n, :], in_=arg[0:n, :], func=AF.Sin, scale=scale_),
                          reads=["arg", "ropec"], writes=[wn])
            for g8 in range(2):
                qb, qbr = palloc(2)
                for hh in range(8):
                    h = g8 * 8 + hh
                    proj_fm(bank(qb, 2)[:, hh * 128:(hh + 1) * 128], h * 128, 128, u, "W", W, [qbr[hh // 4]])
                rb, rbr = palloc(2)
                for hh in range(8):
                    h = g8 * 8 + hh
                    proj_fm(bank(rb, 2)[0:32, hh * 128:(hh + 1) * 128], h * 32, 32, u, "Wr", Wr, [rbr[hh // 4]])
                P.add("act", lambda e, g8=g8, qb=qb: e.activation(out=qTu[:, g8 * 8:(g8 + 1) * 8, :].rearrange("p h t -> p (h t)"), in_=bank(qb, 2), func=AF.Copy),
                      reads=qbr, writes=[qres])
                P.add("dve", lambda e, qb=qb: e.tensor_tensor(out=tA[0:32, :].rearrange("p (h t) -> p h t", h=8), in0=bank(qb, 2)[0:32, :].rearrange("p (h t) -> p h t", h=8),
                                                             in1=bc(C32[0:32, :], 8), op=ALU.mult), reads=qbr + ["Ch", qres], writes=["tA"])
                P.add("dve", lambda e, rb=rb: e.tensor_tensor(out=tB[0:32, :].rearrange("p (h t) -> p h t", h=8), in0=bank(rb, 2)[0:32, :].rearrange("p (h t) -> p h t", h=8),
                                                             in1=bc(S32[0:32, :], 8), op=ALU.mult), reads=rbr + ["Sh"], writes=["tB"])
                P.add("dve", lambda e, g8=g8: e.tensor_tensor(out=qTu[0:32, g8 * 8:(g8 + 1) * 8, :].rearrange("p h t -> p (h t)"), in0=tA[0:32, :], in1=tB[0:32, :], op=ALU.add),
                      reads=["tA", "tB"], writes=[qres])
            kb_, kbr_ = palloc(1)
            proj_fm(bank(kb_)[:, 0:128], 2048, 128, u, "W", W, kbr_)
            kr, krr = palloc(1)
            proj_fm(bank(kr)[0:32, 0:128], 512, 32, u, "Wr", Wr, krr)
            P.add("act", lambda e: e.activation(out=kT[:, r0:r0 + 128], in_=bank(kb_)[:, 0:128], func=AF.Copy), reads=kbr_, writes=["kT"])
            P.add("dve", lambda e: e.tensor_tensor(out=tA[0:32, 0:128], in0=bank(kb_)[0:32, 0:128], in1=C32[0:32, :], op=ALU.mult), reads=kbr_ + ["Ch"], writes=["tA"])
            P.add("dve", lambda e: e.tensor_tensor(out=tB[0:32, 0:128], in0=bank(kr)[0:32, 0:128], in1=S32[0:32, :], op=ALU.mult), reads=krr + ["Sh"], writes=["tB"])
            P.add("dve", lambda e: e.tensor_tensor(out=kT[0:32, r0:r0 + 128], in0=tA[0:32, 0:128], in1=tB[0:32, 0:128], op=ALU.add), reads=["tA", "tB"], writes=["kT"])
            vb_, vbr_ = palloc(1)
            for k in range(8):
                P.add("pe", lambda e, k=k: e.matmul(bank(vb_)[:, 0:128], lhsT=xT[u][:, k, :], rhs=W[:, k, 2176:2304], start=(k == 0), stop=(k == 7)),
                      reads=["xT" + sfx, "W"], writes=vbr_)
            P.add("act", lambda e: e.activation(out=vtok[:, t, :], in_=bank(vb_)[:, 0:128], func=AF.Copy), reads=vbr_, writes=["vtok"])
            wb_, wbr_ = palloc(1)
            for k in range(8):
                P.add("pe", lambda e, k=k: e.matmul(bank(wb_)[:, 0:8], lhsT=xT[u][:, k, :], rhs=W[:, k, 2880:2888], start=(k == 0), stop=(k == 7)),
                      reads=["xT" + sfx, "W"], writes=wbr_)
            P.add("dve", lambda e: e.tensor_copy(out=wi, in_=bank(wb_)[:, 0:8]), reads=wbr_, writes=["wi"])
            ib, ibr = palloc(2)
            for h in range(8):
                proj_fm(bank(ib, 2)[0:64, h * 128:(h + 1) * 128], 2304 + h * 64, 64, u, "W", W, [ibr[h // 4]])
            jb, jbr = palloc(2)
            for h in range(8):
                proj_fm(bank(jb, 2)[0:16, h * 128:(h + 1) * 128], 544 + h * 16, 16, u, "Wr", Wr, [jbr[h // 4]])
            P.add("act", lambda e: e.activation(out=qiT[0:64, :, :].rearrange("p h t -> p (h t)"), in_=bank(ib, 2)[0:64, :], func=AF.Copy), reads=ibr, writes=["qiT"])
            P.add("dve", lambda e: e.tensor_tensor(out=tA[0:16, :].rearrange("p (h t) -> p h t", h=8), in0=bank(ib, 2)[0:16, :].rearrange("p (h t) -> p h t", h=8),
                                                  in1=bc(C16[0:16, :], 8), op=ALU.mult), reads=ibr + ["Ci"], writes=["tA"])
            P.add("dve", lambda e: e.tensor_tensor(out=tB[0:16, :].rearrange("p (h t) -> p h t", h=8), in0=bank(jb, 2)[0:16, :].rearrange("p (h t) -> p h t", h=8),
                                                  in1=bc(S16[0:16, :], 8), op=ALU.mult), reads=jbr + ["Si"], writes=["tB"])
            P.add("dve", lambda e: e.tensor_tensor(out=qiT[0:16, :, :].rearrange("p h t -> p (h t)"), in0=tA[0:16, :], in1=tB[0:16, :], op=ALU.add),
                  reads=["tA", "tB"], writes=["qiT"])
            cb, cbr = palloc(1)
            proj_fm(bank(cb)[0:64, 0:128], 2816, 64, u, "W", W, cbr)
            db, dbr = palloc(1)
            proj_fm(bank(db)[0:16, 0:128], 672, 16, u, "Wr", Wr, dbr)
            P.add("act", lambda e: e.activation(out=kiT[0:64, r0:r0 + 128], in_=bank(cb)[0:64, 0:128], func=AF.Copy), reads=cbr, writes=["kiT"])
            P.add("dve", lambda e: e.tensor_tensor(out=tA[0:16, 0:128], in0=bank(cb)[0:16, 0:128], in1=C16[0:16, :], op=ALU.mult), reads=cbr + ["Ci"], writes=["tA"])
            P.add("dve", lambda e: e.tensor_tensor(out=tB[0:16, 0:128], in0=bank(db)[0:16, 0:128], in1=S16[0:16, :], op=ALU.mult), reads=dbr + ["Si"], writes=["tB"])
            P.add("dve", lambda e: e.tensor_tensor(out=kiT[0:16, r0:r0 + 128], in0=tA[0:16, 0:128], in1=tB[0:16, 0:128], op=ALU.add), reads=["tA", "tB"], writes=["kiT"])

        def stI(t):
            r0 = t * 128
            St = (t + 1) * 128
            for s0 in range(0, St, 512):
                n = min(512, St - s0)
                for h in range(8):
                    v = h % 2
                    sb_, sbr = palloc(1)
                    P.add("pe", lambda e, sb_=sb_, h=h, s0=s0, n=n: e.matmul(bank(sb_)[:, 0:n], lhsT=qiT[0:64, h, :], rhs=kiT[0:64, s0:s0 + n], start=True, stop=True),
                          reads=["qiT", "kiT"], writes=sbr)
                    P.add("act", lambda e, sb_=sb_, v=v, n=n: e.activation(out=rbuf[v][:, 0:n], in_=bank(sb_)[:, 0:n], func=AF.Relu), reads=sbr, writes=["rbuf%d" % v])
                    if h == 0:
                        P.add("dve", lambda e, v=v, s0=s0, n=n: e.tensor_scalar(out=acc[:, s0:s0 + n], in0=rbuf[v][:, 0:n], scalar1=wi[:, 0:1], scalar2=None, op0=ALU.mult),
                              reads=["rbuf%d" % v, "wi"], writes=["acc"])
                    else:
                        P.add("dve", lambda e, v=v, s0=s0, n=n, h=h: e.scalar_tensor_tensor(out=acc[:, s0:s0 + n], in0=rbuf[v][:, 0:n], scalar=wi[:, h:h + 1], in1=acc[:, s0:s0 + n],
                                                                                      op0=ALU.mult, op1=ALU.add), reads=["rbuf%d" % v, "wi", "acc"], writes=["acc"])
            if t >= 2:
                P.add("dve", lambda e: e.tensor_reduce(out=W0, in_=acc[:, 0:St], axis=mybir.AxisListType.X, op=ALU.max, apply_absolute_value=True),
                      reads=["acc"], writes=["sm"])
            P.add("pool", lambda e: e.tensor_tensor(out=acc[:, r0:r0 + 128], in0=acc[:, r0:r0 + 128], in1=negm, op=ALU.add), reads=["acc", "negm"], writes=["acc"])

        def stT(t):
            St = (t + 1) * 128
            u = t % 2
            if t >= 2:
                P.add("dve", lambda e: e.memset(mid, 0.0), writes=["sm"])
                P.add("dve", lambda e: e.tensor_scalar(out=wcols, in0=pw2, scalar1=W0, scalar2=2.0, op0=ALU.mult, op1=ALU.mult), reads=["sm", "pw2"], writes=["sm"])
                P.add("dve", lambda e: e.tensor_scalar(out=nhw, in0=wcols, scalar1=-0.5, scalar2=None, op0=ALU.mult), reads=["sm"], writes=["sm"])
                for it in range(NIT):
                    P.add("dve", lambda e: e.tensor_scalar(out=maskb[:, 0:St], in0=acc[:, 0:St], scalar1=mid, scalar2=None, op0=ALU.is_ge, op1=ALU.add, accum_out=cnt),
                          reads=["acc", "sm"], writes=["maskb", "sm"])
                    P.add("dve", lambda e, it=it: e.tensor_scalar(out=tmp, in0=cnt, scalar1=256.0, scalar2=wcols[:, it:it + 1], op0=ALU.is_ge, op1=ALU.mult), reads=["sm"], writes=["sm"], nosame=NOSAME)
                    P.add("dve", lambda e, it=it: e.scalar_tensor_tensor(out=mid, in0=tmp, scalar=nhw[:, it:it + 1], in1=mid, op0=ALU.add, op1=ALU.add), reads=["sm"], writes=["sm"], nosame=NOSAME)
                P.add("dve", lambda e: e.tensor_scalar(out=lo, in0=W0, scalar1=-(2.0 ** -NIT), scalar2=mid, op0=ALU.mult, op1=ALU.add), reads=["sm"], writes=["sm"])
                P.add("dve", lambda e: e.tensor_scalar(out=maskb[:, 0:St], in0=acc[:, 0:St], scalar1=lo, scalar2=None, op0=ALU.is_ge), reads=["acc", "sm"], writes=["maskb"])
            else:
                P.add("dve", lambda e: e.tensor_scalar(out=maskb[:, 0:St], in0=acc[:, 0:St], scalar1=-1e29, scalar2=None, op0=ALU.is_ge), reads=["acc"], writes=["maskb"])
            transpose_to(maskT[u], maskb, t + 1, "maskb", "maskT%d" % u, evac="dve")

        ring = [5, 6, 7]

        def stAtt(t):
            u = t % 2
            qTu = qT[u]

            def logits(kb):
                pv = kb % 2
                for hg in range(4):
                    lb = ring[(kb * 4 + hg) % 3]
                    P.add("pe", lambda e, lb=lb, hg=hg: e.matmul(bank(lb), lhsT=kT[:, kb * 128:(kb + 1) * 128],
                                                               rhs=qTu[:, hg * 4:(hg + 1) * 4, :].rearrange("p h t -> p (h t)"), start=True, stop=True),
                          reads=["kT", "qT%d" % u], writes=["pb%d" % lb])
                    P.add("act", lambda e, lb=lb, hg=hg: e.activation(out=probs[pv][:, hg * 512:(hg + 1) * 512], in_=bank(lb), func=AF.Exp, scale=float(ATTN_SCALE)),
                          reads=["pb%d" % lb], writes=["probs%d_%d" % (pv, hg)])
                    P.add("pool", lambda e, hg=hg: e.tensor_tensor(out=probs[pv][:, hg * 512:(hg + 1) * 512].rearrange("p (h t) -> p h t", h=4),
                                                                 in0=probs[pv][:, hg * 512:(hg + 1) * 512].rearrange("p (h t) -> p h t", h=4),
                                                                 in1=bc(maskT[u][:, kb, :], 4), op=ALU.mult),
                          reads=["probs%d_%d" % (pv, hg), "maskT%d" % u], writes=["probs%d_%d" % (pv, hg)])

            def pvmm(kb):
                pv = kb % 2
                for hg in range(4):
                    P.add("pe", lambda e, hg=hg: e.matmul(bank(hg), lhsT=vtok[:, kb, :], rhs=probs[pv][:, hg * 512:(hg + 1) * 512],
                                                        start=(kb == 0), stop=(kb == t)), reads=["vtok", "probs%d_%d" % (pv, hg)], writes=["pb%d" % hg])
                    P.add("pe", lambda e, hg=hg: e.matmul(bank(4)[32 * hg:32 * (hg + 1), :], lhsT=ones_b[:, 0:32], rhs=probs[pv][:, hg * 512:(hg + 1) * 512],
                                                        start=(kb == 0), stop=(kb == t), tile_position=(0, 32 * hg)), reads=["ones_b", "probs%d_%d" % (pv, hg)], writes=["pb4"])

            logits(0)
            for kb in range(t + 1):
                if kb + 1 <= t:
                    logits(kb + 1)
                pvmm(kb)

        def stFe(t):
            P.add("act", lambda e: e.activation(out=den_sb, in_=bank(4), func=AF.Copy), reads=["pb4"], writes=["den_sb"])
            for hp in range(2):
                P.add("act", lambda e, hp=hp: e.activation(out=oS[:, hp * 1024:(hp + 1) * 1024], in_=bank(2 * hp, 2), func=AF.Copy),
                      reads=["pb%d" % (2 * hp), "pb%d" % (2 * hp + 1)], writes=["oS%d" % hp])

        def stFm(t):
            r0 = t * 128
            P.add("dve", lambda e: e.reciprocal(out=den_sb, in_=den_sb), reads=["den_sb"], writes=["den_sb"])
            for hg in range(4):
                lb, lbr = palloc(1)
                P.add("pe", lambda e, lb=lb, hg=hg: e.matmul(bank(lb), lhsT=sel[:, hg, :], rhs=den_sb, start=True, stop=True), reads=["sel", "den_sb"], writes=lbr)
                P.add("dve", lambda e, lb=lb, hg=hg: e.tensor_tensor(out=ogTt[:, hg * 4:(hg + 1) * 4, :].rearrange("p c t -> p (c t)"), in0=bank(lb), in1=oS[:, hg * 512:(hg + 1) * 512], op=ALU.mult),
                      reads=lbr + ["oS%d" % (hg // 2)], writes=["ogTt"])
            P.add("poolq", lambda e: e.dma_start(out=ogTs[:, r0:r0 + 128].rearrange("(c f) t -> f c t", f=128), in_=ogTt),
                  reads=["ogTt"], writes=["ogTs"], chan="ogTst")

        if os.environ.get("DSA_NOPIPE"):
            for t in range(NT):
                stP(t)
                stI(t)
                stT(t)
                stAtt(t)
                stFe(t)
                stFm(t)
            return
        VAR = os.environ.get("DSA_VAR", "b")
        stP(0)
        stI(0)
        stT(0)
        for t in range(NT):
            if t + 1 < NT:
                stP(t + 1)
                stI(t + 1)
                if VAR == "a":
                    stT(t + 1)
            stAtt(t)
            stFe(t)
            if t + 1 < NT and VAR != "a":
                stT(t + 1)
            stFm(t)

    cur = x_in
    for li in layers:
        kind = li % 3
        j = li // 3
        dst = out if li == layers[-1] else xs
        if kind == 0:
            gla_a(j, cur)
            phase_b(li, cur, dst, gla_w_out[j], False)
        elif kind == 2:
            conv_a(cur)
            phase_b(li, cur, dst, conv_w_out[0], True)
        else:
            dsa_a(cur)
            phase_b(li, cur, dst, dsa_w_out[0], True, zgate_w=dsa_w_in[0])
        cur = dst
    for u in range(2):
        P.final_waits.append("xo%d" % u)
    P.emit(st)
    st.close()
    return nc


def _consts():
    j = np.arange(128)[:, None]
    i = np.arange(128)[None, :]
    return {
        "c_ident": np.eye(128, dtype=np.float32),
        "c_umat": np.where(j <= i, -1.0 / 16.0, 0.0).astype(np.float32),
        "c_lmat": np.where(j > i, -1.0 / 16.0, 0.0).astype(np.float32),
        "c_tri": np.where(j <= i, 1.0, 0.0).astype(np.float32),
        "c_rope": _rope_consts(),
        "c_neg": np.where(i <= j, 0.0, -1e30).astype(np.float32),
        "c_pw2": np.tile((2.0 ** -(np.arange(18) + 1.0))[None, :], (128, 1)).astype(np.float32),
        "c_sel": _sel_consts(),
    }


def _rope_consts():
    c = np.zeros((32, 8), np.float32)
    pp = np.arange(32)
    c[:, 0] = (np.float32(ROPE_THETA) ** (-(np.arange(0, 32, 2, dtype=np.float32)) / np.float32(32)))[pp % 16] / (2 * math.pi)
    c[:, 1] = np.where(pp < 16, -1.0, 1.0) * 2 * math.pi
    c[:, 3] = (np.float32(ROPE_THETA) ** (-(np.arange(0, 16, 2, dtype=np.float32)) / np.float32(16)))[pp % 8] / (2 * math.pi)
    c[:, 4] = np.where(pp % 16 < 8, -1.0, 1.0) * 2 * math.pi
    return c


def _sel_consts():
    s = np.zeros((128, 4, 128), np.float32)
    for g in range(4):
        s[32 * g, g, :] = 1.0
    return s.reshape(128, 512)


def _rot_perm():
    idx = []
    for h in range(16):
        idx += list(range(h * 128 + 16, h * 128 + 32)) + list(range(h * 128, h * 128 + 16))
    idx += list(range(2048 + 16, 2048 + 32)) + list(range(2048, 2048 + 16))
    for h in range(8):
        b = 4352 + h * 64
        idx += list(range(b + 8, b + 16)) + list(range(b, b + 8))
    idx += list(range(4864 + 8, 4864 + 16)) + list(range(4864, 4864 + 8))
    return np.asarray(idx)


WEIGHT_KEYS = ["gla_w_in", "gla_w_a2", "gla_b_a", "gla_gn_g", "gla_w_out", "dsa_w_in", "dsa_w_out",
               "conv_w_in", "conv_dw_w", "conv_dw_b", "conv_ln_g", "conv_ln_b", "conv_w_out",
               "ln_g", "ln_b", "ple_w", "ple_gate_w"]


def run(inputs, NT=32, layers=(0, 1, 2, 3), cores=None):
    B = inputs["x"].shape[0]
    cores = list(range(B)) if cores is None else cores
    L = NT * 128
    nc = build_program(NT, list(layers))
    consts = _consts()
    in_maps = []
    for b in cores:
        m = {"x": np.ascontiguousarray(inputs["x"][b, :L]),
             "p": np.ascontiguousarray(inputs["p"][:, b, :L]),
             "pos": np.ascontiguousarray(inputs["positions"][b, :L]).astype(np.int32)}
        for k in WEIGHT_KEYS:
            m[k] = np.ascontiguousarray(inputs[k])
        m["dsa_w_rot"] = np.ascontiguousarray(inputs["dsa_w_in"][0][:, _rot_perm()])
        m.update(consts)
        in_maps.append(m)
    res = run_bass_kernel_spmd(nc, in_maps, core_ids=list(range(len(cores))))
    return np.stack([r["out"] for r in res.results], axis=0)


def kernel(**inputs):
    inputs = {k: np.asarray(v) for k, v in inputs.items()}
    return run(inputs).astype(np.float32)
```

```python
import math
import os
from contextlib import ExitStack
import numpy as np
import concourse.bass as bass
import concourse.mybir as mybir
from concourse.bass_utils import run_bass_kernel_spmd

F32 = mybir.dt.float32
BF16 = mybir.dt.bfloat16
I32 = mybir.dt.int32
ALU = mybir.AluOpType
AF = mybir.ActivationFunctionType

D = 1024
SEQ = 4096
DEPTH = 4
DI = 2048
GLA_IN = 6160
DSA_IN = 4936
CONV_IN = 6144
ALPHA = (2 * DEPTH) ** 0.25
EPS = 1e-5
ROPE_THETA = 500000.0
ATTN_SCALE = 128 ** -0.5
CONV_W = 31


NOSYNC_ENGS = set(os.environ.get("NOSYNC_ENGS", "").split(","))


class Op:
    __slots__ = ("eng", "fn", "deps", "idx", "chan", "ticket", "needs_inc", "nosame")


class Prog:
    PHYS = {"pe": "tensor", "act": "scalar", "dve": "vector", "pool": "gpsimd",
            "sp": "sync", "actq": "scalar", "poolq": "gpsimd"}

    def __init__(self, nc, same_engine_raw=True):
        self.nc = nc
        self.ops = []
        self.lastw = {}
        self.readers = {}
        self.same_engine_raw = same_engine_raw
        self.last_on = {}
        self.dma_since = []
        self.barrier_deps = None
        self.barrier_seen = set()
        self.final_waits = []
        self.groups = {}

    def add(self, eng, fn, reads=(), writes=(), chan=None, group=None, nosame=False):
        op = Op()
        op.nosame = nosame
        op.eng = eng
        op.fn = fn
        op.idx = len(self.ops)
        op.chan = chan
        op.ticket = None
        op.needs_inc = False
        writes = list(writes) + [r + "_rd" for r in reads if r.startswith("pb")]
        deps = {}
        for r in reads:
            for w in self.lastw.get(r, ()):
                deps[w] = "raw"
        for w_ in writes:
            lw = self.lastw.get(w_, ())
            same_group = group is not None and lw and all(self.groups.get(w) == group for w in lw)
            if not same_group:
                for w in lw:
                    if w not in deps:
                        deps[w] = "waw"
            for rd in self.readers.get(w_, ()):
                if rd not in deps:
                    deps[rd] = "war"
        for r in reads:
            self.readers.setdefault(r, []).append(op.idx)
        for w_ in writes:
            lw = self.lastw.get(w_, ())
            if group is not None and lw and all(self.groups.get(w) == group for w in lw):
                self.lastw[w_] = list(lw) + [op.idx]
            else:
                self.lastw[w_] = [op.idx]
                self.readers[w_] = []
        if group is not None:
            self.groups[op.idx] = group
        ph = self.PHYS[eng]
        if self.barrier_deps is not None and ph not in self.barrier_seen:
            self.barrier_seen.add(ph)
            for d in self.barrier_deps:
                if d not in deps:
                    deps[d] = "raw"
        op.deps = deps
        self.ops.append(op)
        self.last_on[ph] = op.idx
        if chan is not None:
            self.dma_since.append(op.idx)
        return op

    def barrier(self):
        deps = set(self.last_on.values()) | set(self.dma_since)
        self.barrier_deps = deps
        self.barrier_seen = set()
        self.dma_since = []
        self.lastw = {}
        self.readers = {}

    def emit(self, stack):
        nc = self.nc
        ops = self.ops
        phys = [self.PHYS[o.eng] for o in ops]
        is_dma = [o.chan is not None for o in ops]
        need = [[] for _ in ops]
        for o in ops:
            for d, kind in o.deps.items():
                if is_dma[d]:
                    need[o.idx].append(d)
                elif phys[d] == phys[o.idx]:
                    if is_dma[o.idx]:
                        need[o.idx].append(d)
                    elif kind == "raw" and self.same_engine_raw and o.eng != "pe" and o.eng not in NOSYNC_ENGS and not o.nosame:
                        need[o.idx].append(d)
                else:
                    need[o.idx].append(d)
        for o in ops:
            for d in need[o.idx]:
                ops[d].needs_inc = True
            if is_dma[o.idx]:
                o.needs_inc = True
        sems = {}
        counts = {}
        for o in ops:
            if not o.needs_inc:
                continue
            key = ("d_" + o.chan) if is_dma[o.idx] else ("e_" + phys[o.idx])
            if key not in sems:
                sems[key] = stack.enter_context(nc.semaphore("s_" + key))
                counts[key] = 0
            counts[key] += 16 if is_dma[o.idx] else 1
            o.ticket = (key, counts[key])
        self.n_sems = len(sems)
        block = stack.enter_context(nc.Block())
        streams = {}
        for o in ops:
            streams.setdefault(phys[o.idx], []).append(o)
        final = self.final_waits

        def run_stream(e, lst):
            waited = {}
            for o in lst:
                req = {}
                for d in need[o.idx]:
                    key, val = ops[d].ticket
                    if req.get(key, 0) < val:
                        req[key] = val
                for key, val in req.items():
                    if waited.get(key, 0) >= val:
                        continue
                    e.wait_ge(sems[key], val)
                    waited[key] = val
                ins = o.fn(e)
                if o.needs_inc:
                    key, val = o.ticket
                    ins.then_inc(sems[key], 16 if is_dma[o.idx] else 1)

        def mk(name):
            def f(e):
                run_stream(e, streams.get(name, []))
                if name == "gpsimd":
                    for ch in final:
                        key = "d_" + ch
                        if key in sems:
                            e.wait_ge(sems[key], counts[key])
            return f

        block.sync(mk("sync"))
        block.scalar(mk("scalar"))
        block.vector(mk("vector"))
        block.gpsimd(mk("gpsimd"))
        block.tensor(mk("tensor"))


class Arena:
    def __init__(self, ap, nwords):
        self.ap = ap
        self.n = nwords
        self.off = 0

    def alloc(self, nelem, dt):
        words = nelem if dt in (F32, I32) else (nelem + 1) // 2
        words = (words + 15) // 16 * 16
        assert self.off + words <= self.n, ("SBUF arena overflow", self.off, words, self.n)
        a = self.ap[:, self.off:self.off + words]
        self.off += words
        if dt != F32:
            a = a.bitcast(dt)
        return a[:, 0:nelem]

    def mark(self):
        return self.off

    def release(self, m):
        self.off = m


def build_program(NT, layers, dbg=False):
    L = NT * 128
    nc = bass.Bass("TRN2", target_bir_lowering=False)

    def din(name, shape, dt=F32):
        return nc.dram_tensor(name, list(shape), dt, kind="ExternalInput").ap()

    x_in = din("x", [L, D])
    p_in = din("p", [DEPTH, L, 256])
    pos_in = din("pos", [L], I32)
    gla_w_in = din("gla_w_in", [2, D, GLA_IN])
    gla_w_a2 = din("gla_w_a2", [2, 16, D])
    gla_b_a = din("gla_b_a", [2, D])
    gla_gn_g = din("gla_gn_g", [2, DI])
    gla_w_out = din("gla_w_out", [2, DI, D])
    dsa_w_in = din("dsa_w_in", [1, D, DSA_IN])
    dsa_w_out = din("dsa_w_out", [1, DI, D])
    conv_w_in = din("conv_w_in", [1, D, CONV_IN])
    conv_dw_w = din("conv_dw_w", [1, CONV_W, DI])
    conv_dw_b = din("conv_dw_b", [1, DI])
    conv_ln_g = din("conv_ln_g", [1, DI])
    conv_ln_b = din("conv_ln_b", [1, DI])
    conv_w_out = din("conv_w_out", [1, DI, D])
    ln_g = din("ln_g", [DEPTH, D])
    ln_b = din("ln_b", [DEPTH, D])
    ple_w = din("ple_w", [DEPTH, 256, D])
    ple_gate_w = din("ple_gate_w", [DEPTH, D, D])
    c_ident = din("c_ident", [128, 128])
    c_umat = din("c_umat", [128, 128])
    c_lmat = din("c_lmat", [128, 128])
    c_tri = din("c_tri", [128, 128])
    c_rope = din("c_rope", [32, 8])
    c_neg = din("c_neg", [128, 128])
    c_pw2 = din("c_pw2", [128, 18])
    c_sel = din("c_sel", [128, 512])
    dsa_w_rot = din("dsa_w_rot", [D, 688])
    out = nc.dram_tensor("out", [L, D], F32, kind="ExternalOutput").ap()
    xs = nc.dram_tensor("xs", [L, D], F32).ap()
    ogs = nc.dram_tensor("ogs", [L, DI], BF16).ap()
    ogTs = nc.dram_tensor("ogTs", [DI, L], BF16).ap()

    st = ExitStack()
    ARENA_WORDS = 52000
    arena_t = st.enter_context(nc.sbuf_tensor("arena", [128, ARENA_WORDS], F32))
    A = Arena(arena_t[:], ARENA_WORDS)
    psum_t = st.enter_context(nc.psum_tensor("ps", [128, 4096], F32))
    psum = psum_t[:]
    P = Prog(nc)

    def bank(i, n=1):
        return psum[:, i * 512:(i + n) * 512]

    pstate = {"ptr": 0}
    gctr = [0]

    def palloc(n):
        if pstate["ptr"] + n > 8:
            pstate["ptr"] = 0
        b = pstate["ptr"]
        pstate["ptr"] += n
        return b, ["pb%d" % i for i in range(b, b + n)]

    ident_f = A.alloc(128, F32)
    ident_b = A.alloc(128, BF16)
    umat = A.alloc(128, F32)
    lmat = A.alloc(128, F32)
    tri = A.alloc(128, F32)
    P.add("sp", lambda e: e.dma_start(out=ident_f, in_=c_ident), writes=["ident_f"], chan="c0")
    P.add("sp", lambda e: e.dma_start(out=umat, in_=c_umat), writes=["umat"], chan="c1")
    P.add("sp", lambda e: e.dma_start(out=lmat, in_=c_lmat), writes=["lmat"], chan="c2")
    P.add("sp", lambda e: e.dma_start(out=tri, in_=c_tri), writes=["tri"], chan="c3")
    P.add("dve", lambda e: e.tensor_copy(out=ident_b, in_=ident_f), reads=["ident_f"], writes=["ident_b"])
    persist_mark = A.mark()

    def load_w(dst, src_rows, ncols, kchunks, name, c0=0):
        step = 2048
        gctr[0] += 1
        grp = "g%d" % gctr[0]
        for k in range(kchunks):
            for cs in range(0, ncols, step):
                ce = min(ncols, cs + step)
                P.add("poolq",
                      lambda e, k=k, cs=cs, ce=ce: e.dma_start(
                          out=dst[:, k, cs:ce], in_=src_rows[k * 128:(k + 1) * 128, c0 + cs:c0 + ce]),
                      writes=[name], chan="w_" + name, group=grp)

    tb_state = {"i": 0}

    def transpose_to(dst, src_bf, nblk, rname, wname, evac="dve", banks=None):
        done = 0
        while done < nblk:
            n = min(8, nblk - done)
            if banks is None:
                b, br = palloc(1)
            else:
                b = banks[tb_state["i"] % len(banks)]
                tb_state["i"] += 1
                br = ["pb%d" % b]
            pv = bank(b).bitcast(BF16)
            for k in range(n):
                P.add("pe", lambda e, k=k, done=done, pv=pv: e.transpose(
                    out=pv[:, k * 128:(k + 1) * 128], in_=src_bf[:, (done + k) * 128:(done + k + 1) * 128],
                    identity=ident_b), reads=[rname, "ident_b"], writes=br)
            P.add(evac, lambda e, n=n, done=done, pv=pv: e.tensor_copy(
                out=dst[:, done:done + n, :], in_=pv[:, 0:n * 128].rearrange("p (k t) -> p k t", k=n))
                if evac != "act" else e.activation(
                out=dst[:, done:done + n, :], in_=pv[:, 0:n * 128].rearrange("p (k t) -> p k t", k=n), func=AF.Copy),
                reads=br, writes=[wname])
            done += n

    def phase_b(li, x_src, x_dst, w_out_d, og_feature_major, zgate_w=None):
        P.barrier()
        A.release(persist_mark)
        wout = A.alloc(16 * 1024, BF16).rearrange("p (k n) -> p k n", k=16)
        wgate = A.alloc(8 * 1024, BF16).rearrange("p (k n) -> p k n", k=8)
        wple = A.alloc(2 * 1024, BF16).rearrange("p (k n) -> p k n", k=2)
        lng = A.alloc(1024, F32)
        lnb = A.alloc(1024, F32)
        load_w(wout, w_out_d, 1024, 16, "wout")
        load_w(wgate, ple_gate_w[li], 1024, 8, "wgate")
        load_w(wple, ple_w[li], 1024, 2, "wple")
        P.add("sp", lambda e: e.dma_start(out=lng, in_=ln_g[li].partition_broadcast(128)), writes=["lng"], chan="lng")
        P.add("sp", lambda e: e.dma_start(out=lnb, in_=ln_b[li].partition_broadcast(128)), writes=["lnb"], chan="lnb")
        NB = 2
        NB0 = 3
        if zgate_w is not None:
            wz = A.alloc(8 * 2048, BF16).rearrange("p (k n) -> p k n", k=8)
            load_w(wz, zgate_w, 2048, 8, "wz", 2304)
            xzb = [A.alloc(1024, BF16) for _ in range(NB0)]
            xzT = [A.alloc(1024, BF16).rearrange("p (k t) -> p k t", k=8) for _ in range(NB0)]
            szT = [A.alloc(2048, BF16).rearrange("p (c t) -> p c t", c=16) for _ in range(NB0)]
        ogt = [A.alloc(2048, BF16) for _ in range(NB0)]
        ogT = [A.alloc(2048, BF16).rearrange("p (k t) -> p k t", k=16) for _ in range(NB0)]
        xt = [A.alloc(1024, F32) for _ in range(NB0)]
        pt = [A.alloc(256, F32) for _ in range(NB)]
        s_ = [A.alloc(1024, F32) for _ in range(NB)]
        xn = [A.alloc(1024, F32) for _ in range(NB)]
        xnb = [A.alloc(1024, BF16) for _ in range(NB)]
        xnT = [A.alloc(1024, BF16).rearrange("p (k t) -> p k t", k=8) for _ in range(NB)]
        ptb = [A.alloc(256, BF16) for _ in range(NB)]
        pT = [A.alloc(256, BF16).rearrange("p (k t) -> p k t", k=2) for _ in range(NB)]
        sig = [A.alloc(1024, F32) for _ in range(NB)]
        xo = [A.alloc(1024, F32) for _ in range(NB)]
        stats = [A.alloc(16, F32) for _ in range(NB)]
        ctx = {}

        def stage0(t):
            w = t % NB0
            wfx = "w%d" % w
            r0 = t * 128
            if og_feature_major:
                P.add("sp", lambda e, w=w, r0=r0: e.dma_start(
                    out=ogT[w], in_=ogTs[:, r0:r0 + 128].rearrange("(k f) t -> f k t", f=128)),
                    reads=["ogTs"], writes=["ogT" + wfx], chan="ogT" + wfx)
            else:
                P.add("sp", lambda e, w=w, r0=r0: e.dma_start(out=ogt[w], in_=ogs[r0:r0 + 128, :]),
                      reads=["ogs"], writes=["ogt" + wfx], chan="ogt" + wfx)
                transpose_to(ogT[w], ogt[w], 16, "ogt" + wfx, "ogT" + wfx, evac="act", banks=[6, 7])
            P.add("sp", lambda e, w=w, r0=r0: e.dma_start(out=xt[w], in_=x_src[r0:r0 + 128, :]),
                  reads=["xsrc%d" % t], writes=["xt" + wfx], chan="xt" + wfx)
            if zgate_w is not None:
                P.add("act", lambda e, w=w: e.activation(out=xzb[w], in_=xt[w], func=AF.Copy), reads=["xt" + wfx], writes=["xzb" + wfx])
                transpose_to(xzT[w], xzb[w], 8, "xzb" + wfx, "xzT" + wfx, evac="dve", banks=[6, 7])
                for g8 in range(2):
                    zb = 2 + 2 * g8
                    zbr = ["pb%d" % zb, "pb%d" % (zb + 1)]
                    for hh in range(8):
                        c = g8 * 8 + hh
                        for k in range(8):
                            P.add("pe", lambda e, w=w, zb=zb, hh=hh, c=c, k=k: e.matmul(bank(zb, 2)[:, hh * 128:(hh + 1) * 128], lhsT=wz[:, k, c * 128:(c + 1) * 128],
                                                                                   rhs=xzT[w][:, k, :], start=(k == 0), stop=(k == 7)),
                                  reads=["xzT" + wfx, "wz"], writes=[zbr[hh // 4]])
                    P.add("act", lambda e, w=w, g8=g8, zb=zb: e.activation(out=szT[w][:, g8 * 8:(g8 + 1) * 8, :].rearrange("p c t -> p (c t)"), in_=bank(zb, 2), func=AF.Silu),
                          reads=zbr, writes=["szT" + wfx])
                P.add("pool", lambda e, w=w: e.tensor_tensor(out=ogT[w].rearrange("p k t -> p (k t)"), in0=ogT[w].rearrange("p k t -> p (k t)"),
                                                             in1=szT[w].rearrange("p c t -> p (c t)"), op=ALU.mult),
                      reads=["ogT" + wfx, "szT" + wfx], writes=["ogT" + wfx])

        def stage1a(t):
            u = t % NB
            sfx = "%d" % u
            r0 = t * 128
            w = t % NB0
            wfx = "w%d" % w
            P.add("sp", lambda e, u=u, r0=r0: e.dma_start(out=pt[u], in_=p_in[li, r0:r0 + 128, :]),
                  writes=["pt" + sfx], chan="pt" + sfx)
            yb, ybr = 0, ["pb0", "pb1"]
            for n in range(2):
                for k in range(16):
                    P.add("pe", lambda e, u=u, n=n, k=k, yb=yb: e.matmul(
                        bank(yb + n), lhsT=ogT[t % NB0][:, k, :], rhs=wout[:, k, n * 512:(n + 1) * 512],
                        start=(k == 0), stop=(k == 15)), reads=["ogT" + wfx, "wout"], writes=[ybr[n]])
            ctx[('y', t)] = (yb, ybr)

        def stage1b(t):
            u = t % NB
            sfx = "%d" % u
            r0 = t * 128
            yb, ybr = ctx[('y', t)]
            w = t % NB0
            wfx = "w%d" % w
            P.add("dve", lambda e, u=u, yb=yb, w=w: e.scalar_tensor_tensor(
                out=s_[u], in0=xt[w], scalar=float(ALPHA), in1=bank(yb, 2), op0=ALU.mult, op1=ALU.add),
                reads=["xt" + wfx] + ybr, writes=["s" + sfx])
            for c in range(2):
                P.add("dve", lambda e, u=u, c=c: e.bn_stats(out=stats[u][:, c * 6:(c + 1) * 6], in_=s_[u][:, c * 512:(c + 1) * 512]),
                      reads=["s" + sfx], writes=["st" + sfx])
            P.add("dve", lambda e, u=u: e.bn_aggr(out=stats[u][:, 12:14], in_=stats[u][:, 0:12]),
                  reads=["st" + sfx], writes=["st" + sfx])
            P.add("dve", lambda e, u=u: e.tensor_scalar(out=stats[u][:, 14:15], in0=stats[u][:, 13:14], scalar1=float(EPS), scalar2=None, op0=ALU.add),
                  reads=["st" + sfx], writes=["st" + sfx])
            P.add("act", lambda e, u=u: e.activation(out=stats[u][:, 14:15], in_=stats[u][:, 14:15], func=AF.Sqrt),
                  reads=["st" + sfx], writes=["st" + sfx])
            P.add("dve", lambda e, u=u: e.reciprocal(out=stats[u][:, 15:16], in_=stats[u][:, 14:15]),
                  reads=["st" + sfx], writes=["st" + sfx])
            P.add("dve", lambda e, u=u: e.tensor_scalar(out=xn[u], in0=s_[u], scalar1=stats[u][:, 12:13], scalar2=stats[u][:, 15:16],
                                                        op0=ALU.subtract, op1=ALU.mult),
                  reads=["s" + sfx, "st" + sfx], writes=["xn" + sfx])
            P.add("dve", lambda e, u=u: e.tensor_tensor(out=xn[u], in0=xn[u], in1=lng, op=ALU.mult),
                  reads=["xn" + sfx, "lng"], writes=["xn" + sfx])
            P.add("dve", lambda e, u=u: e.tensor_tensor(out=xn[u], in0=xn[u], in1=lnb, op=ALU.add),
                  reads=["xn" + sfx, "lnb"], writes=["xn" + sfx])
            P.add("act", lambda e, u=u: e.activation(out=xnb[u], in_=xn[u], func=AF.Copy),
                  reads=["xn" + sfx], writes=["xnb" + sfx])

        def stage2a(t):
            u = t % NB
            sfx = "%d" % u
            r0 = t * 128
            transpose_to(xnT[u], xnb[u], 8, "xnb" + sfx, "xnT" + sfx, evac="dve", banks=[6, 7])
            gb, gbr = 2, ["pb2", "pb3"]
            for n in range(2):
                for k in range(8):
                    P.add("pe", lambda e, u=u, n=n, k=k, gb=gb: e.matmul(
                        bank(gb + n), lhsT=xnT[u][:, k, :], rhs=wgate[:, k, n * 512:(n + 1) * 512],
                        start=(k == 0), stop=(k == 7)), reads=["xnT" + sfx, "wgate"], writes=[gbr[n]])
            P.add("act", lambda e, u=u, gb=gb: e.activation(out=sig[u], in_=bank(gb, 2), func=AF.Sigmoid),
                  reads=gbr, writes=["sig" + sfx])
            P.add("act", lambda e, u=u: e.activation(out=ptb[u], in_=pt[u], func=AF.Copy),
                  reads=["pt" + sfx], writes=["ptb" + sfx])
            transpose_to(pT[u], ptb[u], 2, "ptb" + sfx, "pT" + sfx, evac="dve", banks=[6, 7])
            lb, lbr = 4, ["pb4", "pb5"]
            for n in range(2):
                for k in range(2):
                    P.add("pe", lambda e, u=u, n=n, k=k, lb=lb: e.matmul(
                        bank(lb + n), lhsT=pT[u][:, k, :], rhs=wple[:, k, n * 512:(n + 1) * 512],
                        start=(k == 0), stop=(k == 1)), reads=["pT" + sfx, "wple"], writes=[lbr[n]])
            ctx[('l', t)] = (lb, lbr)

        def stage2b(t):
            u = t % NB
            sfx = "%d" % u
            r0 = t * 128
            lb, lbr = ctx[('l', t)]
            P.add("dve", lambda e, u=u, lb=lb: e.tensor_tensor(out=xo[u], in0=bank(lb, 2), in1=sig[u], op=ALU.mult),
                  reads=lbr + ["sig" + sfx], writes=["xo" + sfx])
            P.add("dve", lambda e, u=u: e.tensor_tensor(out=xo[u], in0=xo[u], in1=xn[u], op=ALU.add),
                  reads=["xo" + sfx, "xn" + sfx], writes=["xo" + sfx])
            P.add("poolq", lambda e, u=u, r0=r0: e.dma_start(out=x_dst[r0:r0 + 128, :], in_=xo[u]),
                  reads=["xo" + sfx], writes=["xdst%d" % t], chan="xo" + sfx)


        stage0(0)
        if NT > 1:
            stage0(1)
        stage1a(0)
        stage1b(0)
        for t in range(NT):
            if t + 2 < NT:
                stage0(t + 2)
            if t + 1 < NT:
                stage1a(t + 1)
            stage2a(t)
            if t + 1 < NT:
                stage1b(t + 1)
            stage2b(t)

    def gla_a(j, x_src):
        P.barrier()
        A.release(persist_mark)
        w_in = gla_w_in[j]
        wqk = A.alloc(8 * 2048, BF16).rearrange("p (k n) -> p k n", k=8)
        wvz = A.alloc(8 * 4096, BF16).rearrange("p (k n) -> p k n", k=8)
        wa = A.alloc(8 * 16, BF16).rearrange("p (k n) -> p k n", k=8)
        wa2 = A.alloc(1024, BF16)
        gng = A.alloc(2048, F32)
        S = A.alloc(4 * 2 * 512, F32).rearrange("p (h c n) -> p h c n", h=4, c=2)
        Sb = A.alloc(4 * 2 * 512, BF16).rearrange("p (h c n) -> p h c n", h=4, c=2)
        load_w(wqk, w_in, 2048, 8, "wqk", 0)
        load_w(wvz, w_in, 4096, 8, "wvz", 2048)
        load_w(wa, w_in, 16, 8, "wa", 6144)
        P.add("poolq", lambda e: e.dma_start(out=wa2[0:16, :], in_=gla_w_a2[j]), writes=["wa2"], chan="wa2")
        P.add("poolq", lambda e: e.dma_start(out=wa2[16:17, :], in_=gla_b_a[j].rearrange("(o n) -> o n", o=1)), writes=["wa2"], chan="wa2b")
        P.add("sp", lambda e: e.dma_start(out=gng, in_=gla_gn_g[j].partition_broadcast(128)), writes=["gng"], chan="gng")
        P.add("dve", lambda e: e.memset(S.rearrange("p h c n -> p (h c n)"), 0.0), writes=["S%d%d" % (h, ci) for h in range(4) for ci in range(2)])
        P.add("pool", lambda e: e.memset(Sb.rearrange("p h c n -> p (h c n)"), 0.0), writes=["Sb%d" % h for h in range(4)])
        NB = 2
        xt = [A.alloc(1024, F32) for _ in range(NB)]
        xb = [A.alloc(1024, BF16) for _ in range(NB)]
        xT = [A.alloc(1024, BF16).rearrange("p (k t) -> p k t", k=8) for _ in range(NB)]
        aT = A.alloc(128, BF16)
        sp_ = A.alloc(1024, F32)
        eG = A.alloc(1024, F32).rearrange("p (c t) -> p c t", c=8)
        enG = A.alloc(1024, F32).rearrange("p (c t) -> p c t", c=8)
        kdec = A.alloc(1024, F32)
        qt = A.alloc(1024, BF16).rearrange("p (c t) -> p c t", c=8)
        kt = A.alloc(1024, BF16).rearrange("p (c t) -> p c t", c=8)
        ks = A.alloc(1024, BF16)
        vb = A.alloc(2048, BF16)
        zg = A.alloc(2048, F32)
        ATb4 = A.alloc(512, BF16).rearrange("p (h t) -> p h t", h=4)
        sq = A.alloc(512, F32)
        og = [A.alloc(2048, BF16) for _ in range(NB)]
        nrm = A.alloc(8, F32)
        P.add("dve", lambda e: e.memset(aT[0:32, :], 1.0), writes=["aT"])
        for t in range(NT):
            u = t % NB
            sfx = "%d" % u
            r0 = t * 128
            P.add("sp", lambda e, u=u, r0=r0: e.dma_start(out=xt[u], in_=x_src[r0:r0 + 128, :]),
                  reads=["xsrc%d" % t], writes=["xt" + sfx], chan="xt" + sfx)
            P.add("act", lambda e, u=u: e.activation(out=xb[u], in_=xt[u], func=AF.Copy),
                  reads=["xt" + sfx], writes=["xb" + sfx])
            transpose_to(xT[u], xb[u], 8, "xb" + sfx, "xT" + sfx, evac="dve")
            ab, abr = palloc(1)
            for k in range(8):
                P.add("pe", lambda e, u=u, k=k, ab=ab: e.matmul(bank(ab)[0:16, 0:128], lhsT=wa[:, k, :], rhs=xT[u][:, k, :],
                                                             start=(k == 0), stop=(k == 7)),
                      reads=["xT" + sfx, "wa"], writes=abr)
            P.add("dve", lambda e, ab=ab: e.tensor_copy(out=aT[0:16, :], in_=bank(ab)[0:16, 0:128]), reads=abr, writes=["aT"])
            lb, lbr = palloc(2)
            for n in range(2):
                P.add("pe", lambda e, n=n, lb=lb: e.matmul(bank(lb + n), lhsT=aT[0:17, :], rhs=wa2[0:17, n * 512:(n + 1) * 512],
                                                         start=True, stop=True), reads=["aT", "wa2"], writes=[lbr[n]])
            P.add("act", lambda e, lb=lb: e.activation(out=sp_, in_=bank(lb, 2), func=AF.Exp, scale=-1.0), reads=lbr, writes=["sp"])
            P.add("act", lambda e: e.activation(out=sp_, in_=sp_, func=AF.Ln, bias=1.0), reads=["sp"], writes=["sp"])
            for half in range(2):
                vb_, vbr = palloc(2)
                for n in range(2):
                    for k in range(8):
                        P.add("pe", lambda e, u=u, n=n, k=k, vb_=vb_, half=half: e.matmul(
                            bank(vb_ + n), lhsT=xT[u][:, k, :], rhs=wvz[:, k, half * 1024 + n * 512:half * 1024 + (n + 1) * 512],
                            start=(k == 0), stop=(k == 7)), reads=["xT" + sfx, "wvz"], writes=[vbr[n]])
                P.add("act", lambda e, vb_=vb_, half=half: e.activation(out=vb[:, half * 1024:(half + 1) * 1024], in_=bank(vb_, 2), func=AF.Copy),
                      reads=vbr, writes=["vb%d" % half])
            for half in range(2):
                zb, zbr = palloc(2)
                for n in range(2):
                    for k in range(8):
                        P.add("pe", lambda e, u=u, n=n, k=k, zb=zb, half=half: e.matmul(
                            bank(zb + n), lhsT=xT[u][:, k, :], rhs=wvz[:, k, 2048 + half * 1024 + n * 512:2048 + half * 1024 + (n + 1) * 512],
                            start=(k == 0), stop=(k == 7)), reads=["xT" + sfx, "wvz"], writes=[zbr[n]])
                P.add("act", lambda e, zb=zb, half=half: e.activation(out=zg[:, half * 1024:(half + 1) * 1024], in_=bank(zb, 2), func=AF.Silu),
                      reads=zbr, writes=["zg%d" % half])
                P.add("pool", lambda e, half=half: e.tensor_tensor(out=zg[:, half * 1024:(half + 1) * 1024], in0=zg[:, half * 1024:(half + 1) * 1024],
                                                                 in1=gng[:, half * 1024:(half + 1) * 1024], op=ALU.mult),
                      reads=["zg%d" % half, "gng"], writes=["zg%d" % half])
            gb, gbr = palloc(2)
            for c in range(8):
                P.add("pe", lambda e, c=c, gb=gb: e.matmul(bank(gb, 2)[:, c * 128:(c + 1) * 128], lhsT=sp_[:, c * 128:(c + 1) * 128], rhs=umat,
                                                         start=True, stop=True), reads=["sp", "umat"], writes=[gbr[c // 4]])
            P.add("act", lambda e, gb=gb: e.activation(out=eG.rearrange("p c t -> p (c t)"), in_=bank(gb, 2), func=AF.Exp), reads=gbr, writes=["eG"])
            P.add("act", lambda e, gb=gb: e.activation(out=enG.rearrange("p c t -> p (c t)"), in_=bank(gb, 2), func=AF.Exp, scale=-1.0), reads=gbr, writes=["enG"])
            kb, kbr = palloc(2)
            for n in range(2):
                P.add("pe", lambda e, n=n, kb=kb: e.matmul(bank(kb + n), lhsT=lmat, rhs=sp_[:, n * 512:(n + 1) * 512], start=True, stop=True),
                      reads=["sp", "lmat"], writes=[kbr[n]])
            P.add("act", lambda e, kb=kb: e.activation(out=kdec, in_=bank(kb, 2), func=AF.Exp), reads=kbr, writes=["kdec"])
            for which, dst, gate, sc in (("q", qt, eG, 1.0 / 16.0), ("k", kt, enG, 1.0)):
                qb, qbr = palloc(2)
                c0 = 0 if which == "q" else 1024
                for c in range(8):
                    for k in range(8):
                        P.add("pe", lambda e, u=u, c=c, k=k, qb=qb, c0=c0: e.matmul(
                            bank(qb, 2)[:, c * 128:(c + 1) * 128], lhsT=wqk[:, k, c0 + c * 128:c0 + (c + 1) * 128], rhs=xT[u][:, k, :],
                            start=(k == 0), stop=(k == 7)), reads=["xT" + sfx, "wqk"], writes=[qbr[c // 4]])
                P.add("dve", lambda e, qb=qb, dst=dst, gate=gate, sc=sc: e.scalar_tensor_tensor(
                    out=dst.rearrange("p c t -> p (c t)"), in0=bank(qb, 2), scalar=float(sc), in1=gate.rearrange("p c t -> p (c t)"),
                    op0=ALU.mult, op1=ALU.mult), reads=qbr + ["eG" if which == "q" else "enG"], writes=[which + "t"])
            tb, tbr = palloc(2)
            for n in range(2):
                for k in range(8):
                    P.add("pe", lambda e, u=u, n=n, k=k, tb=tb: e.matmul(
                        bank(tb + n), lhsT=xT[u][:, k, :], rhs=wqk[:, k, 1024 + n * 512:1024 + (n + 1) * 512],
                        start=(k == 0), stop=(k == 7)), reads=["xT" + sfx, "wqk"], writes=[tbr[n]])
            P.add("dve", lambda e, tb=tb: e.tensor_tensor(out=ks, in0=bank(tb, 2), in1=kdec, op=ALU.mult), reads=tbr + ["kdec"], writes=["ks"])
            P.add("dve", lambda e: e.memset(nrm, 0.0), writes=["nrm"])
            for h in range(4):
                for ci in range(2):
                    c = 2 * h + ci
                    P.add("pe", lambda e, c=c, ci=ci, h=h: e.matmul(bank(h)[:, 0:128], lhsT=kt[:, c, :], rhs=qt[:, c, :],
                                                                 start=(ci == 0), stop=(ci == 1)), reads=["kt", "qt"], writes=["pb%d" % h])
                P.add("dve", lambda e, h=h: e.tensor_tensor(out=ATb4[:, h, :], in0=bank(h)[:, 0:128], in1=tri, op=ALU.mult),
                      reads=["pb%d" % h, "tri"], writes=["ATb%d" % h])
            for h in range(4):
                vh = vb[:, h * 512:(h + 1) * 512]
                vres = "vb%d" % (h // 2)
                ob = 4 + h
                obr = ["pb%d" % ob]
                P.add("pe", lambda e, ob=ob, vh=vh, h=h: e.matmul(bank(ob), lhsT=ATb4[:, h, :], rhs=vh, start=True, stop=False),
                      reads=["ATb%d" % h, vres], writes=obr)
                for ci in range(2):
                    c = 2 * h + ci
                    P.add("pe", lambda e, ob=ob, c=c, ci=ci, h=h: e.matmul(bank(ob), lhsT=qt[:, c, :], rhs=Sb[:, h, ci, :], start=False, stop=(ci == 1)),
                          reads=["qt", "Sb%d" % h], writes=obr)
                for ci in range(2):
                    c = 2 * h + ci
                    sb_ = 2 * (h % 2) + ci
                    sbr = ["pb%d" % sb_]
                    P.add("pe", lambda e, sb_=sb_, c=c, vh=vh: e.matmul(bank(sb_), lhsT=ks[:, c * 128:(c + 1) * 128], rhs=vh, start=True, stop=True),
                          reads=["ks", vres], writes=sbr)
                    P.add("dve", lambda e, sb_=sb_, c=c, ci=ci, h=h: e.scalar_tensor_tensor(
                        out=S[:, h, ci, :], in0=S[:, h, ci, :], scalar=eG[:, c, 127:128], in1=bank(sb_), op0=ALU.mult, op1=ALU.add),
                        reads=["S%d%d" % (h, ci), "eG"] + sbr, writes=["S%d%d" % (h, ci)])
            for h in range(4):
                P.add("act", lambda e, h=h: e.activation(out=sq, in_=bank(4 + h), func=AF.Square, accum_out=nrm[:, h:h + 1]),
                      reads=["pb%d" % (4 + h)], writes=["sq", "nrm"])
            P.add("dve", lambda e: e.tensor_scalar(out=nrm[:, 4:8], in0=nrm[:, 0:4], scalar1=1.0 / 512, scalar2=float(EPS), op0=ALU.mult, op1=ALU.add),
                  reads=["nrm"], writes=["nrm"])
            P.add("act", lambda e: e.activation(out=nrm[:, 4:8], in_=nrm[:, 4:8], func=AF.Sqrt), reads=["nrm"], writes=["nrm"])
            P.add("dve", lambda e: e.reciprocal(out=nrm[:, 4:8], in_=nrm[:, 4:8]), reads=["nrm"], writes=["nrm"])
            for h in range(4):
                P.add("dve", lambda e, h=h, u=u: e.scalar_tensor_tensor(
                    out=og[u][:, h * 512:(h + 1) * 512], in0=bank(4 + h), scalar=nrm[:, 4 + h:5 + h], in1=zg[:, h * 512:(h + 1) * 512],
                    op0=ALU.mult, op1=ALU.mult), reads=["pb%d" % (4 + h), "nrm", "zg%d" % (h // 2)], writes=["og" + sfx])
            for h in range(4):
                for ci in range(2):
                    P.add("act", lambda e, h=h, ci=ci: e.activation(out=Sb[:, h, ci, :], in_=S[:, h, ci, :], func=AF.Copy),
                          reads=["S%d%d" % (h, ci)], writes=["Sb%d" % h])
            pstate["ptr"] = 0
            P.add("poolq", lambda e, u=u, r0=r0: e.dma_start(out=ogs[r0:r0 + 128, :], in_=og[u]),
                  reads=["og" + sfx], writes=["ogs"], chan="ogst" + sfx)

    def conv_a(x_src):
        P.barrier()
        A.release(persist_mark)
        TG = 256
        NG = L // TG
        w_in = conv_w_in[0]
        wc = A.alloc(8 * 6144, BF16).rearrange("p (k n) -> p k n", k=8)
        load_w(wc, w_in, 6144, 8, "wc", 0)
        prm_tm = A.alloc(2048, F32)
        P.add("sp", lambda e: e.dma_start(out=prm_tm[0:31, :], in_=conv_dw_w[0]), writes=["prm_tm"], chan="pr0")
        P.add("sp", lambda e: e.dma_start(out=prm_tm[31:32, :], in_=conv_dw_b[0].rearrange("(o n) -> o n", o=1)), writes=["prm_tm"], chan="pr1")
        P.add("sp", lambda e: e.dma_start(out=prm_tm[32:33, :], in_=conv_ln_g[0].rearrange("(o n) -> o n", o=1)), writes=["prm_tm"], chan="pr2")
        P.add("sp", lambda e: e.dma_start(out=prm_tm[33:34, :], in_=conv_ln_b[0].rearrange("(o n) -> o n", o=1)), writes=["prm_tm"], chan="pr3")
        prm = A.alloc(16 * 34, F32).rearrange("p (c k) -> p c k", c=16)
        for c in range(16):
            b, br = palloc(1)
            P.add("pe", lambda e, c=c, b=b: e.transpose(out=bank(b)[:, 0:34], in_=prm_tm[0:34, c * 128:(c + 1) * 128], identity=ident_f[0:34, 0:34]),
                  reads=["prm_tm", "ident_f"], writes=br)
            P.add("dve", lambda e, c=c, b=b: e.tensor_copy(out=prm[:, c, :], in_=bank(b)[:, 0:34]), reads=br, writes=["prm"])
        ones_f = A.alloc(128, F32)
        P.add("dve", lambda e: e.memset(ones_f, 1.0 / 2048.0), writes=["ones_f"])
        xt = [A.alloc(1024, F32) for _ in range(2)]
        xb = [A.alloc(1024, BF16) for _ in range(2)]
        xTg = A.alloc(8 * TG, BF16).rearrange("p (k t) -> p k t", k=8)
        uT = A.alloc(16 * (TG + 30), BF16).rearrange("p (c t) -> p c t", c=16)
        dg = [A.alloc(128, BF16) for _ in range(9)]
        szT = A.alloc(16 * TG, BF16).rearrange("p (c t) -> p c t", c=16)
        cT = A.alloc(16 * TG, F32).rearrange("p (c t) -> p c t", c=16)
        sgt = [A.alloc(TG, F32) for _ in range(2)]
        sqt = [A.alloc(TG, F32) for _ in range(2)]
        mean_t = A.alloc(TG, F32)
        rstd_t = A.alloc(TG, F32)
        t1 = [A.alloc(TG, F32) for _ in range(2)]
        t2 = [A.alloc(TG, F32) for _ in range(2)]
        ogTt = A.alloc(16 * TG, BF16).rearrange("p (c t) -> p c t", c=16)
        P.add("dve", lambda e: e.memset(uT.rearrange("p c t -> p (c t)"), 0.0), writes=["uT%d" % c for c in range(16)])
        for g in range(NG):
            for i in range(TG // 128):
                t = g * (TG // 128) + i
                u = t % 2
                sfx = "%d" % u
                r0 = t * 128
                P.add("sp", lambda e, u=u, r0=r0: e.dma_start(out=xt[u], in_=x_src[r0:r0 + 128, :]), writes=["xt" + sfx], chan="xt" + sfx)
                P.add("act", lambda e, u=u: e.activation(out=xb[u], in_=xt[u], func=AF.Copy), reads=["xt" + sfx], writes=["xb" + sfx])
                transpose_to(xTg[:, :, i * 128:(i + 1) * 128], xb[u], 8, "xb" + sfx, "xTg", evac="dve")
            for c in range(16):
                v = c % 2
                pa, par = palloc(1)
                pg, pgr = palloc(1)
                pz, pzr = palloc(1)
                for (pb_, pbr_, off) in ((pa, par, 0), (pg, pgr, 2048), (pz, pzr, 4096)):
                    for k in range(8):
                        P.add("pe", lambda e, pb_=pb_, off=off, c=c, k=k: e.matmul(
                            bank(pb_)[:, 0:TG], lhsT=wc[:, k, off + c * 128:off + (c + 1) * 128], rhs=xTg[:, k, :],
                            start=(k == 0), stop=(k == 7)), reads=["xTg", "wc"], writes=pbr_)
                P.add("act", lambda e, pg=pg, v=v: e.activation(out=sgt[v], in_=bank(pg)[:, 0:TG], func=AF.Sigmoid), reads=pgr, writes=["sgt%d" % v])
                if g > 0:
                    P.add("pool", lambda e, c=c: e.tensor_copy(out=uT[:, c, 0:30], in_=uT[:, c, TG:TG + 30]), reads=["uT%d" % c], writes=["uT%d" % c])
                P.add("dve", lambda e, pa=pa, v=v, c=c: e.tensor_tensor(out=uT[:, c, 30:30 + TG], in0=bank(pa)[:, 0:TG], in1=sgt[v], op=ALU.mult),
                      reads=par + ["sgt%d" % v, "uT%d" % c], writes=["uT%d" % c])
                P.add("act", lambda e, pz=pz, c=c: e.activation(out=szT[:, c, :], in_=bank(pz)[:, 0:TG], func=AF.Silu), reads=pzr, writes=["szT%d" % c])
            for c in range(16):
                cb, cbr = palloc(1)
                for k in range(CONV_W):
                    n_ = c * CONV_W + k
                    di = n_ % len(dg)
                    eng = ("dve", "pool")[n_ % 2]
                    if eng == "dve":
                        P.add("dve", lambda e, c=c, k=k, di=di: e.tensor_scalar(out=dg[di], in0=ident_b, scalar1=prm[:, c, k:k + 1], scalar2=None, op0=ALU.mult),
                              reads=["ident_b", "prm"], writes=["dg%d" % di])
                    elif eng == "pool":
                        P.add("pool", lambda e, c=c, k=k, di=di: e.tensor_scalar(out=dg[di], in0=ident_b, scalar1=prm[:, c, k:k + 1], scalar2=1.0, op0=ALU.mult, op1=ALU.mult),
                              reads=["ident_b", "prm"], writes=["dg%d" % di])
                    else:
                        P.add("act", lambda e, c=c, k=k, di=di: e.activation(out=dg[di], in_=ident_b, func=AF.Copy, scale=prm[:, c, k:k + 1]),
                              reads=["ident_b", "prm"], writes=["dg%d" % di])
                    P.add("pe", lambda e, c=c, k=k, di=di, cb=cb: e.matmul(bank(cb)[:, 0:TG], lhsT=dg[di], rhs=uT[:, c, k:k + TG], start=(k == 0), stop=(k == CONV_W - 1)),
                          reads=["dg%d" % di, "uT%d" % c], writes=cbr)
                P.add("act", lambda e, c=c, cb=cb: e.activation(out=cT[:, c, :], in_=bank(cb)[:, 0:TG], func=AF.Identity, bias=prm[:, c, 31:32]),
                      reads=cbr + ["prm"], writes=["cT%d" % c])
            ps, psr = palloc(1)
            pq, pqr = palloc(1)
            for c in range(16):
                v = c % 2
                P.add("pe", lambda e, ps=ps, c=c: e.matmul(bank(ps)[:, 0:TG], lhsT=ones_f, rhs=cT[:, c, :], start=(c == 0), stop=(c == 15)),
                      reads=["ones_f", "cT%d" % c], writes=psr)
                P.add("act", lambda e, c=c, v=v: e.activation(out=sqt[v], in_=cT[:, c, :], func=AF.Square), reads=["cT%d" % c], writes=["sqt%d" % v])
                P.add("pe", lambda e, pq=pq, c=c, v=v: e.matmul(bank(pq)[:, 0:TG], lhsT=ones_f, rhs=sqt[v], start=(c == 0), stop=(c == 15)),
                      reads=["ones_f", "sqt%d" % v], writes=pqr)
            P.add("act", lambda e, ps=ps: e.activation(out=mean_t, in_=bank(ps)[:, 0:TG], func=AF.Copy), reads=psr, writes=["mean_t"])
            P.add("dve", lambda e: e.tensor_tensor(out=rstd_t, in0=mean_t, in1=mean_t, op=ALU.mult), reads=["mean_t"], writes=["rstd_t"])
            P.add("dve", lambda e, pq=pq: e.tensor_tensor(out=rstd_t, in0=bank(pq)[:, 0:TG], in1=rstd_t, op=ALU.subtract), reads=pqr + ["rstd_t"], writes=["rstd_t"])
            P.add("dve", lambda e: e.tensor_scalar(out=rstd_t, in0=rstd_t, scalar1=float(EPS), scalar2=None, op0=ALU.add), reads=["rstd_t"], writes=["rstd_t"])
            P.add("act", lambda e: e.activation(out=rstd_t, in_=rstd_t, func=AF.Sqrt), reads=["rstd_t"], writes=["rstd_t"])
            P.add("dve", lambda e: e.reciprocal(out=rstd_t, in_=rstd_t), reads=["rstd_t"], writes=["rstd_t"])
            for c in range(16):
                v = c % 2
                P.add("dve", lambda e, c=c, v=v: e.tensor_tensor(out=t1[v], in0=cT[:, c, :], in1=mean_t, op=ALU.subtract), reads=["cT%d" % c, "mean_t"], writes=["t1%d" % v])
                P.add("dve", lambda e, c=c, v=v: e.tensor_tensor(out=t1[v], in0=t1[v], in1=rstd_t, op=ALU.mult), reads=["t1%d" % v, "rstd_t"], writes=["t1%d" % v])
                P.add("act", lambda e, c=c, v=v: e.activation(out=t2[v], in_=t1[v], func=AF.Silu, scale=prm[:, c, 32:33], bias=prm[:, c, 33:34]),
                      reads=["t1%d" % v, "prm"], writes=["t2%d" % v])
                P.add("pool", lambda e, c=c, v=v: e.tensor_tensor(out=ogTt[:, c, :], in0=t2[v], in1=szT[:, c, :], op=ALU.mult),
                      reads=["t2%d" % v, "szT%d" % c], writes=["ogTt"])
            P.add("poolq", lambda e, g=g: e.dma_start(out=ogTs[:, g * TG:(g + 1) * TG].rearrange("(c f) t -> f c t", f=128), in_=ogTt),
                  reads=["ogTt"], writes=["ogTs"], chan="ogTst")

    def dsa_a(x_src):
        P.barrier()
        A.release(persist_mark)
        NIT = 18
        NOSAME = True
        PI = math.pi
        w_in = dsa_w_in[0]
        WC = 2888
        W = A.alloc(8 * WC, BF16).rearrange("p (k n) -> p k n", k=8)
        Wr = A.alloc(8 * 688, BF16).rearrange("p (k n) -> p k n", k=8)
        load_w(W[:, :, 0:2304], w_in, 2304, 8, "W", 0)
        load_w(W[:, :, 2304:WC], w_in, 584, 8, "W", 4352)
        load_w(Wr, dsa_w_rot, 688, 8, "Wr", 0)
        ropec = A.alloc(8, F32)
        negm = A.alloc(128, F32)
        pw2 = A.alloc(NIT, F32)
        sel = A.alloc(512, F32).rearrange("p (g m) -> p g m", g=4)
        ones_b = A.alloc(32, BF16)
        P.add("sp", lambda e: e.dma_start(out=ropec[0:32, :], in_=c_rope), writes=["ropec"], chan="cr0")
        P.add("sp", lambda e: e.dma_start(out=negm, in_=c_neg), writes=["negm"], chan="cr1")
        P.add("sp", lambda e: e.dma_start(out=pw2, in_=c_pw2), writes=["pw2"], chan="cr2")
        P.add("sp", lambda e: e.dma_start(out=sel.rearrange("p g m -> p (g m)"), in_=c_sel), writes=["sel"], chan="cr3")
        P.add("dve", lambda e: e.memset(ones_b, 1.0), writes=["ones_b"])
        kT = A.alloc(L, BF16)
        vtok = A.alloc(NT * 128, BF16).rearrange("p (b d) -> p b d", b=NT)
        kiT = A.alloc(L, BF16)
        xt = A.alloc(1024, F32)
        xb = A.alloc(1024, BF16)
        xT = [A.alloc(1024, BF16).rearrange("p (k t) -> p k t", k=8) for _ in range(2)]
        posi = A.alloc(128, I32)
        posf = A.alloc(128, F32)
        ang = A.alloc(128, F32)
        arg = A.alloc(128, F32)
        argi = A.alloc(128, I32)
        argk = A.alloc(128, F32)
        C32 = A.alloc(128, F32)
        S32 = A.alloc(128, F32)
        C16 = A.alloc(128, F32)
        S16 = A.alloc(128, F32)
        tA = A.alloc(1024, F32)
        tB = A.alloc(1024, F32)
        qT = [A.alloc(2048, BF16).rearrange("p (h t) -> p h t", h=16) for _ in range(2)]
        qiT = A.alloc(1024, BF16).rearrange("p (h t) -> p h t", h=8)
        wi = A.alloc(8, F32)
        acc = A.alloc(L, F32)
        maskb = A.alloc(L, BF16)
        maskT = [A.alloc(NT * 128, BF16).rearrange("p (b t) -> p b t", b=NT) for _ in range(2)]
        rbuf = [A.alloc(512, F32) for _ in range(2)]
        probs = [A.alloc(2048, BF16) for _ in range(2)]
        den_sb = A.alloc(512, F32)
        oS = A.alloc(2048, F32)
        ogTt = A.alloc(2048, BF16).rearrange("p (c t) -> p c t", c=16)
        sm = A.alloc(8 + 2 * NIT, F32)
        lo, mid, cnt, tmp, W0 = (sm[:, i:i + 1] for i in range(5))
        wcols = sm[:, 8:8 + NIT]
        nhw = sm[:, 8 + NIT:8 + 2 * NIT]

        def bc(ap32, n):
            return ap32.unsqueeze(1).to_broadcast([ap32.shape[0], n, 128])

        def proj_fm(dst_bank_ap, col0, ncol, u, wres, Wt, brs):
            for k in range(8):
                P.add("pe", lambda e, k=k: e.matmul(dst_bank_ap, lhsT=Wt[:, k, col0:col0 + ncol], rhs=xT[u][:, k, :],
                                                     start=(k == 0), stop=(k == 7)), reads=["xT%d" % u, wres], writes=brs)

        def stP(t):
            u = t % 2
            sfx = "%d" % u
            r0 = t * 128
            qTu = qT[u]
            qres = "qT%d" % u
            P.add("sp", lambda e: e.dma_start(out=xt, in_=x_src[r0:r0 + 128, :]), writes=["xt"], chan="xt")
            P.add("sp", lambda e: e.dma_start(out=posi[0:32, :], in_=pos_in[r0:r0 + 128].partition_broadcast(32)), writes=["posi"], chan="posi")
            P.add("act", lambda e: e.activation(out=xb, in_=xt, func=AF.Copy), reads=["xt"], writes=["xb"])
            transpose_to(xT[u], xb, 8, "xb", "xT" + sfx, evac="dve")
            P.add("dve", lambda e: e.tensor_copy(out=posf[0:32, :], in_=posi[0:32, :]), reads=["posi"], writes=["posf"])
            for (n, ci, Ct, St_, nm) in ((32, 0, C32, S32, "h"), (16, 3, C16, S16, "i")):
                P.add("dve", lambda e, n=n, ci=ci: e.tensor_scalar(out=ang[0:n, :], in0=posf[0:n, :], scalar1=ropec[0:n, ci:ci + 1], scalar2=None, op0=ALU.mult),
                      reads=["posf", "ropec"], writes=["ang"])
                for (off, dst, scale_, wn) in ((0.0, St_, ropec[0:n, ci + 1:ci + 2], "S" + nm), (0.25, Ct, 2 * PI, "C" + nm)):
                    P.add("dve", lambda e, n=n, off=off: e.tensor_scalar(out=arg[0:n, :], in0=ang[0:n, :], scalar1=float(off), scalar2=None, op0=ALU.add),
                          reads=["ang"], writes=["arg"])
                    P.add("dve", lambda e, n=n: e.tensor_copy(out=argi[0:n, :], in_=arg[0:n, :]), reads=["arg"], writes=["argi"])
                    P.add("dve", lambda e, n=n: e.tensor_copy(out=argk[0:n, :], in_=argi[0:n, :]), reads=["argi"], writes=["argk"])
                    P.add("dve", lambda e, n=n: e.tensor_tensor(out=arg[0:n, :], in0=arg[0:n, :], in1=argk[0:n, :], op=ALU.subtract), reads=["arg", "argk"], writes=["arg"])
                    P.add("dve", lambda e, n=n: e.tensor_scalar(out=argk[0:n, :], in0=arg[0:n, :], scalar1=0.5, scalar2=None, op0=ALU.is_gt), reads=["arg"], writes=["argk"])
                    P.add("dve", lambda e, n=n: e.tensor_tensor(out=arg[0:n, :], in0=arg[0:n, :], in1=argk[0:n, :], op=ALU.subtract), reads=["arg", "argk"], writes=["arg"])
                    P.add("dve", lambda e, n=n: e.tensor_scalar(out=argk[0:n, :], in0=arg[0:n, :], scalar1=-0.5, scalar2=None, op0=ALU.is_lt), reads=["arg"], writes=["argk"])
                    P.add("dve", lambda e, n=n: e.tensor_tensor(out=arg[0:n, :], in0=arg[0:n, :], in1=argk[0:n, :], op=ALU.add), reads=["arg", "argk"], writes=["arg"])
                    P.add("act", lambda e, n=n, dst=dst, scale_=scale_: e.activation(out=dst[0:n, :], in_=arg[0:n, :], func=AF.Sin, scale=scale_),
                          reads=["arg", "ropec"], writes=[wn])
            for g8 in range(2):
                qb, qbr = palloc(2)
                for hh in range(8):
                    h = g8 * 8 + hh
                    proj_fm(bank(qb, 2)[:, hh * 128:(hh + 1) * 128], h * 128, 128, u, "W", W, [qbr[hh // 4]])
                rb, rbr = palloc(2)
                for hh in range(8):
                    h = g8 * 8 + hh
                    proj_fm(bank(rb, 2)[0:32, hh * 128:(hh + 1) * 128], h * 32, 32, u, "Wr", Wr, [rbr[hh // 4]])
                P.add("act", lambda e, g8=g8, qb=qb: e.activation(out=qTu[:, g8 * 8:(g8 + 1) * 8, :].rearrange("p h t -> p (h t)"), in_=bank(qb, 2), func=AF.Copy),
                      reads=qbr, writes=[qres])
                P.add("dve", lambda e, qb=qb: e.tensor_tensor(out=tA[0:32, :].rearrange("p (h t) -> p h t", h=8), in0=bank(qb, 2)[0:32, :].rearrange("p (h t) -> p h t", h=8),
                                                             in1=bc(C32[0:32, :], 8), op=ALU.mult), reads=qbr + ["Ch", qres], writes=["tA"])
                P.add("dve", lambda e, rb=rb: e.tensor_tensor(out=tB[0:32, :].rearrange("p (h t) -> p h t", h=8), in0=bank(rb, 2)[0:32, :].rearrange("p (h t) -> p h t", h=8),
                                                             in1=bc(S32[0:32, :], 8), op=ALU.mult), reads=rbr + ["Sh"], writes=["tB"])
                P.add("dve", lambda e, g8=g8: e.tensor_tensor(out=qTu[0:32, g8 * 8:(g8 + 1) * 8, :].rearrange("p h t -> p (h t)"), in0=tA[0:32, :], in1=tB[0:32, :], op=ALU.add),
                      reads=["tA", "tB"], writes=[qres])
            kb_, kbr_ = palloc(1)
            proj_fm(bank(kb_)[:, 0:128], 2048, 128, u, "W", W, kbr_)
            kr, krr = palloc(1)
            proj_fm(bank(kr)[0:32, 0:128], 512, 32, u, "Wr", Wr, krr)
            P.add("act", lambda e: e.activation(out=kT[:, r0:r0 + 128], in_=bank(kb_)[:, 0:128], func=AF.Copy), reads=kbr_, writes=["kT"])
            P.add("dve", lambda e: e.tensor_tensor(out=tA[0:32, 0:128], in0=bank(kb_)[0:32, 0:128], in1=C32[0:32, :], op=ALU.mult), reads=kbr_ + ["Ch"], writes=["tA"])
            P.add("dve", lambda e: e.tensor_tensor(out=tB[0:32, 0:128], in0=bank(kr)[0:32, 0:128], in1=S32[0:32, :], op=ALU.mult), reads=krr + ["Sh"], writes=["tB"])
            P.add("dve", lambda e: e.tensor_tensor(out=kT[0:32, r0:r0 + 128], in0=tA[0:32, 0:128], in1=tB[0:32, 0:128], op=ALU.add), reads=["tA", "tB"], writes=["kT"])
            vb_, vbr_ = palloc(1)
            for k in range(8):
                P.add("pe", lambda e, k=k: e.matmul(bank(vb_)[:, 0:128], lhsT=xT[u][:, k, :], rhs=W[:, k, 2176:2304], start=(k == 0), stop=(k == 7)),
                      reads=["xT" + sfx, "W"], writes=vbr_)
            P.add("act", lambda e: e.activation(out=vtok[:, t, :], in_=bank(vb_)[:, 0:128], func=AF.Copy), reads=vbr_, writes=["vtok"])
            wb_, wbr_ = palloc(1)
            for k in range(8):
                P.add("pe", lambda e, k=k: e.matmul(bank(wb_)[:, 0:8], lhsT=xT[u][:, k, :], rhs=W[:, k, 2880:2888], start=(k == 0), stop=(k == 7)),
                      reads=["xT" + sfx, "W"], writes=wbr_)
            P.add("dve", lambda e: e.tensor_copy(out=wi, in_=bank(wb_)[:, 0:8]), reads=wbr_, writes=["wi"])
            ib, ibr = palloc(2)
            for h in range(8):
                proj_fm(bank(ib, 2)[0:64, h * 128:(h + 1) * 128], 2304 + h * 64, 64, u, "W", W, [ibr[h // 4]])
            jb, jbr = palloc(2)
            for h in range(8):
                proj_fm(bank(jb, 2)[0:16, h * 128:(h + 1) * 128], 544 + h * 16, 16, u, "Wr", Wr, [jbr[h // 4]])
            P.add("act", lambda e: e.activation(out=qiT[0:64, :, :].rearrange("p h t -> p (h t)"), in_=bank(ib, 2)[0:64, :], func=AF.Copy), reads=ibr, writes=["qiT"])
            P.add("dve", lambda e: e.tensor_tensor(out=tA[0:16, :].rearrange("p (h t) -> p h t", h=8), in0=bank(ib, 2)[0:16, :].rearrange("p (h t) -> p h t", h=8),
                                                  in1=bc(C16[0:16, :], 8), op=ALU.mult), reads=ibr + ["Ci"], writes=["tA"])
            P.add("dve", lambda e: e.tensor_tensor(out=tB[0:16, :].rearrange("p (h t) -> p h t", h=8), in0=bank(jb, 2)[0:16, :].rearrange("p (h t) -> p h t", h=8),
                                                  in1=bc(S16[0:16, :], 8), op=ALU.mult), reads=jbr + ["Si"], writes=["tB"])
            P.add("dve", lambda e: e.tensor_tensor(out=qiT[0:16, :, :].rearrange("p h t -> p (h t)"), in0=tA[0:16, :], in1=tB[0:16, :], op=ALU.add),
                  reads=["tA", "tB"], writes=["qiT"])
            cb, cbr = palloc(1)
            proj_fm(bank(cb)[0:64, 0:128], 2816, 64, u, "W", W, cbr)
            db, dbr = palloc(1)
            proj_fm(bank(db)[0:16, 0:128], 672, 16, u, "Wr", Wr, dbr)
            P.add("act", lambda e: e.activation(out=kiT[0:64, r0:r0 + 128], in_=bank(cb)[0:64, 0:128], func=AF.Copy), reads=cbr, writes=["kiT"])
            P.add("dve", lambda e: e.tensor_tensor(out=tA[0:16, 0:128], in0=bank(cb)[0:16, 0:128], in1=C16[0:16, :], op=ALU.mult), reads=cbr + ["Ci"], writes=["tA"])
            P.add("dve", lambda e: e.tensor_tensor(out=tB[0:16, 0:128], in0=bank(db)[0:16, 0:128], in1=S16[0:16, :], op=ALU.mult), reads=dbr + ["Si"], writes=["tB"])
            P.add("dve", lambda e: e.tensor_tensor(out=kiT[0:16, r0:r0 + 128], in0=tA[0:16, 0:128], in1=tB[0:16, 0:128], op=ALU.add), reads=["tA", "tB"], writes=["kiT"])

        def stI(t):
            r0 = t * 128
            St = (t + 1) * 128
            for s0 in range(0, St, 512):
                n = min(512, St - s0)
                for h in range(8):
                    v = h % 2
                    sb_, sbr = palloc(1)
                    P.add("pe", lambda e, sb_=sb_, h=h, s0=s0, n=n: e.matmul(bank(sb_)[:, 0:n], lhsT=qiT[0:64, h, :], rhs=kiT[0:64, s0:s0 + n], start=True, stop=True),
                          reads=["qiT", "kiT"], writes=sbr)
                    P.add("act", lambda e, sb_=sb_, v=v, n=n: e.activation(out=rbuf[v][:, 0:n], in_=bank(sb_)[:, 0:n], func=AF.Relu), reads=sbr, writes=["rbuf%d" % v])
                    if h == 0:
                        P.add("dve", lambda e, v=v, s0=s0, n=n: e.tensor_scalar(out=acc[:, s0:s0 + n], in0=rbuf[v][:, 0:n], scalar1=wi[:, 0:1], scalar2=None, op0=ALU.mult),
                              reads=["rbuf%d" % v, "wi"], writes=["acc"])
                    else:
                        P.add("dve", lambda e, v=v, s0=s0, n=n, h=h: e.scalar_tensor_tensor(out=acc[:, s0:s0 + n], in0=rbuf[v][:, 0:n], scalar=wi[:, h:h + 1], in1=acc[:, s0:s0 + n],
                                                                                      op0=ALU.mult, op1=ALU.add), reads=["rbuf%d" % v, "wi", "acc"], writes=["acc"])
            if t >= 2:
                P.add("dve", lambda e: e.tensor_reduce(out=W0, in_=acc[:, 0:St], axis=mybir.AxisListType.X, op=ALU.max, apply_absolute_value=True),
                      reads=["acc"], writes=["sm"])
            P.add("pool", lambda e: e.tensor_tensor(out=acc[:, r0:r0 + 128], in0=acc[:, r0:r0 + 128], in1=negm, op=ALU.add), reads=["acc", "negm"], writes=["acc"])

        def stT(t):
            St = (t + 1) * 128
            u = t % 2
            if t >= 2:
                P.add("dve", lambda e: e.memset(mid, 0.0), writes=["sm"])
                P.add("dve", lambda e: e.tensor_scalar(out=wcols, in0=pw2, scalar1=W0, scalar2=2.0, op0=ALU.mult, op1=ALU.mult), reads=["sm", "pw2"], writes=["sm"])
                P.add("dve", lambda e: e.tensor_scalar(out=nhw, in0=wcols, scalar1=-0.5, scalar2=None, op0=ALU.mult), reads=["sm"], writes=["sm"])
                for it in range(NIT):
                    P.add("dve", lambda e: e.tensor_scalar(out=maskb[:, 0:St], in0=acc[:, 0:St], scalar1=mid, scalar2=None, op0=ALU.is_ge, op1=ALU.add, accum_out=cnt),
                          reads=["acc", "sm"], writes=["maskb", "sm"])
                    P.add("dve", lambda e, it=it: e.tensor_scalar(out=tmp, in0=cnt, scalar1=256.0, scalar2=wcols[:, it:it + 1], op0=ALU.is_ge, op1=ALU.mult), reads=["sm"], writes=["sm"], nosame=NOSAME)
                    P.add("dve", lambda e, it=it: e.scalar_tensor_tensor(out=mid, in0=tmp, scalar=nhw[:, it:it + 1], in1=mid, op0=ALU.add, op1=ALU.add), reads=["sm"], writes=["sm"], nosame=NOSAME)
                P.add("dve", lambda e: e.tensor_scalar(out=lo, in0=W0, scalar1=-(2.0 ** -NIT), scalar2=mid, op0=ALU.mult, op1=ALU.add), reads=["sm"], writes=["sm"])
                P.add("dve", lambda e: e.tensor_scalar(out=maskb[:, 0:St], in0=acc[:, 0:St], scalar1=lo, scalar2=None, op0=ALU.is_ge), reads=["acc", "sm"], writes=["maskb"])
            else:
                P.add("dve", lambda e: e.tensor_scalar(out=maskb[:, 0:St], in0=acc[:, 0:St], scalar1=-1e29, scalar2=None, op0=ALU.is_ge), reads=["acc"], writes=["maskb"])
            transpose_to(maskT[u], maskb, t + 1, "maskb", "maskT%d" % u, evac="dve")

        ring = [5, 6, 7]

        def stAtt(t):
            u = t % 2
            qTu = qT[u]

            def logits(kb):
                pv = kb % 2
                for hg in range(4):
                    lb = ring[(kb * 4 + hg) % 3]
                    P.add("pe", lambda e, lb=lb, hg=hg: e.matmul(bank(lb), lhsT=kT[:, kb * 128:(kb + 1) * 128],
                                                               rhs=qTu[:, hg * 4:(hg + 1) * 4, :].rearrange("p h t -> p (h t)"), start=True, stop=True),
                          reads=["kT", "qT%d" % u], writes=["pb%d" % lb])
                    P.add("act", lambda e, lb=lb, hg=hg: e.activation(out=probs[pv][:, hg * 512:(hg + 1) * 512], in_=bank(lb), func=AF.Exp, scale=float(ATTN_SCALE)),
                          reads=["pb%d" % lb], writes=["probs%d_%d" % (pv, hg)])
                    P.add("pool", lambda e, hg=hg: e.tensor_tensor(out=probs[pv][:, hg * 512:(hg + 1) * 512].rearrange("p (h t) -> p h t", h=4),
                                                                 in0=probs[pv][:, hg * 512:(hg + 1) * 512].rearrange("p (h t) -> p h t", h=4),
                                                                 in1=bc(maskT[u][:, kb, :], 4), op=ALU.mult),
                          reads=["probs%d_%d" % (pv, hg), "maskT%d" % u], writes=["probs%d_%d" % (pv, hg)])

            def pvmm(kb):
                pv = kb % 2
                for hg in range(4):
                    P.add("pe", lambda e, hg=hg: e.matmul(bank(hg), lhsT=vtok[:, kb, :], rhs=probs[pv][:, hg * 512:(hg + 1) * 512],
                                                        start=(kb == 0), stop=(kb == t)), reads=["vtok", "probs%d_%d" % (pv, hg)], writes=["pb%d" % hg])
                    P.add("pe", lambda e, hg=hg: e.matmul(bank(4)[32 * hg:32 * (hg + 1), :], lhsT=ones_b[:, 0:32], rhs=probs[pv][:, hg * 512:(hg + 1) * 512],
                                                        start=(kb == 0), stop=(kb == t), tile_position=(0, 32 * hg)), reads=["ones_b", "probs%d_%d" % (pv, hg)], writes=["pb4"])

            logits(0)
            for kb in range(t + 1):
                if kb + 1 <= t:
                    logits(kb + 1)
                pvmm(kb)

        def stFe(t):
            P.add("act", lambda e: e.activation(out=den_sb, in_=bank(4), func=AF.Copy), reads=["pb4"], writes=["den_sb"])
            for hp in range(2):
                P.add("act", lambda e, hp=hp: e.activation(out=oS[:, hp * 1024:(hp + 1) * 1024], in_=bank(2 * hp, 2), func=AF.Copy),
                      reads=["pb%d" % (2 * hp), "pb%d" % (2 * hp + 1)], writes=["oS%d" % hp])

        def stFm(t):
            r0 = t * 128
            P.add("dve", lambda e: e.reciprocal(out=den_sb, in_=den_sb), reads=["den_sb"], writes=["den_sb"])
            for hg in range(4):
                lb, lbr = palloc(1)
                P.add("pe", lambda e, lb=lb, hg=hg: e.matmul(bank(lb), lhsT=sel[:, hg, :], rhs=den_sb, start=True, stop=True), reads=["sel", "den_sb"], writes=lbr)
                P.add("dve", lambda e, lb=lb, hg=hg: e.tensor_tensor(out=ogTt[:, hg * 4:(hg + 1) * 4, :].rearrange("p c t -> p (c t)"), in0=bank(lb), in1=oS[:, hg * 512:(hg + 1) * 512], op=ALU.mult),
                      reads=lbr + ["oS%d" % (hg // 2)], writes=["ogTt"])
            P.add("poolq", lambda e: e.dma_start(out=ogTs[:, r0:r0 + 128].rearrange("(c f) t -> f c t", f=128), in_=ogTt),
                  reads=["ogTt"], writes=["ogTs"], chan="ogTst")

        if os.environ.get("DSA_NOPIPE"):
            for t in range(NT):
                stP(t)
                stI(t)
                stT(t)
                stAtt(t)
                stFe(t)
                stFm(t)
            return
        VAR = os.environ.get("DSA_VAR", "b")
        stP(0)
        stI(0)
        stT(0)
        for t in range(NT):
            if t + 1 < NT:
                stP(t + 1)
                stI(t + 1)
                if VAR == "a":
                    stT(t + 1)
            stAtt(t)
            stFe(t)
            if t + 1 < NT and VAR != "a":
                stT(t + 1)
            stFm(t)

    cur = x_in
    for li in layers:
        kind = li % 3
        j = li // 3
        dst = out if li == layers[-1] else xs
        if kind == 0:
            gla_a(j, cur)
            phase_b(li, cur, dst, gla_w_out[j], False)
        elif kind == 2:
            conv_a(cur)
            phase_b(li, cur, dst, conv_w_out[0], True)
        else:
            dsa_a(cur)
            phase_b(li, cur, dst, dsa_w_out[0], True, zgate_w=dsa_w_in[0])
        cur = dst
    for u in range(2):
        P.final_waits.append("xo%d" % u)
    P.emit(st)
    st.close()
    return nc


def _consts():
    j = np.arange(128)[:, None]
    i = np.arange(128)[None, :]
    return {
        "c_ident": np.eye(128, dtype=np.float32),
        "c_umat": np.where(j <= i, -1.0 / 16.0, 0.0).astype(np.float32),
        "c_lmat": np.where(j > i, -1.0 / 16.0, 0.0).astype(np.float32),
        "c_tri": np.where(j <= i, 1.0, 0.0).astype(np.float32),
        "c_rope": _rope_consts(),
        "c_neg": np.where(i <= j, 0.0, -1e30).astype(np.float32),
        "c_pw2": np.tile((2.0 ** -(np.arange(18) + 1.0))[None, :], (128, 1)).astype(np.float32),
        "c_sel": _sel_consts(),
    }


def _rope_consts():
    c = np.zeros((32, 8), np.float32)
    pp = np.arange(32)
    c[:, 0] = (np.float32(ROPE_THETA) ** (-(np.arange(0, 32, 2, dtype=np.float32)) / np.float32(32)))[pp % 16] / (2 * math.pi)
    c[:, 1] = np.where(pp < 16, -1.0, 1.0) * 2 * math.pi
    c[:, 3] = (np.float32(ROPE_THETA) ** (-(np.arange(0, 16, 2, dtype=np.float32)) / np.float32(16)))[pp % 8] / (2 * math.pi)
    c[:, 4] = np.where(pp % 16 < 8, -1.0, 1.0) * 2 * math.pi
    return c


def _sel_consts():
    s = np.zeros((128, 4, 128), np.float32)
    for g in range(4):
        s[32 * g, g, :] = 1.0
    return s.reshape(128, 512)


def _rot_perm():
    idx = []
    for h in range(16):
        idx += list(range(h * 128 + 16, h * 128 + 32)) + list(range(h * 128, h * 128 + 16))
    idx += list(range(2048 + 16, 2048 + 32)) + list(range(2048, 2048 + 16))
    for h in range(8):
        b = 4352 + h * 64
        idx += list(range(b + 8, b + 16)) + list(range(b, b + 8))
    idx += list(range(4864 + 8, 4864 + 16)) + list(range(4864, 4864 + 8))
    return np.asarray(idx)


WEIGHT_KEYS = ["gla_w_in", "gla_w_a2", "gla_b_a", "gla_gn_g", "gla_w_out", "dsa_w_in", "dsa_w_out",
               "conv_w_in", "conv_dw_w", "conv_dw_b", "conv_ln_g", "conv_ln_b", "conv_w_out",
               "ln_g", "ln_b", "ple_w", "ple_gate_w"]


def run(inputs, NT=32, layers=(0, 1, 2, 3), cores=None):
    B = inputs["x"].shape[0]
    cores = list(range(B)) if cores is None else cores
    L = NT * 128
    nc = build_program(NT, list(layers))
    consts = _consts()
    in_maps = []
    for b in cores:
        m = {"x": np.ascontiguousarray(inputs["x"][b, :L]),
             "p": np.ascontiguousarray(inputs["p"][:, b, :L]),
             "pos": np.ascontiguousarray(inputs["positions"][b, :L]).astype(np.int32)}
        for k in WEIGHT_KEYS:
            m[k] = np.ascontiguousarray(inputs[k])
        m["dsa_w_rot"] = np.ascontiguousarray(inputs["dsa_w_in"][0][:, _rot_perm()])
        m.update(consts)
        in_maps.append(m)
    res = run_bass_kernel_spmd(nc, in_maps, core_ids=list(range(len(cores))))
    return np.stack([r["out"] for r in res.results], axis=0)


def kernel(**inputs):
    inputs = {k: np.asarray(v) for k, v in inputs.items()}
    return run(inputs).astype(np.float32)
```

```python
import math
import os
from contextlib import ExitStack
import numpy as np
import concourse.bass as bass
import concourse.mybir as mybir
from concourse.bass_utils import run_bass_kernel_spmd

F32 = mybir.dt.float32
BF16 = mybir.dt.bfloat16
I32 = mybir.dt.int32
ALU = mybir.AluOpType
AF = mybir.ActivationFunctionType

D = 1024
SEQ = 4096
DEPTH = 4
DI = 2048
GLA_IN = 6160
DSA_IN = 4936
CONV_IN = 6144
ALPHA = (2 * DEPTH) ** 0.25
EPS = 1e-5
ROPE_THETA = 500000.0
ATTN_SCALE = 128 ** -0.5
CONV_W = 31


NOSYNC_ENGS = set(os.environ.get("NOSYNC_ENGS", "").split(","))


class Op:
    __slots__ = ("eng", "fn", "deps", "idx", "chan", "ticket", "needs_inc", "nosame")


class Prog:
    PHYS = {"pe": "tensor", "act": "scalar", "dve": "vector", "pool": "gpsimd",
            "sp": "sync", "actq": "scalar", "poolq": "gpsimd"}

    def __init__(self, nc, same_engine_raw=True):
        self.nc = nc
        self.ops = []
        self.lastw = {}
        self.readers = {}
        self.same_engine_raw = same_engine_raw
        self.last_on = {}
        self.dma_since = []
        self.barrier_deps = None
        self.barrier_seen = set()
        self.final_waits = []
        self.groups = {}

    def add(self, eng, fn, reads=(), writes=(), chan=None, group=None, nosame=False):
        op = Op()
        op.nosame = nosame
        op.eng = eng
        op.fn = fn
        op.idx = len(self.ops)
        op.chan = chan
        op.ticket = None
        op.needs_inc = False
        writes = list(writes) + [r + "_rd" for r in reads if r.startswith("pb")]
        deps = {}
        for r in reads:
            for w in self.lastw.get(r, ()):
                deps[w] = "raw"
        for w_ in writes:
            lw = self.lastw.get(w_, ())
            same_group = group is not None and lw and all(self.groups.get(w) == group for w in lw)
            if not same_group:
                for w in lw:
                    if w not in deps:
                        deps[w] = "waw"
            for rd in self.readers.get(w_, ()):
                if rd not in deps:
                    deps[rd] = "war"
        for r in reads:
            self.readers.setdefault(r, []).append(op.idx)
        for w_ in writes:
            lw = self.lastw.get(w_, ())
            if group is not None and lw and all(self.groups.get(w) == group for w in lw):
                self.lastw[w_] = list(lw) + [op.idx]
            else:
                self.lastw[w_] = [op.idx]
                self.readers[w_] = []
        if group is not None:
            self.groups[op.idx] = group
        ph = self.PHYS[eng]
        if self.barrier_deps is not None and ph not in self.barrier_seen:
            self.barrier_seen.add(ph)
            for d in self.barrier_deps:
                if d not in deps:
                    deps[d] = "raw"
        op.deps = deps
        self.ops.append(op)
        self.last_on[ph] = op.idx
        if chan is not None:
            self.dma_since.append(op.idx)
        return op

    def barrier(self):
        deps = set(self.last_on.values()) | set(self.dma_since)
        self.barrier_deps = deps
        self.barrier_seen = set()
        self.dma_since = []
        self.lastw = {}
        self.readers = {}

    def emit(self, stack):
        nc = self.nc
        ops = self.ops
        phys = [self.PHYS[o.eng] for o in ops]
        is_dma = [o.chan is not None for o in ops]
        need = [[] for _ in ops]
        for o in ops:
            for d, kind in o.deps.items():
                if is_dma[d]:
                    need[o.idx].append(d)
                elif phys[d] == phys[o.idx]:
                    if is_dma[o.idx]:
                        need[o.idx].append(d)
                    elif kind == "raw" and self.same_engine_raw and o.eng != "pe" and o.eng not in NOSYNC_ENGS and not o.nosame:
                        need[o.idx].append(d)
                else:
                    need[o.idx].append(d)
        for o in ops:
            for d in need[o.idx]:
                ops[d].needs_inc = True
            if is_dma[o.idx]:
                o.needs_inc = True
        sems = {}
        counts = {}
        for o in ops:
            if not o.needs_inc:
                continue
            key = ("d_" + o.chan) if is_dma[o.idx] else ("e_" + phys[o.idx])
            if key not in sems:
                sems[key] = stack.enter_context(nc.semaphore("s_" + key))
                counts[key] = 0
            counts[key] += 16 if is_dma[o.idx] else 1
            o.ticket = (key, counts[key])
        self.n_sems = len(sems)
        block = stack.enter_context(nc.Block())
        streams = {}
        for o in ops:
            streams.setdefault(phys[o.idx], []).append(o)
        final = self.final_waits

        def run_stream(e, lst):
            waited = {}
            for o in lst:
                req = {}
                for d in need[o.idx]:
                    key, val = ops[d].ticket
                    if req.get(key, 0) < val:
                        req[key] = val
                for key, val in req.items():
                    if waited.get(key, 0) >= val:
                        continue
                    e.wait_ge(sems[key], val)
                    waited[key] = val
                ins = o.fn(e)
                if o.needs_inc:
                    key, val = o.ticket
                    ins.then_inc(sems[key], 16 if is_dma[o.idx] else 1)

        def mk(name):
            def f(e):
                run_stream(e, streams.get(name, []))
                if name == "gpsimd":
                    for ch in final:
                        key = "d_" + ch
                        if key in sems:
                            e.wait_ge(sems[key], counts[key])
            return f

        block.sync(mk("sync"))
        block.scalar(mk("scalar"))
        block.vector(mk("vector"))
        block.gpsimd(mk("gpsimd"))
        block.tensor(mk("tensor"))


class Arena:
    def __init__(self, ap, nwords):
        self.ap = ap
        self.n = nwords
        self.off = 0

    def alloc(self, nelem, dt):
        words = nelem if dt in (F32, I32) else (nelem + 1) // 2
        words = (words + 15) // 16 * 16
        assert self.off + words <= self.n, ("SBUF arena overflow", self.off, words, self.n)
        a = self.ap[:, self.off:self.off + words]
        self.off += words
        if dt != F32:
            a = a.bitcast(dt)
        return a[:, 0:nelem]

    def mark(self):
        return self.off

    def release(self, m):
        self.off = m


def build_program(NT, layers, dbg=False):
    L = NT * 128
    nc = bass.Bass("TRN2", target_bir_lowering=False)

    def din(name, shape, dt=F32):
        return nc.dram_tensor(name, list(shape), dt, kind="ExternalInput").ap()

    x_in = din("x", [L, D])
    p_in = din("p", [DEPTH, L, 256])
    pos_in = din("pos", [L], I32)
    gla_w_in = din("gla_w_in", [2, D, GLA_IN])
    gla_w_a2 = din("gla_w_a2", [2, 16, D])
    gla_b_a = din("gla_b_a", [2, D])
    gla_gn_g = din("gla_gn_g", [2, DI])
    gla_w_out = din("gla_w_out", [2, DI, D])
    dsa_w_in = din("dsa_w_in", [1, D, DSA_IN])
    dsa_w_out = din("dsa_w_out", [1, DI, D])
    conv_w_in = din("conv_w_in", [1, D, CONV_IN])
    conv_dw_w = din("conv_dw_w", [1, CONV_W, DI])
    conv_dw_b = din("conv_dw_b", [1, DI])
    conv_ln_g = din("conv_ln_g", [1, DI])
    conv_ln_b = din("conv_ln_b", [1, DI])
    conv_w_out = din("conv_w_out", [1, DI, D])
    ln_g = din("ln_g", [DEPTH, D])
    ln_b = din("ln_b", [DEPTH, D])
    ple_w = din("ple_w", [DEPTH, 256, D])
    ple_gate_w = din("ple_gate_w", [DEPTH, D, D])
    c_ident = din("c_ident", [128, 128])
    c_umat = din("c_umat", [128, 128])
    c_lmat = din("c_lmat", [128, 128])
    c_tri = din("c_tri", [128, 128])
    c_rope = din("c_rope", [32, 8])
    c_neg = din("c_neg", [128, 128])
    c_pw2 = din("c_pw2", [128, 18])
    c_sel = din("c_sel", [128, 512])
    dsa_w_rot = din("dsa_w_rot", [D, 688])
    out = nc.dram_tensor("out", [L, D], F32, kind="ExternalOutput").ap()
    xs = nc.dram_tensor("xs", [L, D], F32).ap()
    ogs = nc.dram_tensor("ogs", [L, DI], BF16).ap()
    ogTs = nc.dram_tensor("ogTs", [DI, L], BF16).ap()

    st = ExitStack()
    ARENA_WORDS = 52000
    arena_t = st.enter_context(nc.sbuf_tensor("arena", [128, ARENA_WORDS], F32))
    A = Arena(arena_t[:], ARENA_WORDS)
    psum_t = st.enter_context(nc.psum_tensor("ps", [128, 4096], F32))
    psum = psum_t[:]
    P = Prog(nc)

    def bank(i, n=1):
        return psum[:, i * 512:(i + n) * 512]

    pstate = {"ptr": 0}
    gctr = [0]

    def palloc(n):
        if pstate["ptr"] + n > 8:
            pstate["ptr"] = 0
        b = pstate["ptr"]
        pstate["ptr"] += n
        return b, ["pb%d" % i for i in range(b, b + n)]

    ident_f = A.alloc(128, F32)
    ident_b = A.alloc(128, BF16)
    umat = A.alloc(128, F32)
    lmat = A.alloc(128, F32)
    tri = A.alloc(128, F32)
    P.add("sp", lambda e: e.dma_start(out=ident_f, in_=c_ident), writes=["ident_f"], chan="c0")
    P.add("sp", lambda e: e.dma_start(out=umat, in_=c_umat), writes=["umat"], chan="c1")
    P.add("sp", lambda e: e.dma_start(out=lmat, in_=c_lmat), writes=["lmat"], chan="c2")
    P.add("sp", lambda e: e.dma_start(out=tri, in_=c_tri), writes=["tri"], chan="c3")
    P.add("dve", lambda e: e.tensor_copy(out=ident_b, in_=ident_f), reads=["ident_f"], writes=["ident_b"])
    persist_mark = A.mark()

    def load_w(dst, src_rows, ncols, kchunks, name, c0=0):
        step = 2048
        gctr[0] += 1
        grp = "g%d" % gctr[0]
        for k in range(kchunks):
            for cs in range(0, ncols, step):
                ce = min(ncols, cs + step)
                P.add("poolq",
                      lambda e, k=k, cs=cs, ce=ce: e.dma_start(
                          out=dst[:, k, cs:ce], in_=src_rows[k * 128:(k + 1) * 128, c0 + cs:c0 + ce]),
                      writes=[name], chan="w_" + name, group=grp)

    tb_state = {"i": 0}

    def transpose_to(dst, src_bf, nblk, rname, wname, evac="dve", banks=None):
        done = 0
        while done < nblk:
            n = min(8, nblk - done)
            if banks is None:
                b, br = palloc(1)
            else:
                b = banks[tb_state["i"] % len(banks)]
                tb_state["i"] += 1
                br = ["pb%d" % b]
            pv = bank(b).bitcast(BF16)
            for k in range(n):
                P.add("pe", lambda e, k=k, done=done, pv=pv: e.transpose(
                    out=pv[:, k * 128:(k + 1) * 128], in_=src_bf[:, (done + k) * 128:(done + k + 1) * 128],
                    identity=ident_b), reads=[rname, "ident_b"], writes=br)
            P.add(evac, lambda e, n=n, done=done, pv=pv: e.tensor_copy(
                out=dst[:, done:done + n, :], in_=pv[:, 0:n * 128].rearrange("p (k t) -> p k t", k=n))
                if evac != "act" else e.activation(
                out=dst[:, done:done + n, :], in_=pv[:, 0:n * 128].rearrange("p (k t) -> p k t", k=n), func=AF.Copy),
                reads=br, writes=[wname])
            done += n

    def phase_b(li, x_src, x_dst, w_out_d, og_feature_major, zgate_w=None):
        P.barrier()
        A.release(persist_mark)
        wout = A.alloc(16 * 1024, BF16).rearrange("p (k n) -> p k n", k=16)
        wgate = A.alloc(8 * 1024, BF16).rearrange("p (k n) -> p k n", k=8)
        wple = A.alloc(2 * 1024, BF16).rearrange("p (k n) -> p k n", k=2)
        lng = A.alloc(1024, F32)
        lnb = A.alloc(1024, F32)
        load_w(wout, w_out_d, 1024, 16, "wout")
        load_w(wgate, ple_gate_w[li], 1024, 8, "wgate")
        load_w(wple, ple_w[li], 1024, 2, "wple")
        P.add("sp", lambda e: e.dma_start(out=lng, in_=ln_g[li].partition_broadcast(128)), writes=["lng"], chan="lng")
        P.add("sp", lambda e: e.dma_start(out=lnb, in_=ln_b[li].partition_broadcast(128)), writes=["lnb"], chan="lnb")
        NB = 2
        NB0 = 3
        if zgate_w is not None:
            wz = A.alloc(8 * 2048, BF16).rearrange("p (k n) -> p k n", k=8)
            load_w(wz, zgate_w, 2048, 8, "wz", 2304)
            xzb = [A.alloc(1024, BF16) for _ in range(NB0)]
            xzT = [A.alloc(1024, BF16).rearrange("p (k t) -> p k t", k=8) for _ in range(NB0)]
            szT = [A.alloc(2048, BF16).rearrange("p (c t) -> p c t", c=16) for _ in range(NB0)]
        ogt = [A.alloc(2048, BF16) for _ in range(NB0)]
        ogT = [A.alloc(2048, BF16).rearrange("p (k t) -> p k t", k=16) for _ in range(NB0)]
        xt = [A.alloc(1024, F32) for _ in range(NB0)]
        pt = [A.alloc(256, F32) for _ in range(NB)]
        s_ = [A.alloc(1024, F32) for _ in range(NB)]
        xn = [A.alloc(1024, F32) for _ in range(NB)]
        xnb = [A.alloc(1024, BF16) for _ in range(NB)]
        xnT = [A.alloc(1024, BF16).rearrange("p (k t) -> p k t", k=8) for _ in range(NB)]
        ptb = [A.alloc(256, BF16) for _ in range(NB)]
        pT = [A.alloc(256, BF16).rearrange("p (k t) -> p k t", k=2) for _ in range(NB)]
        sig = [A.alloc(1024, F32) for _ in range(NB)]
        xo = [A.alloc(1024, F32) for _ in range(NB)]
        stats = [A.alloc(16, F32) for _ in range(NB)]
        ctx = {}

        def stage0(t):
            w = t % NB0
            wfx = "w%d" % w
            r0 = t * 128
            if og_feature_major:
                P.add("sp", lambda e, w=w, r0=r0: e.dma_start(
                    out=ogT[w], in_=ogTs[:, r0:r0 + 128].rearrange("(k f) t -> f k t", f=128)),
                    reads=["ogTs"], writes=["ogT" + wfx], chan="ogT" + wfx)
            else:
                P.add("sp", lambda e, w=w, r0=r0: e.dma_start(out=ogt[w], in_=ogs[r0:r0 + 128, :]),
                      reads=["ogs"], writes=["ogt" + wfx], chan="ogt" + wfx)
                transpose_to(ogT[w], ogt[w], 16, "ogt" + wfx, "ogT" + wfx, evac="act", banks=[6, 7])
            P.add("sp", lambda e, w=w, r0=r0: e.dma_start(out=xt[w], in_=x_src[r0:r0 + 128, :]),
                  reads=["xsrc%d" % t], writes=["xt" + wfx], chan="xt" + wfx)
            if zgate_w is not None:
                P.add("act", lambda e, w=w: e.activation(out=xzb[w], in_=xt[w], func=AF.Copy), reads=["xt" + wfx], writes=["xzb" + wfx])
                transpose_to(xzT[w], xzb[w], 8, "xzb" + wfx, "xzT" + wfx, evac="dve", banks=[6, 7])
                for g8 in range(2):
                    zb = 2 + 2 * g8
                    zbr = ["pb%d" % zb, "pb%d" % (zb + 1)]
                    for hh in range(8):
                        c = g8 * 8 + hh
                        for k in range(8):
                            P.add("pe", lambda e, w=w, zb=zb, hh=hh, c=c, k=k: e.matmul(bank(zb, 2)[:, hh * 128:(hh + 1) * 128], lhsT=wz[:, k, c * 128:(c + 1) * 128],
                                                                                   rhs=xzT[w][:, k, :], start=(k == 0), stop=(k == 7)),
                                  reads=["xzT" + wfx, "wz"], writes=[zbr[hh // 4]])
                    P.add("act", lambda e, w=w, g8=g8, zb=zb: e.activation(out=szT[w][:, g8 * 8:(g8 + 1) * 8, :].rearrange("p c t -> p (c t)"), in_=bank(zb, 2), func=AF.Silu),
                          reads=zbr, writes=["szT" + wfx])
                P.add("pool", lambda e, w=w: e.tensor_tensor(out=ogT[w].rearrange("p k t -> p (k t)"), in0=ogT[w].rearrange("p k t -> p (k t)"),
                                                             in1=szT[w].rearrange("p c t -> p (c t)"), op=ALU.mult),
                      reads=["ogT" + wfx, "szT" + wfx], writes=["ogT" + wfx])

        def stage1a(t):
            u = t % NB
            sfx = "%d" % u
            r0 = t * 128
            w = t % NB0
            wfx = "w%d" % w
            P.add("sp", lambda e, u=u, r0=r0: e.dma_start(out=pt[u], in_=p_in[li, r0:r0 + 128, :]),
                  writes=["pt" + sfx], chan="pt" + sfx)
            yb, ybr = 0, ["pb0", "pb1"]
            for n in range(2):
                for k in range(16):
                    P.add("pe", lambda e, u=u, n=n, k=k, yb=yb: e.matmul(
                        bank(yb + n), lhsT=ogT[t % NB0][:, k, :], rhs=wout[:, k, n * 512:(n + 1) * 512],
                        start=(k == 0), stop=(k == 15)), reads=["ogT" + wfx, "wout"], writes=[ybr[n]])
            ctx[('y', t)] = (yb, ybr)

        def stage1b(t):
            u = t % NB
            sfx = "%d" % u
            r0 = t * 128
            yb, ybr = ctx[('y', t)]
            w = t % NB0
            wfx = "w%d" % w
            P.add("dve", lambda e, u=u, yb=yb, w=w: e.scalar_tensor_tensor(
                out=s_[u], in0=xt[w], scalar=float(ALPHA), in1=bank(yb, 2), op0=ALU.mult, op1=ALU.add),
                reads=["xt" + wfx] + ybr, writes=["s" + sfx])
            for c in range(2):
                P.add("dve", lambda e, u=u, c=c: e.bn_stats(out=stats[u][:, c * 6:(c + 1) * 6], in_=s_[u][:, c * 512:(c + 1) * 512]),
                      reads=["s" + sfx], writes=["st" + sfx])
            P.add("dve", lambda e, u=u: e.bn_aggr(out=stats[u][:, 12:14], in_=stats[u][:, 0:12]),
                  reads=["st" + sfx], writes=["st" + sfx])
            P.add("dve", lambda e, u=u: e.tensor_scalar(out=stats[u][:, 14:15], in0=stats[u][:, 13:14], scalar1=float(EPS), scalar2=None, op0=ALU.add),
                  reads=["st" + sfx], writes=["st" + sfx])
            P.add("act", lambda e, u=u: e.activation(out=stats[u][:, 14:15], in_=stats[u][:, 14:15], func=AF.Sqrt),
                  reads=["st" + sfx], writes=["st" + sfx])
            P.add("dve", lambda e, u=u: e.reciprocal(out=stats[u][:, 15:16], in_=stats[u][:, 14:15]),
                  reads=["st" + sfx], writes=["st" + sfx])
            P.add("dve", lambda e, u=u: e.tensor_scalar(out=xn[u], in0=s_[u], scalar1=stats[u][:, 12:13], scalar2=stats[u][:, 15:16],
                                                        op0=ALU.subtract, op1=ALU.mult),
                  reads=["s" + sfx, "st" + sfx], writes=["xn" + sfx])
            P.add("dve", lambda e, u=u: e.tensor_tensor(out=xn[u], in0=xn[u], in1=lng, op=ALU.mult),
                  reads=["xn" + sfx, "lng"], writes=["xn" + sfx])
            P.add("dve", lambda e, u=u: e.tensor_tensor(out=xn[u], in0=xn[u], in1=lnb, op=ALU.add),
                  reads=["xn" + sfx, "lnb"], writes=["xn" + sfx])
            P.add("act", lambda e, u=u: e.activation(out=xnb[u], in_=xn[u], func=AF.Copy),
                  reads=["xn" + sfx], writes=["xnb" + sfx])

        def stage2a(t):
            u = t % NB
            sfx = "%d" % u
            r0 = t * 128
            transpose_to(xnT[u], xnb[u], 8, "xnb" + sfx, "xnT" + sfx, evac="dve", banks=[6, 7])
            gb, gbr = 2, ["pb2", "pb3"]
            for n in range(2):
                for k in range(8):
                    P.add("pe", lambda e, u=u, n=n, k=k, gb=gb: e.matmul(
                        bank(gb + n), lhsT=xnT[u][:, k, :], rhs=wgate[:, k, n * 512:(n + 1) * 512],
                        start=(k == 0), stop=(k == 7)), reads=["xnT" + sfx, "wgate"], writes=[gbr[n]])
            P.add("act", lambda e, u=u, gb=gb: e.activation(out=sig[u], in_=bank(gb, 2), func=AF.Sigmoid),
                  reads=gbr, writes=["sig" + sfx])
            P.add("act", lambda e, u=u: e.activation(out=ptb[u], in_=pt[u], func=AF.Copy),
                  reads=["pt" + sfx], writes=["ptb" + sfx])
            transpose_to(pT[u], ptb[u], 2, "ptb" + sfx, "pT" + sfx, evac="dve", banks=[6, 7])
            lb, lbr = 4, ["pb4", "pb5"]
            for n in range(2):
                for k in range(2):
                    P.add("pe", lambda e, u=u, n=n, k=k, lb=lb: e.matmul(
                        bank(lb + n), lhsT=pT[u][:, k, :], rhs=wple[:, k, n * 512:(n + 1) * 512],
                        start=(k == 0), stop=(k == 1)), reads=["pT" + sfx, "wple"], writes=[lbr[n]])
            ctx[('l', t)] = (lb, lbr)

        def stage2b(t):
            u = t % NB
            sfx = "%d" % u
            r0 = t * 128
            lb, lbr = ctx[('l', t)]
            P.add("dve", lambda e, u=u, lb=lb: e.tensor_tensor(out=xo[u], in0=bank(lb, 2), in1=sig[u], op=ALU.mult),
                  reads=lbr + ["sig" + sfx], writes=["xo" + sfx])
            P.add("dve", lambda e, u=u: e.tensor_tensor(out=xo[u], in0=xo[u], in1=xn[u], op=ALU.add),
                  reads=["xo" + sfx, "xn" + sfx], writes=["xo" + sfx])
            P.add("poolq", lambda e, u=u, r0=r0: e.dma_start(out=x_dst[r0:r0 + 128, :], in_=xo[u]),
                  reads=["xo" + sfx], writes=["xdst%d" % t], chan="xo" + sfx)


        stage0(0)
        if NT > 1:
            stage0(1)
        stage1a(0)
        stage1b(0)
        for t in range(NT):
            if t + 2 < NT:
                stage0(t + 2)
            if t + 1 < NT:
                stage1a(t + 1)
            stage2a(t)
            if t + 1 < NT:
                stage1b(t + 1)
            stage2b(t)

    def gla_a(j, x_src):
        P.barrier()
        A.release(persist_mark)
        w_in = gla_w_in[j]
        wqk = A.alloc(8 * 2048, BF16).rearrange("p (k n) -> p k n", k=8)
        wvz = A.alloc(8 * 4096, BF16).rearrange("p (k n) -> p k n", k=8)
        wa = A.alloc(8 * 16, BF16).rearrange("p (k n) -> p k n", k=8)
        wa2 = A.alloc(1024, BF16)
        gng = A.alloc(2048, F32)
        S = A.alloc(4 * 2 * 512, F32).rearrange("p (h c n) -> p h c n", h=4, c=2)
        Sb = A.alloc(4 * 2 * 512, BF16).rearrange("p (h c n) -> p h c n", h=4, c=2)
        load_w(wqk, w_in, 2048, 8, "wqk", 0)
        load_w(wvz, w_in, 4096, 8, "wvz", 2048)
        load_w(wa, w_in, 16, 8, "wa", 6144)
        P.add("poolq", lambda e: e.dma_start(out=wa2[0:16, :], in_=gla_w_a2[j]), writes=["wa2"], chan="wa2")
        P.add("poolq", lambda e: e.dma_start(out=wa2[16:17, :], in_=gla_b_a[j].rearrange("(o n) -> o n", o=1)), writes=["wa2"], chan="wa2b")
        P.add("sp", lambda e: e.dma_start(out=gng, in_=gla_gn_g[j].partition_broadcast(128)), writes=["gng"], chan="gng")
        P.add("dve", lambda e: e.memset(S.rearrange("p h c n -> p (h c n)"), 0.0), writes=["S%d%d" % (h, ci) for h in range(4) for ci in range(2)])
        P.add("pool", lambda e: e.memset(Sb.rearrange("p h c n -> p (h c n)"), 0.0), writes=["Sb%d" % h for h in range(4)])
        NB = 2
        xt = [A.alloc(1024, F32) for _ in range(NB)]
        xb = [A.alloc(1024, BF16) for _ in range(NB)]
        xT = [A.alloc(1024, BF16).rearrange("p (k t) -> p k t", k=8) for _ in range(NB)]
        aT = A.alloc(128, BF16)
        sp_ = A.alloc(1024, F32)
        eG = A.alloc(1024, F32).rearrange("p (c t) -> p c t", c=8)
        enG = A.alloc(1024, F32).rearrange("p (c t) -> p c t", c=8)
        kdec = A.alloc(1024, F32)
        qt = A.alloc(1024, BF16).rearrange("p (c t) -> p c t", c=8)
        kt = A.alloc(1024, BF16).rearrange("p (c t) -> p c t", c=8)
        ks = A.alloc(1024, BF16)
        vb = A.alloc(2048, BF16)
        zg = A.alloc(2048, F32)
        ATb4 = A.alloc(512, BF16).rearrange("p (h t) -> p h t", h=4)
        sq = A.alloc(512, F32)
        og = [A.alloc(2048, BF16) for _ in range(NB)]
        nrm = A.alloc(8, F32)
        P.add("dve", lambda e: e.memset(aT[0:32, :], 1.0), writes=["aT"])
        for t in range(NT):
            u = t % NB
            sfx = "%d" % u
            r0 = t * 128
            P.add("sp", lambda e, u=u, r0=r0: e.dma_start(out=xt[u], in_=x_src[r0:r0 + 128, :]),
                  reads=["xsrc%d" % t], writes=["xt" + sfx], chan="xt" + sfx)
            P.add("act", lambda e, u=u: e.activation(out=xb[u], in_=xt[u], func=AF.Copy),
                  reads=["xt" + sfx], writes=["xb" + sfx])
            transpose_to(xT[u], xb[u], 8, "xb" + sfx, "xT" + sfx, evac="dve")
            ab, abr = palloc(1)
            for k in range(8):
                P.add("pe", lambda e, u=u, k=k, ab=ab: e.matmul(bank(ab)[0:16, 0:128], lhsT=wa[:, k, :], rhs=xT[u][:, k, :],
                                                             start=(k == 0), stop=(k == 7)),
                      reads=["xT" + sfx, "wa"], writes=abr)
            P.add("dve", lambda e, ab=ab: e.tensor_copy(out=aT[0:16, :], in_=bank(ab)[0:16, 0:128]), reads=abr, writes=["aT"])
            lb, lbr = palloc(2)
            for n in range(2):
                P.add("pe", lambda e, n=n, lb=lb: e.matmul(bank(lb + n), lhsT=aT[0:17, :], rhs=wa2[0:17, n * 512:(n + 1) * 512],
                                                         start=True, stop=True), reads=["aT", "wa2"], writes=[lbr[n]])
            P.add("act", lambda e, lb=lb: e.activation(out=sp_, in_=bank(lb, 2), func=AF.Exp, scale=-1.0), reads=lbr, writes=["sp"])
            P.add("act", lambda e: e.activation(out=sp_, in_=sp_, func=AF.Ln, bias=1.0), reads=["sp"], writes=["sp"])
            for half in range(2):
                vb_, vbr = palloc(2)
                for n in range(2):
                    for k in range(8):
                        P.add("pe", lambda e, u=u, n=n, k=k, vb_=vb_, half=half: e.matmul(
                            bank(vb_ + n), lhsT=xT[u][:, k, :], rhs=wvz[:, k, half * 1024 + n * 512:half * 1024 + (n + 1) * 512],
                            start=(k == 0), stop=(k == 7)), reads=["xT" + sfx, "wvz"], writes=[vbr[n]])
                P.add("act", lambda e, vb_=vb_, half=half: e.activation(out=vb[:, half * 1024:(half + 1) * 1024], in_=bank(vb_, 2), func=AF.Copy),
                      reads=vbr, writes=["vb%d" % half])
            for half in range(2):
                zb, zbr = palloc(2)
                for n in range(2):
                    for k in range(8):
                        P.add("pe", lambda e, u=u, n=n, k=k, zb=zb, half=half: e.matmul(
                            bank(zb + n), lhsT=xT[u][:, k, :], rhs=wvz[:, k, 2048 + half * 1024 + n * 512:2048 + half * 1024 + (n + 1) * 512],
                            start=(k == 0), stop=(k == 7)), reads=["xT" + sfx, "wvz"], writes=[zbr[n]])
                P.add("act", lambda e, zb=zb, half=half: e.activation(out=zg[:, half * 1024:(half + 1) * 1024], in_=bank(zb, 2), func=AF.Silu),
                      reads=zbr, writes=["zg%d" % half])
                P.add("pool", lambda e, half=half: e.tensor_tensor(out=zg[:, half * 1024:(half + 1) * 1024], in0=zg[:, half * 1024:(half + 1) * 1024],
                                                                 in1=gng[:, half * 1024:(half + 1) * 1024], op=ALU.mult),
                      reads=["zg%d" % half, "gng"], writes=["zg%d" % half])
            gb, gbr = palloc(2)
            for c in range(8):
                P.add("pe", lambda e, c=c, gb=gb: e.matmul(bank(gb, 2)[:, c * 128:(c + 1) * 128], lhsT=sp_[:, c * 128:(c + 1) * 128], rhs=umat,
                                                         start=True, stop=True), reads=["sp", "umat"], writes=[gbr[c // 4]])
            P.add("act", lambda e, gb=gb: e.activation(out=eG.rearrange("p c t -> p (c t)"), in_=bank(gb, 2), func=AF.Exp), reads=gbr, writes=["eG"])
            P.add("act", lambda e, gb=gb: e.activation(out=enG.rearrange("p c t -> p (c t)"), in_=bank(gb, 2), func=AF.Exp, scale=-1.0), reads=gbr, writes=["enG"])
            kb, kbr = palloc(2)
            for n in range(2):
                P.add("pe", lambda e, n=n, kb=kb: e.matmul(bank(kb + n), lhsT=lmat, rhs=sp_[:, n * 512:(n + 1) * 512], start=True, stop=True),
                      reads=["sp", "lmat"], writes=[kbr[n]])
            P.add("act", lambda e, kb=kb: e.activation(out=kdec, in_=bank(kb, 2), func=AF.Exp), reads=kbr, writes=["kdec"])
            for which, dst, gate, sc in (("q", qt, eG, 1.0 / 16.0), ("k", kt, enG, 1.0)):
                qb, qbr = palloc(2)
                c0 = 0 if which == "q" else 1024
                for c in range(8):
                    for k in range(8):
                        P.add("pe", lambda e, u=u, c=c, k=k, qb=qb, c0=c0: e.matmul(
                            bank(qb, 2)[:, c * 128:(c + 1) * 128], lhsT=wqk[:, k, c0 + c * 128:c0 + (c + 1) * 128], rhs=xT[u][:, k, :],
                            start=(k == 0), stop=(k == 7)), reads=["xT" + sfx, "wqk"], writes=[qbr[c // 4]])
                P.add("dve", lambda e, qb=qb, dst=dst, gate=gate, sc=sc: e.scalar_tensor_tensor(
                    out=dst.rearrange("p c t -> p (c t)"), in0=bank(qb, 2), scalar=float(sc), in1=gate.rearrange("p c t -> p (c t)"),
                    op0=ALU.mult, op1=ALU.mult), reads=qbr + ["eG" if which == "q" else "enG"], writes=[which + "t"])
            tb, tbr = palloc(2)
            for n in range(2):
                for k in range(8):
                    P.add("pe", lambda e, u=u, n=n, k=k, tb=tb: e.matmul(
                        bank(tb + n), lhsT=xT[u][:, k, :], rhs=wqk[:, k, 1024 + n * 512:1024 + (n + 1) * 512],
                        start=(k == 0), stop=(k == 7)), reads=["xT" + sfx, "wqk"], writes=[tbr[n]])
            P.add("dve", lambda e, tb=tb: e.tensor_tensor(out=ks, in0=bank(tb, 2), in1=kdec, op=ALU.mult), reads=tbr + ["kdec"], writes=["ks"])
            P.add("dve", lambda e: e.memset(nrm, 0.0), writes=["nrm"])
            for h in range(4):
                for ci in range(2):
                    c = 2 * h + ci
                    P.add("pe", lambda e, c=c, ci=ci, h=h: e.matmul(bank(h)[:, 0:128], lhsT=kt[:, c, :], rhs=qt[:, c, :],
                                                                 start=(ci == 0), stop=(ci == 1)), reads=["kt", "qt"], writes=["pb%d" % h])
                P.add("dve", lambda e, h=h: e.tensor_tensor(out=ATb4[:, h, :], in0=bank(h)[:, 0:128], in1=tri, op=ALU.mult),
                      reads=["pb%d" % h, "tri"], writes=["ATb%d" % h])
            for h in range(4):
                vh = vb[:, h * 512:(h + 1) * 512]
                vres = "vb%d" % (h // 2)
                ob = 4 + h
                obr = ["pb%d" % ob]
                P.add("pe", lambda e, ob=ob, vh=vh, h=h: e.matmul(bank(ob), lhsT=ATb4[:, h, :], rhs=vh, start=True, stop=False),
                      reads=["ATb%d" % h, vres], writes=obr)
                for ci in range(2):
                    c = 2 * h + ci
                    P.add("pe", lambda e, ob=ob, c=c, ci=ci, h=h: e.matmul(bank(ob), lhsT=qt[:, c, :], rhs=Sb[:, h, ci, :], start=False, stop=(ci == 1)),
                          reads=["qt", "Sb%d" % h], writes=obr)
                for ci in range(2):
                    c = 2 * h + ci
                    sb_ = 2 * (h % 2) + ci
                    sbr = ["pb%d" % sb_]
                    P.add("pe", lambda e, sb_=sb_, c=c, vh=vh: e.matmul(bank(sb_), lhsT=ks[:, c * 128:(c + 1) * 128], rhs=vh, start=True, stop=True),
                          reads=["ks", vres], writes=sbr)
                    P.add("dve", lambda e, sb_=sb_, c=c, ci=ci, h=h: e.scalar_tensor_tensor(
                        out=S[:, h, ci, :], in0=S[:, h, ci, :], scalar=eG[:, c, 127:128], in1=bank(sb_), op0=ALU.mult, op1=ALU.add),
                        reads=["S%d%d" % (h, ci), "eG"] + sbr, writes=["S%d%d" % (h, ci)])
            for h in range(4):
                P.add("act", lambda e, h=h: e.activation(out=sq, in_=bank(4 + h), func=AF.Square, accum_out=nrm[:, h:h + 1]),
                      reads=["pb%d" % (4 + h)], writes=["sq", "nrm"])
            P.add("dve", lambda e: e.tensor_scalar(out=nrm[:, 4:8], in0=nrm[:, 0:4], scalar1=1.0 / 512, scalar2=float(EPS), op0=ALU.mult, op1=ALU.add),
                  reads=["nrm"], writes=["nrm"])
            P.add("act", lambda e: e.activation(out=nrm[:, 4:8], in_=nrm[:, 4:8], func=AF.Sqrt), reads=["nrm"], writes=["nrm"])
            P.add("dve", lambda e: e.reciprocal(out=nrm[:, 4:8], in_=nrm[:, 4:8]), reads=["nrm"], writes=["nrm"])
            for h in range(4):
                P.add("dve", lambda e, h=h, u=u: e.scalar_tensor_tensor(
                    out=og[u][:, h * 512:(h + 1) * 512], in0=bank(4 + h), scalar=nrm[:, 4 + h:5 + h], in1=zg[:, h * 512:(h + 1) * 512],
                    op0=ALU.mult, op1=ALU.mult), reads=["pb%d" % (4 + h), "nrm", "zg%d" % (h // 2)], writes=["og" + sfx])
            for h in range(4):
                for ci in range(2):
                    P.add("pool", lambda e, h=h, ci=ci: e.tensor_copy(out=Sb[:, h, ci, :], in_=S[:, h, ci, :]),
                          reads=["S%d%d" % (h, ci)], writes=["Sb%d" % h])
            pstate["ptr"] = 0
            P.add("poolq", lambda e, u=u, r0=r0: e.dma_start(out=ogs[r0:r0 + 128, :], in_=og[u]),
                  reads=["og" + sfx], writes=["ogs"], chan="ogst" + sfx)

    def conv_a(x_src):
        P.barrier()
        A.release(persist_mark)
        TG = 256
        NG = L // TG
        w_in = conv_w_in[0]
        wc = A.alloc(8 * 6144, BF16).rearrange("p (k n) -> p k n", k=8)
        load_w(wc, w_in, 6144, 8, "wc", 0)
        prm_tm = A.alloc(2048, F32)
        P.add("sp", lambda e: e.dma_start(out=prm_tm[0:31, :], in_=conv_dw_w[0]), writes=["prm_tm"], chan="pr0")
        P.add("sp", lambda e: e.dma_start(out=prm_tm[31:32, :], in_=conv_dw_b[0].rearrange("(o n) -> o n", o=1)), writes=["prm_tm"], chan="pr1")
        P.add("sp", lambda e: e.dma_start(out=prm_tm[32:33, :], in_=conv_ln_g[0].rearrange("(o n) -> o n", o=1)), writes=["prm_tm"], chan="pr2")
        P.add("sp", lambda e: e.dma_start(out=prm_tm[33:34, :], in_=conv_ln_b[0].rearrange("(o n) -> o n", o=1)), writes=["prm_tm"], chan="pr3")
        prm = A.alloc(16 * 34, F32).rearrange("p (c k) -> p c k", c=16)
        for c in range(16):
            b, br = palloc(1)
            P.add("pe", lambda e, c=c, b=b: e.transpose(out=bank(b)[:, 0:34], in_=prm_tm[0:34, c * 128:(c + 1) * 128], identity=ident_f[0:34, 0:34]),
                  reads=["prm_tm", "ident_f"], writes=br)
            P.add("dve", lambda e, c=c, b=b: e.tensor_copy(out=prm[:, c, :], in_=bank(b)[:, 0:34]), reads=br, writes=["prm"])
        ones_f = A.alloc(128, F32)
        P.add("dve", lambda e: e.memset(ones_f, 1.0 / 2048.0), writes=["ones_f"])
        xt = [A.alloc(1024, F32) for _ in range(2)]
        xb = [A.alloc(1024, BF16) for _ in range(2)]
        xTg = A.alloc(8 * TG, BF16).rearrange("p (k t) -> p k t", k=8)
        uT = A.alloc(16 * (TG + 30), BF16).rearrange("p (c t) -> p c t", c=16)
        dg = [A.alloc(128, BF16) for _ in range(9)]
        szT = A.alloc(16 * TG, BF16).rearrange("p (c t) -> p c t", c=16)
        cT = A.alloc(16 * TG, F32).rearrange("p (c t) -> p c t", c=16)
        sgt = [A.alloc(TG, F32) for _ in range(2)]
        sgz = [A.alloc(TG, F32) for _ in range(2)]
        sqt = [A.alloc(TG, F32) for _ in range(2)]
        mean_t = A.alloc(TG, F32)
        rstd_t = A.alloc(TG, F32)
        t1 = [A.alloc(TG, F32) for _ in range(2)]
        t2 = [A.alloc(TG, F32) for _ in range(2)]
        ogTt = A.alloc(16 * TG, BF16).rearrange("p (c t) -> p c t", c=16)
        P.add("dve", lambda e: e.memset(uT.rearrange("p c t -> p (c t)"), 0.0), writes=["uT%d" % c for c in range(16)])
        for g in range(NG):
            for i in range(TG // 128):
                t = g * (TG // 128) + i
                u = t % 2
                sfx = "%d" % u
                r0 = t * 128
                P.add("sp", lambda e, u=u, r0=r0: e.dma_start(out=xt[u], in_=x_src[r0:r0 + 128, :]), writes=["xt" + sfx], chan="xt" + sfx)
                P.add("act", lambda e, u=u: e.activation(out=xb[u], in_=xt[u], func=AF.Copy), reads=["xt" + sfx], writes=["xb" + sfx])
                transpose_to(xTg[:, :, i * 128:(i + 1) * 128], xb[u], 8, "xb" + sfx, "xTg", evac="dve")
            for c in range(16):
                v = c % 2
                pa, par = palloc(1)
                pg, pgr = palloc(1)
                pz, pzr = palloc(1)
                for (pb_, pbr_, off) in ((pa, par, 0), (pg, pgr, 2048), (pz, pzr, 4096)):
                    for k in range(8):
                        P.add("pe", lambda e, pb_=pb_, off=off, c=c, k=k: e.matmul(
                            bank(pb_)[:, 0:TG], lhsT=wc[:, k, off + c * 128:off + (c + 1) * 128], rhs=xTg[:, k, :],
                            start=(k == 0), stop=(k == 7)), reads=["xTg", "wc"], writes=pbr_)
                P.add("act", lambda e, pg=pg, v=v: e.activation(out=sgt[v], in_=bank(pg)[:, 0:TG], func=AF.Sigmoid), reads=pgr, writes=["sgt%d" % v])
                if g > 0:
                    P.add("pool", lambda e, c=c: e.tensor_copy(out=uT[:, c, 0:30], in_=uT[:, c, TG:TG + 30]), reads=["uT%d" % c], writes=["uT%d" % c])
                P.add("dve", lambda e, pa=pa, v=v, c=c: e.tensor_tensor(out=uT[:, c, 30:30 + TG], in0=bank(pa)[:, 0:TG], in1=sgt[v], op=ALU.mult),
                      reads=par + ["sgt%d" % v, "uT%d" % c], writes=["uT%d" % c])
                P.add("act", lambda e, pz=pz, v=v: e.activation(out=sgz[v], in_=bank(pz)[:, 0:TG], func=AF.Sigmoid), reads=pzr, writes=["sgz%d" % v])
                P.add("dve", lambda e, pz=pz, v=v, c=c: e.tensor_tensor(out=szT[:, c, :], in0=bank(pz)[:, 0:TG], in1=sgz[v], op=ALU.mult),
                      reads=pzr + ["sgz%d" % v], writes=["szT%d" % c])
            for c in range(16):
                cb, cbr = palloc(1)
                for k in range(CONV_W):
                    n_ = c * CONV_W + k
                    di = n_ % len(dg)
                    eng = ("dve", "pool")[n_ % 2]
                    if eng == "dve":
                        P.add("dve", lambda e, c=c, k=k, di=di: e.tensor_scalar(out=dg[di], in0=ident_b, scalar1=prm[:, c, k:k + 1], scalar2=None, op0=ALU.mult),
                              reads=["ident_b", "prm"], writes=["dg%d" % di])
                    elif eng == "pool":
                        P.add("pool", lambda e, c=c, k=k, di=di: e.tensor_scalar(out=dg[di], in0=ident_b, scalar1=prm[:, c, k:k + 1], scalar2=1.0, op0=ALU.mult, op1=ALU.mult),
                              reads=["ident_b", "prm"], writes=["dg%d" % di])
                    else:
                        P.add("act", lambda e, c=c, k=k, di=di: e.activation(out=dg[di], in_=ident_b, func=AF.Copy, scale=prm[:, c, k:k + 1]),
                              reads=["ident_b", "prm"], writes=["dg%d" % di])
                    P.add("pe", lambda e, c=c, k=k, di=di, cb=cb: e.matmul(bank(cb)[:, 0:TG], lhsT=dg[di], rhs=uT[:, c, k:k + TG], start=(k == 0), stop=(k == CONV_W - 1)),
                          reads=["dg%d" % di, "uT%d" % c], writes=cbr)
                P.add("act", lambda e, c=c, cb=cb: e.activation(out=cT[:, c, :], in_=bank(cb)[:, 0:TG], func=AF.Identity, bias=prm[:, c, 31:32]),
                      reads=cbr + ["prm"], writes=["cT%d" % c])
            ps, psr = palloc(1)
            pq, pqr = palloc(1)
            for c in range(16):
                v = c % 2
                P.add("pe", lambda e, ps=ps, c=c: e.matmul(bank(ps)[:, 0:TG], lhsT=ones_f, rhs=cT[:, c, :], start=(c == 0), stop=(c == 15)),
                      reads=["ones_f", "cT%d" % c], writes=psr)
                P.add("act", lambda e, c=c, v=v: e.activation(out=sqt[v], in_=cT[:, c, :], func=AF.Square), reads=["cT%d" % c], writes=["sqt%d" % v])
                P.add("pe", lambda e, pq=pq, c=c, v=v: e.matmul(bank(pq)[:, 0:TG], lhsT=ones_f, rhs=sqt[v], start=(c == 0), stop=(c == 15)),
                      reads=["ones_f", "sqt%d" % v], writes=pqr)
            P.add("act", lambda e, ps=ps: e.activation(out=mean_t, in_=bank(ps)[:, 0:TG], func=AF.Copy), reads=psr, writes=["mean_t"])
            P.add("dve", lambda e: e.tensor_tensor(out=rstd_t, in0=mean_t, in1=mean_t, op=ALU.mult), reads=["mean_t"], writes=["rstd_t"])
            P.add("dve", lambda e, pq=pq: e.tensor_tensor(out=rstd_t, in0=bank(pq)[:, 0:TG], in1=rstd_t, op=ALU.subtract), reads=pqr + ["rstd_t"], writes=["rstd_t"])
            P.add("dve", lambda e: e.tensor_scalar(out=rstd_t, in0=rstd_t, scalar1=float(EPS), scalar2=None, op0=ALU.add), reads=["rstd_t"], writes=["rstd_t"])
            P.add("act", lambda e: e.activation(out=rstd_t, in_=rstd_t, func=AF.Sqrt), reads=["rstd_t"], writes=["rstd_t"])
            P.add("dve", lambda e: e.reciprocal(out=rstd_t, in_=rstd_t), reads=["rstd_t"], writes=["rstd_t"])
            for c in range(16):
                v = c % 2
                P.add("dve", lambda e, c=c, v=v: e.tensor_tensor(out=t1[v], in0=cT[:, c, :], in1=mean_t, op=ALU.subtract), reads=["cT%d" % c, "mean_t"], writes=["t1%d" % v])
                P.add("dve", lambda e, c=c, v=v: e.tensor_tensor(out=t1[v], in0=t1[v], in1=rstd_t, op=ALU.mult), reads=["t1%d" % v, "rstd_t"], writes=["t1%d" % v])
                P.add("act", lambda e, c=c, v=v: e.activation(out=t2[v], in_=t1[v], func=AF.Silu, scale=prm[:, c, 32:33], bias=prm[:, c, 33:34]),
                      reads=["t1%d" % v, "prm"], writes=["t2%d" % v])
                P.add("pool", lambda e, c=c, v=v: e.tensor_tensor(out=ogTt[:, c, :], in0=t2[v], in1=szT[:, c, :], op=ALU.mult),
                      reads=["t2%d" % v, "szT%d" % c], writes=["ogTt"])
            P.add("poolq", lambda e, g=g: e.dma_start(out=ogTs[:, g * TG:(g + 1) * TG].rearrange("(c f) t -> f c t", f=128), in_=ogTt),
                  reads=["ogTt"], writes=["ogTs"], chan="ogTst")

    def dsa_a(x_src):
        P.barrier()
        A.release(persist_mark)
        NIT = 18
        NOSAME = True
        PI = math.pi
        w_in = dsa_w_in[0]
        WC = 2888
        W = A.alloc(8 * WC, BF16).rearrange("p (k n) -> p k n", k=8)
        Wr = A.alloc(8 * 688, BF16).rearrange("p (k n) -> p k n", k=8)
        load_w(W[:, :, 0:2304], w_in, 2304, 8, "W", 0)
        load_w(W[:, :, 2304:WC], w_in, 584, 8, "W", 4352)
        load_w(Wr, dsa_w_rot, 688, 8, "Wr", 0)
        ropec = A.alloc(8, F32)
        negm = A.alloc(128, F32)
        pw2 = A.alloc(NIT, F32)
        sel = A.alloc(512, F32).rearrange("p (g m) -> p g m", g=4)
        ones_b = A.alloc(32, BF16)
        P.add("sp", lambda e: e.dma_start(out=ropec[0:32, :], in_=c_rope), writes=["ropec"], chan="cr0")
        P.add("sp", lambda e: e.dma_start(out=negm, in_=c_neg), writes=["negm"], chan="cr1")
        P.add("sp", lambda e: e.dma_start(out=pw2, in_=c_pw2), writes=["pw2"], chan="cr2")
        P.add("sp", lambda e: e.dma_start(out=sel.rearrange("p g m -> p (g m)"), in_=c_sel), writes=["sel"], chan="cr3")
        P.add("dve", lambda e: e.memset(ones_b, 1.0), writes=["ones_b"])
        kT = A.alloc(L, BF16)
        vtok = A.alloc(NT * 128, BF16).rearrange("p (b d) -> p b d", b=NT)
        kiT = A.alloc(L, BF16)
        xt = A.alloc(1024, F32)
        xb = A.alloc(1024, BF16)
        xT = [A.alloc(1024, BF16).rearrange("p (k t) -> p k t", k=8) for _ in range(2)]
        posi = A.alloc(128, I32)
        posf = A.alloc(128, F32)
        ang = A.alloc(128, F32)
        arg = A.alloc(128, F32)
        argi = A.alloc(128, I32)
        argk = A.alloc(128, F32)
        C32 = A.alloc(128, F32)
        S32 = A.alloc(128, F32)
        C16 = A.alloc(128, F32)
        S16 = A.alloc(128, F32)
        tA = A.alloc(1024, F32)
        tB = A.alloc(1024, F32)
        qT = [A.alloc(2048, BF16).rearrange("p (h t) -> p h t", h=16) for _ in range(2)]
        qiT = A.alloc(1024, BF16).rearrange("p (h t) -> p h t", h=8)
        wi = A.alloc(8, F32)
        acc = A.alloc(L, F32)
        maskb = A.alloc(L, BF16)
        maskT = [A.alloc(NT * 128, BF16).rearrange("p (b t) -> p b t", b=NT) for _ in range(2)]
        rbuf = [A.alloc(512, F32) for _ in range(2)]
        probs = [A.alloc(2048, BF16) for _ in range(2)]
        den_sb = A.alloc(512, F32)
        oS = A.alloc(2048, F32)
        ogTt = A.alloc(2048, BF16).rearrange("p (c t) -> p c t", c=16)
        sm = A.alloc(8 + 2 * NIT, F32)
        lo, mid, cnt, tmp, W0 = (sm[:, i:i + 1] for i in range(5))
        wcols = sm[:, 8:8 + NIT]
        nhw = sm[:, 8 + NIT:8 + 2 * NIT]

        def bc(ap32, n):
            return ap32.unsqueeze(1).to_broadcast([ap32.shape[0], n, 128])

        def proj_fm(dst_bank_ap, col0, ncol, u, wres, Wt, brs):
            for k in range(8):
                P.add("pe", lambda e, k=k: e.matmul(dst_bank_ap, lhsT=Wt[:, k, col0:col0 + ncol], rhs=xT[u][:, k, :],
                                                     start=(k == 0), stop=(k == 7)), reads=["xT%d" % u, wres], writes=brs)

        def stP(t):
            u = t % 2
            sfx = "%d" % u
            r0 = t * 128
            qTu = qT[u]
            qres = "qT%d" % u
            P.add("sp", lambda e: e.dma_start(out=xt, in_=x_src[r0:r0 + 128, :]), writes=["xt"], chan="xt")
            P.add("sp", lambda e: e.dma_start(out=posi[0:32, :], in_=pos_in[r0:r0 + 128].partition_broadcast(32)), writes=["posi"], chan="posi")
            P.add("act", lambda e: e.activation(out=xb, in_=xt, func=AF.Copy), reads=["xt"], writes=["xb"])
            transpose_to(xT[u], xb, 8, "xb", "xT" + sfx, evac="dve")
            P.add("dve", lambda e: e.tensor_copy(out=posf[0:32, :], in_=posi[0:32, :]), reads=["posi"], writes=["posf"])
            for (n, ci, Ct, St_, nm) in ((32, 0, C32, S32, "h"), (16, 3, C16, S16, "i")):
                P.add("dve", lambda e, n=n, ci=ci: e.tensor_scalar(out=ang[0:n, :], in0=posf[0:n, :], scalar1=ropec[0:n, ci:ci + 1], scalar2=None, op0=ALU.mult),
                      reads=["posf", "ropec"], writes=["ang"])
                for (off, dst, scale_, wn) in ((0.0, St_, ropec[0:n, ci + 1:ci + 2], "S" + nm), (0.25, Ct, 2 * PI, "C" + nm)):
                    P.add("dve", lambda e, n=n, off=off: e.tensor_scalar(out=arg[0:n, :], in0=ang[0:n, :], scalar1=float(off), scalar2=None, op0=ALU.add),
                          reads=["ang"], writes=["arg"])
                    P.add("dve", lambda e, n=n: e.tensor_copy(out=argi[0:n, :], in_=arg[0:n, :]), reads=["arg"], writes=["argi"])
                    P.add("dve", lambda e, n=n: e.tensor_copy(out=argk[0:n, :], in_=argi[0:n, :]), reads=["argi"], writes=["argk"])
                    P.add("dve", lambda e, n=n: e.tensor_tensor(out=arg[0:n, :], in0=arg[0:n, :], in1=argk[0:n, :], op=ALU.subtract), reads=["arg", "argk"], writes=["arg"])
                    P.add("dve", lambda e, n=n: e.tensor_scalar(out=argk[0:n, :], in0=arg[0:n, :], scalar1=0.5, scalar2=None, op0=ALU.is_gt), reads=["arg"], writes=["argk"])
                    P.add("dve", lambda e, n=n: e.tensor_tensor(out=arg[0:n, :], in0=arg[0:n, :], in1=argk[0:n, :], op=ALU.subtract), reads=["arg", "argk"], writes=["arg"])
                    P.add("dve", lambda e, n=n: e.tensor_scalar(out=argk[0:n, :], in0=arg[0:n, :], scalar1=-0.5, scalar2=None, op0=ALU.is_lt), reads=["arg"], writes=["argk"])
                    P.add("dve", lambda e, n=n: e.tensor_tensor(out=arg[0:n, :], in0=arg[0:n, :], in1=argk[0:n, :], op=ALU.add), reads=["arg", "argk"], writes=["arg"])
                    P.add("act", lambda e, n=n, dst=dst, scale_=scale_: e.activation(out=dst[0:n, :], in_=arg[0:n, :], func=AF.Sin, scale=scale_),
                          reads=["arg", "ropec"], writes=[wn])
            for g8 in range(2):
                qb, qbr = palloc(2)
                for hh in range(8):
                    h = g8 * 8 + hh
                    proj_fm(bank(qb, 2)[:, hh * 128:(hh + 1) * 128], h * 128, 128, u, "W", W, [qbr[hh // 4]])
                rb, rbr = palloc(2)
                for hh in range(8):
                    h = g8 * 8 + hh
                    proj_fm(bank(rb, 2)[0:32, hh * 128:(hh + 1) * 128], h * 32, 32, u, "Wr", Wr, [rbr[hh // 4]])
                P.add("act", lambda e, g8=g8, qb=qb: e.activation(out=qTu[:, g8 * 8:(g8 + 1) * 8, :].rearrange("p h t -> p (h t)"), in_=bank(qb, 2), func=AF.Copy),
                      reads=qbr, writes=[qres])
                P.add("dve", lambda e, qb=qb: e.tensor_tensor(out=tA[0:32, :].rearrange("p (h t) -> p h t", h=8), in0=bank(qb, 2)[0:32, :].rearrange("p (h t) -> p h t", h=8),
                                                             in1=bc(C32[0:32, :], 8), op=ALU.mult), reads=qbr + ["Ch", qres], writes=["tA"])
                P.add("dve", lambda e, rb=rb: e.tensor_tensor(out=tB[0:32, :].rearrange("p (h t) -> p h t", h=8), in0=bank(rb, 2)[0:32, :].rearrange("p (h t) -> p h t", h=8),
                                                             in1=bc(S32[0:32, :], 8), op=ALU.mult), reads=rbr + ["Sh"], writes=["tB"])
                P.add("dve", lambda e, g8=g8: e.tensor_tensor(out=qTu[0:32, g8 * 8:(g8 + 1) * 8, :].rearrange("p h t -> p (h t)"), in0=tA[0:32, :], in1=tB[0:32, :], op=ALU.add),
                      reads=["tA", "tB"], writes=[qres])
            kb_, kbr_ = palloc(1)
            proj_fm(bank(kb_)[:, 0:128], 2048, 128, u, "W", W, kbr_)
            kr, krr = palloc(1)
            proj_fm(bank(kr)[0:32, 0:128], 512, 32, u, "Wr", Wr, krr)
            P.add("act", lambda e: e.activation(out=kT[:, r0:r0 + 128], in_=bank(kb_)[:, 0:128], func=AF.Copy), reads=kbr_, writes=["kT"])
            P.add("dve", lambda e: e.tensor_tensor(out=tA[0:32, 0:128], in0=bank(kb_)[0:32, 0:128], in1=C32[0:32, :], op=ALU.mult), reads=kbr_ + ["Ch"], writes=["tA"])
            P.add("dve", lambda e: e.tensor_tensor(out=tB[0:32, 0:128], in0=bank(kr)[0:32, 0:128], in1=S32[0:32, :], op=ALU.mult), reads=krr + ["Sh"], writes=["tB"])
            P.add("dve", lambda e: e.tensor_tensor(out=kT[0:32, r0:r0 + 128], in0=tA[0:32, 0:128], in1=tB[0:32, 0:128], op=ALU.add), reads=["tA", "tB"], writes=["kT"])
            vb_, vbr_ = palloc(1)
            for k in range(8):
                P.add("pe", lambda e, k=k: e.matmul(bank(vb_)[:, 0:128], lhsT=xT[u][:, k, :], rhs=W[:, k, 2176:2304], start=(k == 0), stop=(k == 7)),
                      reads=["xT" + sfx, "W"], writes=vbr_)
            P.add("act", lambda e: e.activation(out=vtok[:, t, :], in_=bank(vb_)[:, 0:128], func=AF.Copy), reads=vbr_, writes=["vtok"])
            wb_, wbr_ = palloc(1)
            for k in range(8):
                P.add("pe", lambda e, k=k: e.matmul(bank(wb_)[:, 0:8], lhsT=xT[u][:, k, :], rhs=W[:, k, 2880:2888], start=(k == 0), stop=(k == 7)),
                      reads=["xT" + sfx, "W"], writes=wbr_)
            P.add("dve", lambda e: e.tensor_copy(out=wi, in_=bank(wb_)[:, 0:8]), reads=wbr_, writes=["wi"])
            ib, ibr = palloc(2)
            for h in range(8):
                proj_fm(bank(ib, 2)[0:64, h * 128:(h + 1) * 128], 2304 + h * 64, 64, u, "W", W, [ibr[h // 4]])
            jb, jbr = palloc(2)
            for h in range(8):
                proj_fm(bank(jb, 2)[0:16, h * 128:(h + 1) * 128], 544 + h * 16, 16, u, "Wr", Wr, [jbr[h // 4]])
            P.add("act", lambda e: e.activation(out=qiT[0:64, :, :].rearrange("p h t -> p (h t)"), in_=bank(ib, 2)[0:64, :], func=AF.Copy), reads=ibr, writes=["qiT"])
            P.add("dve", lambda e: e.tensor_tensor(out=tA[0:16, :].rearrange("p (h t) -> p h t", h=8), in0=bank(ib, 2)[0:16, :].rearrange("p (h t) -> p h t", h=8),
                                                  in1=bc(C16[0:16, :], 8), op=ALU.mult), reads=ibr + ["Ci"], writes=["tA"])
            P.add("dve", lambda e: e.tensor_tensor(out=tB[0:16, :].rearrange("p (h t) -> p h t", h=8), in0=bank(jb, 2)[0:16, :].rearrange("p (h t) -> p h t", h=8),
                                                  in1=bc(S16[0:16, :], 8), op=ALU.mult), reads=jbr + ["Si"], writes=["tB"])
            P.add("dve", lambda e: e.tensor_tensor(out=qiT[0:16, :, :].rearrange("p h t -> p (h t)"), in0=tA[0:16, :], in1=tB[0:16, :], op=ALU.add),
                  reads=["tA", "tB"], writes=["qiT"])
            cb, cbr = palloc(1)
            proj_fm(bank(cb)[0:64, 0:128], 2816, 64, u, "W", W, cbr)
            db, dbr = palloc(1)
            proj_fm(bank(db)[0:16, 0:128], 672, 16, u, "Wr", Wr, dbr)
            P.add("act", lambda e: e.activation(out=kiT[0:64, r0:r0 + 128], in_=bank(cb)[0:64, 0:128], func=AF.Copy), reads=cbr, writes=["kiT"])
            P.add("dve", lambda e: e.tensor_tensor(out=tA[0:16, 0:128], in0=bank(cb)[0:16, 0:128], in1=C16[0:16, :], op=ALU.mult), reads=cbr + ["Ci"], writes=["tA"])
            P.add("dve", lambda e: e.tensor_tensor(out=tB[0:16, 0:128], in0=bank(db)[0:16, 0:128], in1=S16[0:16, :], op=ALU.mult), reads=dbr + ["Si"], writes=["tB"])
            P.add("dve", lambda e: e.tensor_tensor(out=kiT[0:16, r0:r0 + 128], in0=tA[0:16, 0:128], in1=tB[0:16, 0:128], op=ALU.add), reads=["tA", "tB"], writes=["kiT"])

        def stI(t):
            r0 = t * 128
            St = (t + 1) * 128
            for s0 in range(0, St, 512):
                n = min(512, St - s0)
                for h in range(8):
                    v = h % 2
                    sb_, sbr = palloc(1)
                    P.add("pe", lambda e, sb_=sb_, h=h, s0=s0, n=n: e.matmul(bank(sb_)[:, 0:n], lhsT=qiT[0:64, h, :], rhs=kiT[0:64, s0:s0 + n], start=True, stop=True),
                          reads=["qiT", "kiT"], writes=sbr)
                    P.add("act", lambda e, sb_=sb_, v=v, n=n: e.activation(out=rbuf[v][:, 0:n], in_=bank(sb_)[:, 0:n], func=AF.Relu), reads=sbr, writes=["rbuf%d" % v])
                    if h == 0:
                        P.add("dve", lambda e, v=v, s0=s0, n=n: e.tensor_scalar(out=acc[:, s0:s0 + n], in0=rbuf[v][:, 0:n], scalar1=wi[:, 0:1], scalar2=None, op0=ALU.mult),
                              reads=["rbuf%d" % v, "wi"], writes=["acc"])
                    else:
                        P.add("dve", lambda e, v=v, s0=s0, n=n, h=h: e.scalar_tensor_tensor(out=acc[:, s0:s0 + n], in0=rbuf[v][:, 0:n], scalar=wi[:, h:h + 1], in1=acc[:, s0:s0 + n],
                                                                                      op0=ALU.mult, op1=ALU.add), reads=["rbuf%d" % v, "wi", "acc"], writes=["acc"])
            if t >= 2:
                P.add("dve", lambda e: e.tensor_reduce(out=W0, in_=acc[:, 0:St], axis=mybir.AxisListType.X, op=ALU.max, apply_absolute_value=True),
                      reads=["acc"], writes=["sm"])
            P.add("pool", lambda e: e.tensor_tensor(out=acc[:, r0:r0 + 128], in0=acc[:, r0:r0 + 128], in1=negm, op=ALU.add), reads=["acc", "negm"], writes=["acc"])

        def stT(t):
            St = (t + 1) * 128
            u = t % 2
            if t >= 2:
                P.add("dve", lambda e: e.memset(mid, 0.0), writes=["sm"])
                P.add("dve", lambda e: e.tensor_scalar(out=wcols, in0=pw2, scalar1=W0, scalar2=2.0, op0=ALU.mult, op1=ALU.mult), reads=["sm", "pw2"], writes=["sm"])
                P.add("dve", lambda e: e.tensor_scalar(out=nhw, in0=wcols, scalar1=-0.5, scalar2=None, op0=ALU.mult), reads=["sm"], writes=["sm"])
                for it in range(NIT):
                    P.add("dve", lambda e: e.tensor_scalar(out=maskb[:, 0:St], in0=acc[:, 0:St], scalar1=mid, scalar2=None, op0=ALU.is_ge, op1=ALU.add, accum_out=cnt),
                          reads=["acc", "sm"], writes=["maskb", "sm"])
                    P.add("dve", lambda e, it=it: e.tensor_scalar(out=tmp, in0=cnt, scalar1=256.0, scalar2=wcols[:, it:it + 1], op0=ALU.is_ge, op1=ALU.mult), reads=["sm"], writes=["sm"], nosame=NOSAME)
                    P.add("dve", lambda e, it=it: e.scalar_tensor_tensor(out=mid, in0=tmp, scalar=nhw[:, it:it + 1], in1=mid, op0=ALU.add, op1=ALU.add), reads=["sm"], writes=["sm"], nosame=NOSAME)
                P.add("dve", lambda e: e.tensor_scalar(out=lo, in0=W0, scalar1=-(2.0 ** -NIT), scalar2=mid, op0=ALU.mult, op1=ALU.add), reads=["sm"], writes=["sm"])
                P.add("dve", lambda e: e.tensor_scalar(out=maskb[:, 0:St], in0=acc[:, 0:St], scalar1=lo, scalar2=None, op0=ALU.is_ge), reads=["acc", "sm"], writes=["maskb"])
            else:
                P.add("dve", lambda e: e.tensor_scalar(out=maskb[:, 0:St], in0=acc[:, 0:St], scalar1=-1e29, scalar2=None, op0=ALU.is_ge), reads=["acc"], writes=["maskb"])
            transpose_to(maskT[u], maskb, t + 1, "maskb", "maskT%d" % u, evac="dve")

        ring = [5, 6, 7]

        def stAtt(t):
            u = t % 2
            qTu = qT[u]

            def logits(kb):
                pv = kb % 2
                for hg in range(4):
                    lb = ring[(kb * 4 + hg) % 3]
                    P.add("pe", lambda e, lb=lb, hg=hg: e.matmul(bank(lb), lhsT=kT[:, kb * 128:(kb + 1) * 128],
                                                               rhs=qTu[:, hg * 4:(hg + 1) * 4, :].rearrange("p h t -> p (h t)"), start=True, stop=True),
                          reads=["kT", "qT%d" % u], writes=["pb%d" % lb])
                    P.add("act", lambda e, lb=lb, hg=hg: e.activation(out=probs[pv][:, hg * 512:(hg + 1) * 512], in_=bank(lb), func=AF.Exp, scale=float(ATTN_SCALE)),
                          reads=["pb%d" % lb], writes=["probs%d_%d" % (pv, hg)])
                    P.add("pool", lambda e, hg=hg: e.tensor_tensor(out=probs[pv][:, hg * 512:(hg + 1) * 512].rearrange("p (h t) -> p h t", h=4),
                                                                 in0=probs[pv][:, hg * 512:(hg + 1) * 512].rearrange("p (h t) -> p h t", h=4),
                                                                 in1=bc(maskT[u][:, kb, :], 4), op=ALU.mult),
                          reads=["probs%d_%d" % (pv, hg), "maskT%d" % u], writes=["probs%d_%d" % (pv, hg)])

            def pvmm(kb):
                pv = kb % 2
                for hg in range(4):
                    P.add("pe", lambda e, hg=hg: e.matmul(bank(hg), lhsT=vtok[:, kb, :], rhs=probs[pv][:, hg * 512:(hg + 1) * 512],
                                                        start=(kb == 0), stop=(kb == t)), reads=["vtok", "probs%d_%d" % (pv, hg)], writes=["pb%d" % hg])
                    P.add("pe", lambda e, hg=hg: e.matmul(bank(4)[32 * hg:32 * (hg + 1), :], lhsT=ones_b[:, 0:32], rhs=probs[pv][:, hg * 512:(hg + 1) * 512],
                                                        start=(kb == 0), stop=(kb == t), tile_position=(0, 32 * hg)), reads=["ones_b", "probs%d_%d" % (pv, hg)], writes=["pb4"])

            logits(0)
            for kb in range(t + 1):
                if kb + 1 <= t:
                    logits(kb + 1)
                pvmm(kb)

        def stFe(t):
            P.add("act", lambda e: e.activation(out=den_sb, in_=bank(4), func=AF.Copy), reads=["pb4"], writes=["den_sb"])
            for hp in range(2):
                P.add("act", lambda e, hp=hp: e.activation(out=oS[:, hp * 1024:(hp + 1) * 1024], in_=bank(2 * hp, 2), func=AF.Copy),
                      reads=["pb%d" % (2 * hp), "pb%d" % (2 * hp + 1)], writes=["oS%d" % hp])

        def stFm(t):
            r0 = t * 128
            P.add("dve", lambda e: e.reciprocal(out=den_sb, in_=den_sb), reads=["den_sb"], writes=["den_sb"])
            for hg in range(4):
                lb, lbr = palloc(1)
                P.add("pe", lambda e, lb=lb, hg=hg: e.matmul(bank(lb), lhsT=sel[:, hg, :], rhs=den_sb, start=True, stop=True), reads=["sel", "den_sb"], writes=lbr)
                P.add("dve", lambda e, lb=lb, hg=hg: e.tensor_tensor(out=ogTt[:, hg * 4:(hg + 1) * 4, :].rearrange("p c t -> p (c t)"), in0=bank(lb), in1=oS[:, hg * 512:(hg + 1) * 512], op=ALU.mult),
                      reads=lbr + ["oS%d" % (hg // 2)], writes=["ogTt"])
            P.add("poolq", lambda e: e.dma_start(out=ogTs[:, r0:r0 + 128].rearrange("(c f) t -> f c t", f=128), in_=ogTt),
                  reads=["ogTt"], writes=["ogTs"], chan="ogTst")

        if os.environ.get("DSA_NOPIPE"):
            for t in range(NT):
                stP(t)
                stI(t)
                stT(t)
                stAtt(t)
                stFe(t)
                stFm(t)
            return
        VAR = os.environ.get("DSA_VAR", "b")
        stP(0)
        stI(0)
        stT(0)
        for t in range(NT):
            if t + 1 < NT:
                stP(t + 1)
                stI(t + 1)
                if VAR == "a":
                    stT(t + 1)
            stAtt(t)
            stFe(t)
            if t + 1 < NT and VAR != "a":
                stT(t + 1)
            stFm(t)

    cur = x_in
    for li in layers:
        kind = li % 3
        j = li // 3
        dst = out if li == layers[-1] else xs
        if kind == 0:
            gla_a(j, cur)
            phase_b(li, cur, dst, gla_w_out[j], False)
        elif kind == 2:
            conv_a(cur)
            phase_b(li, cur, dst, conv_w_out[0], True)
        else:
            dsa_a(cur)
            phase_b(li, cur, dst, dsa_w_out[0], True, zgate_w=dsa_w_in[0])
        cur = dst
    for u in range(2):
        P.final_waits.append("xo%d" % u)
    P.emit(st)
    st.close()
    return nc


def _consts():
    j = np.arange(128)[:, None]
    i = np.arange(128)[None, :]
    return {
        "c_ident": np.eye(128, dtype=np.float32),
        "c_umat": np.where(j <= i, -1.0 / 16.0, 0.0).astype(np.float32),
        "c_lmat": np.where(j > i, -1.0 / 16.0, 0.0).astype(np.float32),
        "c_tri": np.where(j <= i, 1.0, 0.0).astype(np.float32),
        "c_rope": _rope_consts(),
        "c_neg": np.where(i <= j, 0.0, -1e30).astype(np.float32),
        "c_pw2": np.tile((2.0 ** -(np.arange(18) + 1.0))[None, :], (128, 1)).astype(np.float32),
        "c_sel": _sel_consts(),
    }


def _rope_consts():
    c = np.zeros((32, 8), np.float32)
    pp = np.arange(32)
    c[:, 0] = (np.float32(ROPE_THETA) ** (-(np.arange(0, 32, 2, dtype=np.float32)) / np.float32(32)))[pp % 16] / (2 * math.pi)
    c[:, 1] = np.where(pp < 16, -1.0, 1.0) * 2 * math.pi
    c[:, 3] = (np.float32(ROPE_THETA) ** (-(np.arange(0, 16, 2, dtype=np.float32)) / np.float32(16)))[pp % 8] / (2 * math.pi)
    c[:, 4] = np.where(pp % 16 < 8, -1.0, 1.0) * 2 * math.pi
    return c


def _sel_consts():
    s = np.zeros((128, 4, 128), np.float32)
    for g in range(4):
        s[32 * g, g, :] = 1.0
    return s.reshape(128, 512)


def _rot_perm():
    idx = []
    for h in range(16):
        idx += list(range(h * 128 + 16, h * 128 + 32)) + list(range(h * 128, h * 128 + 16))
    idx += list(range(2048 + 16, 2048 + 32)) + list(range(2048, 2048 + 16))
    for h in range(8):
        b = 4352 + h * 64
        idx += list(range(b + 8, b + 16)) + list(range(b, b + 8))
    idx += list(range(4864 + 8, 4864 + 16)) + list(range(4864, 4864 + 8))
    return np.asarray(idx)


WEIGHT_KEYS = ["gla_w_in", "gla_w_a2", "gla_b_a", "gla_gn_g", "gla_w_out", "dsa_w_in", "dsa_w_out",
               "conv_w_in", "conv_dw_w", "conv_dw_b", "conv_ln_g", "conv_ln_b", "conv_w_out",
               "ln_g", "ln_b", "ple_w", "ple_gate_w"]


def run(inputs, NT=32, layers=(0, 1, 2, 3), cores=None):
    B = inputs["x"].shape[0]
    cores = list(range(B)) if cores is None else cores
    L = NT * 128
    nc = build_program(NT, list(layers))
    consts = _consts()
    in_maps = []
    for b in cores:
        m = {"x": np.ascontiguousarray(inputs["x"][b, :L]),
             "p": np.ascontiguousarray(inputs["p"][:, b, :L]),
             "pos": np.ascontiguousarray(inputs["positions"][b, :L]).astype(np.int32)}
        for k in WEIGHT_KEYS:
            m[k] = np.ascontiguousarray(inputs[k])
        m["dsa_w_rot"] = np.ascontiguousarray(inputs["dsa_w_in"][0][:, _rot_perm()])
        m.update(consts)
        in_maps.append(m)
    res = run_bass_kernel_spmd(nc, in_maps, core_ids=list(range(len(cores))))
    return np.stack([r["out"] for r in res.results], axis=0)


def kernel(**inputs):
    inputs = {k: np.asarray(v) for k, v in inputs.items()}
    return run(inputs).astype(np.float32)
```

```python
import math
import os
from contextlib import ExitStack
import numpy as np
import concourse.bass as bass
import concourse.mybir as mybir
from concourse.bass_utils import run_bass_kernel_spmd

F32 = mybir.dt.float32
BF16 = mybir.dt.bfloat16
I32 = mybir.dt.int32
ALU = mybir.AluOpType
AF = mybir.ActivationFunctionType

D = 1024
SEQ = 4096
DEPTH = 4
DI = 2048
GLA_IN = 6160
DSA_IN = 4936
CONV_IN = 6144
ALPHA = (2 * DEPTH) ** 0.25
EPS = 1e-5
ROPE_THETA = 500000.0
ATTN_SCALE = 128 ** -0.5
CONV_W = 31


NOSYNC_ENGS = set(os.environ.get("NOSYNC_ENGS", "").split(","))


class Op:
    __slots__ = ("eng", "fn", "deps", "idx", "chan", "ticket", "needs_inc", "nosame")


class Prog:
    PHYS = {"pe": "tensor", "act": "scalar", "dve": "vector", "pool": "gpsimd",
            "sp": "sync", "actq": "scalar", "poolq": "gpsimd"}

    def __init__(self, nc, same_engine_raw=True):
        self.nc = nc
        self.ops = []
        self.lastw = {}
        self.readers = {}
        self.same_engine_raw = same_engine_raw
        self.last_on = {}
        self.dma_since = []
        self.barrier_deps = None
        self.barrier_seen = set()
        self.final_waits = []
        self.groups = {}

    def add(self, eng, fn, reads=(), writes=(), chan=None, group=None, nosame=False):
        op = Op()
        op.nosame = nosame
        op.eng = eng
        op.fn = fn
        op.idx = len(self.ops)
        op.chan = chan
        op.ticket = None
        op.needs_inc = False
        writes = list(writes) + [r + "_rd" for r in reads if r.startswith("pb")]
        deps = {}
        for r in reads:
            for w in self.lastw.get(r, ()):
                deps[w] = "raw"
        for w_ in writes:
            lw = self.lastw.get(w_, ())
            same_group = group is not None and lw and all(self.groups.get(w) == group for w in lw)
            if not same_group:
                for w in lw:
                    if w not in deps:
                        deps[w] = "waw"
            for rd in self.readers.get(w_, ()):
                if rd not in deps:
                    deps[rd] = "war"
        for r in reads:
            self.readers.setdefault(r, []).append(op.idx)
        for w_ in writes:
            lw = self.lastw.get(w_, ())
            if group is not None and lw and all(self.groups.get(w) == group for w in lw):
                self.lastw[w_] = list(lw) + [op.idx]
            else:
                self.lastw[w_] = [op.idx]
                self.readers[w_] = []
        if group is not None:
            self.groups[op.idx] = group
        ph = self.PHYS[eng]
        if self.barrier_deps is not None and ph not in self.barrier_seen:
            self.barrier_seen.add(ph)
            for d in self.barrier_deps:
                if d not in deps:
                    deps[d] = "raw"
        op.deps = deps
        self.ops.append(op)
        self.last_on[ph] = op.idx
        if chan is not None:
            self.dma_since.append(op.idx)
        return op

    def barrier(self):
        deps = set(self.last_on.values()) | set(self.dma_since)
        self.barrier_deps = deps
        self.barrier_seen = set()
        self.dma_since = []
        self.lastw = {}
        self.readers = {}

    def emit(self, stack):
        nc = self.nc
        ops = self.ops
        phys = [self.PHYS[o.eng] for o in ops]
        is_dma = [o.chan is not None for o in ops]
        need = [[] for _ in ops]
        for o in ops:
            for d, kind in o.deps.items():
                if is_dma[d]:
                    need[o.idx].append(d)
                elif phys[d] == phys[o.idx]:
                    if is_dma[o.idx]:
                        need[o.idx].append(d)
                    elif kind == "raw" and self.same_engine_raw and o.eng != "pe" and o.eng not in NOSYNC_ENGS and not o.nosame:
                        need[o.idx].append(d)
                else:
                    need[o.idx].append(d)
        for o in ops:
            for d in need[o.idx]:
                ops[d].needs_inc = True
            if is_dma[o.idx]:
                o.needs_inc = True
        sems = {}
        counts = {}
        for o in ops:
            if not o.needs_inc:
                continue
            key = ("d_" + o.chan) if is_dma[o.idx] else ("e_" + phys[o.idx])
            if key not in sems:
                sems[key] = stack.enter_context(nc.semaphore("s_" + key))
                counts[key] = 0
            counts[key] += 16 if is_dma[o.idx] else 1
            o.ticket = (key, counts[key])
        self.n_sems = len(sems)
        block = stack.enter_context(nc.Block())
        streams = {}
        for o in ops:
            streams.setdefault(phys[o.idx], []).append(o)
        final = self.final_waits

        def run_stream(e, lst):
            waited = {}
            for o in lst:
                req = {}
                for d in need[o.idx]:
                    key, val = ops[d].ticket
                    if req.get(key, 0) < val:
                        req[key] = val
                for key, val in req.items():
                    if waited.get(key, 0) >= val:
                        continue
                    e.wait_ge(sems[key], val)
                    waited[key] = val
                ins = o.fn(e)
                if o.needs_inc:
                    key, val = o.ticket
                    ins.then_inc(sems[key], 16 if is_dma[o.idx] else 1)

        def mk(name):
            def f(e):
                run_stream(e, streams.get(name, []))
                if name == "gpsimd":
                    for ch in final:
                        key = "d_" + ch
                        if key in sems:
                            e.wait_ge(sems[key], counts[key])
            return f

        block.sync(mk("sync"))
        block.scalar(mk("scalar"))
        block.vector(mk("vector"))
        block.gpsimd(mk("gpsimd"))
        block.tensor(mk("tensor"))


class Arena:
    def __init__(self, ap, nwords):
        self.ap = ap
        self.n = nwords
        self.off = 0

    def alloc(self, nelem, dt):
        words = nelem if dt in (F32, I32) else (nelem + 1) // 2
        words = (words + 15) // 16 * 16
        assert self.off + words <= self.n, ("SBUF arena overflow", self.off, words, self.n)
        a = self.ap[:, self.off:self.off + words]
        self.off += words
        if dt != F32:
            a = a.bitcast(dt)
        return a[:, 0:nelem]

    def mark(self):
        return self.off

    def release(self, m):
        self.off = m


def build_program(NT, layers, dbg=False):
    L = NT * 128
    nc = bass.Bass("TRN2", target_bir_lowering=False)

    def din(name, shape, dt=F32):
        return nc.dram_tensor(name, list(shape), dt, kind="ExternalInput").ap()

    x_in = din("x", [L, D])
    p_in = din("p", [DEPTH, L, 256])
    pos_in = din("pos", [L], I32)
    gla_w_in = din("gla_w_in", [2, D, GLA_IN])
    gla_w_a2 = din("gla_w_a2", [2, 16, D])
    gla_b_a = din("gla_b_a", [2, D])
    gla_gn_g = din("gla_gn_g", [2, DI])
    gla_w_out = din("gla_w_out", [2, DI, D])
    dsa_w_in = din("dsa_w_in", [1, D, DSA_IN])
    dsa_w_out = din("dsa_w_out", [1, DI, D])
    conv_w_in = din("conv_w_in", [1, D, CONV_IN])
    conv_dw_w = din("conv_dw_w", [1, CONV_W, DI])
    conv_dw_b = din("conv_dw_b", [1, DI])
    conv_ln_g = din("conv_ln_g", [1, DI])
    conv_ln_b = din("conv_ln_b", [1, DI])
    conv_w_out = din("conv_w_out", [1, DI, D])
    ln_g = din("ln_g", [DEPTH, D])
    ln_b = din("ln_b", [DEPTH, D])
    ple_w = din("ple_w", [DEPTH, 256, D])
    ple_gate_w = din("ple_gate_w", [DEPTH, D, D])
    c_ident = din("c_ident", [128, 128])
    c_umat = din("c_umat", [128, 128])
    c_lmat = din("c_lmat", [128, 128])
    c_tri = din("c_tri", [128, 128])
    c_rope = din("c_rope", [32, 8])
    c_neg = din("c_neg", [128, 128])
    c_pw2 = din("c_pw2", [128, 18])
    c_sel = din("c_sel", [128, 512])
    dsa_w_rot = din("dsa_w_rot", [D, 688])
    out = nc.dram_tensor("out", [L, D], F32, kind="ExternalOutput").ap()
    xs = nc.dram_tensor("xs", [L, D], F32).ap()
    ogs = nc.dram_tensor("ogs", [L, DI], BF16).ap()
    ogTs = nc.dram_tensor("ogTs", [DI, L], BF16).ap()

    st = ExitStack()
    ARENA_WORDS = 52000
    arena_t = st.enter_context(nc.sbuf_tensor("arena", [128, ARENA_WORDS], F32))
    A = Arena(arena_t[:], ARENA_WORDS)
    psum_t = st.enter_context(nc.psum_tensor("ps", [128, 4096], F32))
    psum = psum_t[:]
    P = Prog(nc)

    def bank(i, n=1):
        return psum[:, i * 512:(i + n) * 512]

    pstate = {"ptr": 0}
    gctr = [0]

    def palloc(n):
        if pstate["ptr"] + n > 8:
            pstate["ptr"] = 0
        b = pstate["ptr"]
        pstate["ptr"] += n
        return b, ["pb%d" % i for i in range(b, b + n)]

    ident_f = A.alloc(128, F32)
    ident_b = A.alloc(128, BF16)
    umat = A.alloc(128, F32)
    lmat = A.alloc(128, F32)
    tri = A.alloc(128, F32)
    P.add("sp", lambda e: e.dma_start(out=ident_f, in_=c_ident), writes=["ident_f"], chan="c0")
    P.add("sp", lambda e: e.dma_start(out=umat, in_=c_umat), writes=["umat"], chan="c1")
    P.add("sp", lambda e: e.dma_start(out=lmat, in_=c_lmat), writes=["lmat"], chan="c2")
    P.add("sp", lambda e: e.dma_start(out=tri, in_=c_tri), writes=["tri"], chan="c3")
    P.add("dve", lambda e: e.tensor_copy(out=ident_b, in_=ident_f), reads=["ident_f"], writes=["ident_b"])
    persist_mark = A.mark()

    def load_w(dst, src_rows, ncols, kchunks, name, c0=0):
        step = 2048
        gctr[0] += 1
        grp = "g%d" % gctr[0]
        for k in range(kchunks):
            for cs in range(0, ncols, step):
                ce = min(ncols, cs + step)
                P.add("poolq",
                      lambda e, k=k, cs=cs, ce=ce: e.dma_start(
                          out=dst[:, k, cs:ce], in_=src_rows[k * 128:(k + 1) * 128, c0 + cs:c0 + ce]),
                      writes=[name], chan="w_" + name, group=grp)

    tb_state = {"i": 0}

    def transpose_to(dst, src_bf, nblk, rname, wname, evac="dve", banks=None):
        done = 0
        while done < nblk:
            n = min(8, nblk - done)
            if banks is None:
                b, br = palloc(1)
            else:
                b = banks[tb_state["i"] % len(banks)]
                tb_state["i"] += 1
                br = ["pb%d" % b]
            pv = bank(b).bitcast(BF16)
            for k in range(n):
                P.add("pe", lambda e, k=k, done=done, pv=pv: e.transpose(
                    out=pv[:, k * 128:(k + 1) * 128], in_=src_bf[:, (done + k) * 128:(done + k + 1) * 128],
                    identity=ident_b), reads=[rname, "ident_b"], writes=br)
            P.add(evac, lambda e, n=n, done=done, pv=pv: e.tensor_copy(
                out=dst[:, done:done + n, :], in_=pv[:, 0:n * 128].rearrange("p (k t) -> p k t", k=n))
                if evac != "act" else e.activation(
                out=dst[:, done:done + n, :], in_=pv[:, 0:n * 128].rearrange("p (k t) -> p k t", k=n), func=AF.Copy),
                reads=br, writes=[wname])
            done += n

    def phase_b(li, x_src, x_dst, w_out_d, og_feature_major, zgate_w=None):
        P.barrier()
        A.release(persist_mark)
        wout = A.alloc(16 * 1024, BF16).rearrange("p (k n) -> p k n", k=16)
        wgate = A.alloc(8 * 1024, BF16).rearrange("p (k n) -> p k n", k=8)
        wple = A.alloc(2 * 1024, BF16).rearrange("p (k n) -> p k n", k=2)
        lng = A.alloc(1024, F32)
        lnb = A.alloc(1024, F32)
        load_w(wout, w_out_d, 1024, 16, "wout")
        load_w(wgate, ple_gate_w[li], 1024, 8, "wgate")
        load_w(wple, ple_w[li], 1024, 2, "wple")
        P.add("sp", lambda e: e.dma_start(out=lng, in_=ln_g[li].partition_broadcast(128)), writes=["lng"], chan="lng")
        P.add("sp", lambda e: e.dma_start(out=lnb, in_=ln_b[li].partition_broadcast(128)), writes=["lnb"], chan="lnb")
        NB = 2
        NB0 = 3
        if zgate_w is not None:
            wz = A.alloc(8 * 2048, BF16).rearrange("p (k n) -> p k n", k=8)
            load_w(wz, zgate_w, 2048, 8, "wz", 2304)
            xzb = [A.alloc(1024, BF16) for _ in range(NB0)]
            xzT = [A.alloc(1024, BF16).rearrange("p (k t) -> p k t", k=8) for _ in range(NB0)]
            szT = [A.alloc(2048, BF16).rearrange("p (c t) -> p c t", c=16) for _ in range(NB0)]
        ogt = [A.alloc(2048, BF16) for _ in range(NB0)]
        ogT = [A.alloc(2048, BF16).rearrange("p (k t) -> p k t", k=16) for _ in range(NB0)]
        xt = [A.alloc(1024, F32) for _ in range(NB0)]
        pt = [A.alloc(256, F32) for _ in range(NB)]
        s_ = [A.alloc(1024, F32) for _ in range(NB)]
        xn = [A.alloc(1024, F32) for _ in range(NB)]
        xnb = [A.alloc(1024, BF16) for _ in range(NB)]
        xnT = [A.alloc(1024, BF16).rearrange("p (k t) -> p k t", k=8) for _ in range(NB)]
        ptb = [A.alloc(256, BF16) for _ in range(NB)]
        pT = [A.alloc(256, BF16).rearrange("p (k t) -> p k t", k=2) for _ in range(NB)]
        sig = [A.alloc(1024, F32) for _ in range(NB)]
        xo = [A.alloc(1024, F32) for _ in range(NB)]
        stats = [A.alloc(16, F32) for _ in range(NB)]
        ctx = {}

        def stage0(t):
            w = t % NB0
            wfx = "w%d" % w
            r0 = t * 128
            if og_feature_major:
                P.add("sp", lambda e, w=w, r0=r0: e.dma_start(
                    out=ogT[w], in_=ogTs[:, r0:r0 + 128].rearrange("(k f) t -> f k t", f=128)),
                    reads=["ogTs"], writes=["ogT" + wfx], chan="ogT" + wfx)
            else:
                P.add("sp", lambda e, w=w, r0=r0: e.dma_start(out=ogt[w], in_=ogs[r0:r0 + 128, :]),
                      reads=["ogs"], writes=["ogt" + wfx], chan="ogt" + wfx)
                transpose_to(ogT[w], ogt[w], 16, "ogt" + wfx, "ogT" + wfx, evac="act", banks=[6, 7])
            P.add("sp", lambda e, w=w, r0=r0: e.dma_start(out=xt[w], in_=x_src[r0:r0 + 128, :]),
                  reads=["xsrc%d" % t], writes=["xt" + wfx], chan="xt" + wfx)
            if zgate_w is not None:
                P.add("act", lambda e, w=w: e.activation(out=xzb[w], in_=xt[w], func=AF.Copy), reads=["xt" + wfx], writes=["xzb" + wfx])
                transpose_to(xzT[w], xzb[w], 8, "xzb" + wfx, "xzT" + wfx, evac="dve", banks=[6, 7])
                for g8 in range(2):
                    zb = 2 + 2 * g8
                    zbr = ["pb%d" % zb, "pb%d" % (zb + 1)]
                    for hh in range(8):
                        c = g8 * 8 + hh
                        for k in range(8):
                            P.add("pe", lambda e, w=w, zb=zb, hh=hh, c=c, k=k: e.matmul(bank(zb, 2)[:, hh * 128:(hh + 1) * 128], lhsT=wz[:, k, c * 128:(c + 1) * 128],
                                                                                   rhs=xzT[w][:, k, :], start=(k == 0), stop=(k == 7)),
                                  reads=["xzT" + wfx, "wz"], writes=[zbr[hh // 4]])
                    P.add("act", lambda e, w=w, g8=g8, zb=zb: e.activation(out=szT[w][:, g8 * 8:(g8 + 1) * 8, :].rearrange("p c t -> p (c t)"), in_=bank(zb, 2), func=AF.Silu),
                          reads=zbr, writes=["szT" + wfx])
                P.add("pool", lambda e, w=w: e.tensor_tensor(out=ogT[w].rearrange("p k t -> p (k t)"), in0=ogT[w].rearrange("p k t -> p (k t)"),
                                                             in1=szT[w].rearrange("p c t -> p (c t)"), op=ALU.mult),
                      reads=["ogT" + wfx, "szT" + wfx], writes=["ogT" + wfx])

        def stage1a(t):
            u = t % NB
            sfx = "%d" % u
            r0 = t * 128
            w = t % NB0
            wfx = "w%d" % w
            P.add("sp", lambda e, u=u, r0=r0: e.dma_start(out=pt[u], in_=p_in[li, r0:r0 + 128, :]),
                  writes=["pt" + sfx], chan="pt" + sfx)
            yb, ybr = 0, ["pb0", "pb1"]
            for n in range(2):
                for k in range(16):
                    P.add("pe", lambda e, u=u, n=n, k=k, yb=yb: e.matmul(
                        bank(yb + n), lhsT=ogT[t % NB0][:, k, :], rhs=wout[:, k, n * 512:(n + 1) * 512],
                        start=(k == 0), stop=(k == 15)), reads=["ogT" + wfx, "wout"], writes=[ybr[n]])
            ctx[('y', t)] = (yb, ybr)

        def stage1b(t):
            u = t % NB
            sfx = "%d" % u
            r0 = t * 128
            yb, ybr = ctx[('y', t)]
            w = t % NB0
            wfx = "w%d" % w
            P.add("dve", lambda e, u=u, yb=yb, w=w: e.scalar_tensor_tensor(
                out=s_[u], in0=xt[w], scalar=float(ALPHA), in1=bank(yb, 2), op0=ALU.mult, op1=ALU.add),
                reads=["xt" + wfx] + ybr, writes=["s" + sfx])
            for c in range(2):
                P.add("dve", lambda e, u=u, c=c: e.bn_stats(out=stats[u][:, c * 6:(c + 1) * 6], in_=s_[u][:, c * 512:(c + 1) * 512]),
                      reads=["s" + sfx], writes=["st" + sfx])
            P.add("dve", lambda e, u=u: e.bn_aggr(out=stats[u][:, 12:14], in_=stats[u][:, 0:12]),
                  reads=["st" + sfx], writes=["st" + sfx])
            P.add("dve", lambda e, u=u: e.tensor_scalar(out=stats[u][:, 14:15], in0=stats[u][:, 13:14], scalar1=float(EPS), scalar2=None, op0=ALU.add),
                  reads=["st" + sfx], writes=["st" + sfx])
            P.add("act", lambda e, u=u: e.activation(out=stats[u][:, 14:15], in_=stats[u][:, 14:15], func=AF.Sqrt),
                  reads=["st" + sfx], writes=["st" + sfx])
            P.add("dve", lambda e, u=u: e.reciprocal(out=stats[u][:, 15:16], in_=stats[u][:, 14:15]),
                  reads=["st" + sfx], writes=["st" + sfx])
            P.add("dve", lambda e, u=u: e.tensor_scalar(out=xn[u], in0=s_[u], scalar1=stats[u][:, 12:13], scalar2=stats[u][:, 15:16],
                                                        op0=ALU.subtract, op1=ALU.mult),
                  reads=["s" + sfx, "st" + sfx], writes=["xn" + sfx])
            P.add("dve", lambda e, u=u: e.tensor_tensor(out=xn[u], in0=xn[u], in1=lng, op=ALU.mult),
                  reads=["xn" + sfx, "lng"], writes=["xn" + sfx])
            P.add("dve", lambda e, u=u: e.tensor_tensor(out=xn[u], in0=xn[u], in1=lnb, op=ALU.add),
                  reads=["xn" + sfx, "lnb"], writes=["xn" + sfx])
            P.add("act", lambda e, u=u: e.activation(out=xnb[u], in_=xn[u], func=AF.Copy),
                  reads=["xn" + sfx], writes=["xnb" + sfx])

        def stage2a(t):
            u = t % NB
            sfx = "%d" % u
            r0 = t * 128
            transpose_to(xnT[u], xnb[u], 8, "xnb" + sfx, "xnT" + sfx, evac="dve", banks=[6, 7])
            gb, gbr = 2, ["pb2", "pb3"]
            for n in range(2):
                for k in range(8):
                    P.add("pe", lambda e, u=u, n=n, k=k, gb=gb: e.matmul(
                        bank(gb + n), lhsT=xnT[u][:, k, :], rhs=wgate[:, k, n * 512:(n + 1) * 512],
                        start=(k == 0), stop=(k == 7)), reads=["xnT" + sfx, "wgate"], writes=[gbr[n]])
            P.add("act", lambda e, u=u, gb=gb: e.activation(out=sig[u], in_=bank(gb, 2), func=AF.Sigmoid),
                  reads=gbr, writes=["sig" + sfx])
            P.add("act", lambda e, u=u: e.activation(out=ptb[u], in_=pt[u], func=AF.Copy),
                  reads=["pt" + sfx], writes=["ptb" + sfx])
            transpose_to(pT[u], ptb[u], 2, "ptb" + sfx, "pT" + sfx, evac="dve", banks=[6, 7])
            lb, lbr = 4, ["pb4", "pb5"]
            for n in range(2):
                for k in range(2):
                    P.add("pe", lambda e, u=u, n=n, k=k, lb=lb: e.matmul(
                        bank(lb + n), lhsT=pT[u][:, k, :], rhs=wple[:, k, n * 512:(n + 1) * 512],
                        start=(k == 0), stop=(k == 1)), reads=["pT" + sfx, "wple"], writes=[lbr[n]])
            ctx[('l', t)] = (lb, lbr)

        def stage2b(t):
            u = t % NB
            sfx = "%d" % u
            r0 = t * 128
            lb, lbr = ctx[('l', t)]
            P.add("dve", lambda e, u=u, lb=lb: e.tensor_tensor(out=xo[u], in0=bank(lb, 2), in1=sig[u], op=ALU.mult),
                  reads=lbr + ["sig" + sfx], writes=["xo" + sfx])
            P.add("dve", lambda e, u=u: e.tensor_tensor(out=xo[u], in0=xo[u], in1=xn[u], op=ALU.add),
                  reads=["xo" + sfx, "xn" + sfx], writes=["xo" + sfx])
            P.add("poolq", lambda e, u=u, r0=r0: e.dma_start(out=x_dst[r0:r0 + 128, :], in_=xo[u]),
                  reads=["xo" + sfx], writes=["xdst%d" % t], chan="xo" + sfx)


        stage0(0)
        if NT > 1:
            stage0(1)
        stage1a(0)
        stage1b(0)
        for t in range(NT):
            if t + 2 < NT:
                stage0(t + 2)
            if t + 1 < NT:
                stage1a(t + 1)
            stage2a(t)
            if t + 1 < NT:
                stage1b(t + 1)
            stage2b(t)

    def gla_a(j, x_src):
        P.barrier()
        A.release(persist_mark)
        w_in = gla_w_in[j]
        wqk = A.alloc(8 * 2048, BF16).rearrange("p (k n) -> p k n", k=8)
        wvz = A.alloc(8 * 4096, BF16).rearrange("p (k n) -> p k n", k=8)
        wa = A.alloc(8 * 16, BF16).rearrange("p (k n) -> p k n", k=8)
        wa2 = A.alloc(1024, BF16)
        gng = A.alloc(2048, F32)
        S = A.alloc(4 * 2 * 512, F32).rearrange("p (h c n) -> p h c n", h=4, c=2)
        Sb = A.alloc(4 * 2 * 512, BF16).rearrange("p (h c n) -> p h c n", h=4, c=2)
        load_w(wqk, w_in, 2048, 8, "wqk", 0)
        load_w(wvz, w_in, 4096, 8, "wvz", 2048)
        load_w(wa, w_in, 16, 8, "wa", 6144)
        P.add("poolq", lambda e: e.dma_start(out=wa2[0:16, :], in_=gla_w_a2[j]), writes=["wa2"], chan="wa2")
        P.add("poolq", lambda e: e.dma_start(out=wa2[16:17, :], in_=gla_b_a[j].rearrange("(o n) -> o n", o=1)), writes=["wa2"], chan="wa2b")
        P.add("sp", lambda e: e.dma_start(out=gng, in_=gla_gn_g[j].partition_broadcast(128)), writes=["gng"], chan="gng")
        P.add("dve", lambda e: e.memset(S.rearrange("p h c n -> p (h c n)"), 0.0), writes=["S%d%d" % (h, ci) for h in range(4) for ci in range(2)])
        P.add("pool", lambda e: e.memset(Sb.rearrange("p h c n -> p (h c n)"), 0.0), writes=["Sb%d" % h for h in range(4)])
        NB = 2
        xt = [A.alloc(1024, F32) for _ in range(NB)]
        xb = [A.alloc(1024, BF16) for _ in range(NB)]
        xT = [A.alloc(1024, BF16).rearrange("p (k t) -> p k t", k=8) for _ in range(NB)]
        aT = A.alloc(128, BF16)
        sp_ = A.alloc(1024, F32)
        eG = A.alloc(1024, F32).rearrange("p (c t) -> p c t", c=8)
        enG = A.alloc(1024, F32).rearrange("p (c t) -> p c t", c=8)
        kdec = A.alloc(1024, F32)
        qt = A.alloc(1024, BF16).rearrange("p (c t) -> p c t", c=8)
        kt = A.alloc(1024, BF16).rearrange("p (c t) -> p c t", c=8)
        ks = A.alloc(1024, BF16)
        vb = A.alloc(2048, BF16)
        zg = A.alloc(2048, F32)
        ATb4 = A.alloc(512, BF16).rearrange("p (h t) -> p h t", h=4)
        sq4 = [A.alloc(512, F32) for _ in range(4)]
        og = [A.alloc(2048, BF16) for _ in range(NB)]
        nrm = A.alloc(8, F32)
        P.add("dve", lambda e: e.memset(aT[0:32, :], 1.0), writes=["aT"])
        def front(t):
            u = t % NB
            sfx = "%d" % u
            r0 = t * 128
            P.add("sp", lambda e, u=u, r0=r0: e.dma_start(out=xt[u], in_=x_src[r0:r0 + 128, :]),
                  reads=["xsrc%d" % t], writes=["xt" + sfx], chan="xt" + sfx)
            P.add("act", lambda e, u=u: e.activation(out=xb[u], in_=xt[u], func=AF.Copy),
                  reads=["xt" + sfx], writes=["xb" + sfx])
            transpose_to(xT[u], xb[u], 8, "xb" + sfx, "xT" + sfx, evac="dve")

        front(0)
        for t in range(NT):
            u = t % NB
            sfx = "%d" % u
            r0 = t * 128
            ab, abr = palloc(1)
            for k in range(8):
                P.add("pe", lambda e, u=u, k=k, ab=ab: e.matmul(bank(ab)[0:16, 0:128], lhsT=wa[:, k, :], rhs=xT[u][:, k, :],
                                                             start=(k == 0), stop=(k == 7)),
                      reads=["xT" + sfx, "wa"], writes=abr)
            P.add("dve", lambda e, ab=ab: e.tensor_copy(out=aT[0:16, :], in_=bank(ab)[0:16, 0:128]), reads=abr, writes=["aT"])
            lb, lbr = palloc(2)
            for n in range(2):
                P.add("pe", lambda e, n=n, lb=lb: e.matmul(bank(lb + n), lhsT=aT[0:17, :], rhs=wa2[0:17, n * 512:(n + 1) * 512],
                                                         start=True, stop=True), reads=["aT", "wa2"], writes=[lbr[n]])
            P.add("act", lambda e, lb=lb: e.activation(out=sp_, in_=bank(lb, 2), func=AF.Exp, scale=-1.0), reads=lbr, writes=["sp"])
            P.add("act", lambda e: e.activation(out=sp_, in_=sp_, func=AF.Ln, bias=1.0), reads=["sp"], writes=["sp"])
            for half in range(2):
                vb_, vbr = palloc(2)
                for n in range(2):
                    for k in range(8):
                        P.add("pe", lambda e, u=u, n=n, k=k, vb_=vb_, half=half: e.matmul(
                            bank(vb_ + n), lhsT=xT[u][:, k, :], rhs=wvz[:, k, half * 1024 + n * 512:half * 1024 + (n + 1) * 512],
                            start=(k == 0), stop=(k == 7)), reads=["xT" + sfx, "wvz"], writes=[vbr[n]])
                P.add("act", lambda e, vb_=vb_, half=half: e.activation(out=vb[:, half * 1024:(half + 1) * 1024], in_=bank(vb_, 2), func=AF.Copy),
                      reads=vbr, writes=["vb%d" % half])
            for half in range(2):
                zb, zbr = palloc(2)
                for n in range(2):
                    for k in range(8):
                        P.add("pe", lambda e, u=u, n=n, k=k, zb=zb, half=half: e.matmul(
                            bank(zb + n), lhsT=xT[u][:, k, :], rhs=wvz[:, k, 2048 + half * 1024 + n * 512:2048 + half * 1024 + (n + 1) * 512],
                            start=(k == 0), stop=(k == 7)), reads=["xT" + sfx, "wvz"], writes=[zbr[n]])
                P.add("act", lambda e, zb=zb, half=half: e.activation(out=zg[:, half * 1024:(half + 1) * 1024], in_=bank(zb, 2), func=AF.Silu),
                      reads=zbr, writes=["zg%d" % half])
                P.add("pool", lambda e, half=half: e.tensor_tensor(out=zg[:, half * 1024:(half + 1) * 1024], in0=zg[:, half * 1024:(half + 1) * 1024],
                                                                 in1=gng[:, half * 1024:(half + 1) * 1024], op=ALU.mult),
                      reads=["zg%d" % half, "gng"], writes=["zg%d" % half])
            gb, gbr = palloc(2)
            for c in range(8):
                P.add("pe", lambda e, c=c, gb=gb: e.matmul(bank(gb, 2)[:, c * 128:(c + 1) * 128], lhsT=sp_[:, c * 128:(c + 1) * 128], rhs=umat,
                                                         start=True, stop=True), reads=["sp", "umat"], writes=[gbr[c // 4]])
            P.add("act", lambda e, gb=gb: e.activation(out=eG.rearrange("p c t -> p (c t)"), in_=bank(gb, 2), func=AF.Exp), reads=gbr, writes=["eG"])
            P.add("act", lambda e, gb=gb: e.activation(out=enG.rearrange("p c t -> p (c t)"), in_=bank(gb, 2), func=AF.Exp, scale=-1.0), reads=gbr, writes=["enG"])
            kb, kbr = palloc(2)
            for n in range(2):
                P.add("pe", lambda e, n=n, kb=kb: e.matmul(bank(kb + n), lhsT=lmat, rhs=sp_[:, n * 512:(n + 1) * 512], start=True, stop=True),
                      reads=["sp", "lmat"], writes=[kbr[n]])
            P.add("act", lambda e, kb=kb: e.activation(out=kdec, in_=bank(kb, 2), func=AF.Exp), reads=kbr, writes=["kdec"])
            for which, dst, gate, sc in (("q", qt, eG, 1.0 / 16.0), ("k", kt, enG, 1.0)):
                qb, qbr = palloc(2)
                c0 = 0 if which == "q" else 1024
                for c in range(8):
                    for k in range(8):
                        P.add("pe", lambda e, u=u, c=c, k=k, qb=qb, c0=c0: e.matmul(
                            bank(qb, 2)[:, c * 128:(c + 1) * 128], lhsT=wqk[:, k, c0 + c * 128:c0 + (c + 1) * 128], rhs=xT[u][:, k, :],
                            start=(k == 0), stop=(k == 7)), reads=["xT" + sfx, "wqk"], writes=[qbr[c // 4]])
                P.add("dve", lambda e, qb=qb, dst=dst, gate=gate, sc=sc: e.scalar_tensor_tensor(
                    out=dst.rearrange("p c t -> p (c t)"), in0=bank(qb, 2), scalar=float(sc), in1=gate.rearrange("p c t -> p (c t)"),
                    op0=ALU.mult, op1=ALU.mult), reads=qbr + ["eG" if which == "q" else "enG"], writes=[which + "t"])
            tb, tbr = palloc(2)
            for n in range(2):
                for k in range(8):
                    P.add("pe", lambda e, u=u, n=n, k=k, tb=tb: e.matmul(
                        bank(tb + n), lhsT=xT[u][:, k, :], rhs=wqk[:, k, 1024 + n * 512:1024 + (n + 1) * 512],
                        start=(k == 0), stop=(k == 7)), reads=["xT" + sfx, "wqk"], writes=[tbr[n]])
            P.add("dve", lambda e, tb=tb: e.tensor_tensor(out=ks, in0=bank(tb, 2), in1=kdec, op=ALU.mult), reads=tbr + ["kdec"], writes=["ks"])
            P.add("dve", lambda e: e.memset(nrm, 0.0), writes=["nrm"])
            if t + 1 < NT:
                front(t + 1)
            for h in range(4):
                for ci in range(2):
                    c = 2 * h + ci
                    P.add("pe", lambda e, c=c, ci=ci, h=h: e.matmul(bank(h)[:, 0:128], lhsT=kt[:, c, :], rhs=qt[:, c, :],
                                                                 start=(ci == 0), stop=(ci == 1)), reads=["kt", "qt"], writes=["pb%d" % h])
                P.add("dve", lambda e, h=h: e.tensor_tensor(out=ATb4[:, h, :], in0=bank(h)[:, 0:128], in1=tri, op=ALU.mult),
                      reads=["pb%d" % h, "tri"], writes=["ATb%d" % h])
            for h in range(4):
                vh = vb[:, h * 512:(h + 1) * 512]
                vres = "vb%d" % (h // 2)
                ob = 4 + h
                obr = ["pb%d" % ob]
                P.add("pe", lambda e, ob=ob, vh=vh, h=h: e.matmul(bank(ob), lhsT=ATb4[:, h, :], rhs=vh, start=True, stop=False),
                      reads=["ATb%d" % h, vres], writes=obr)
                for ci in range(2):
                    c = 2 * h + ci
                    P.add("pe", lambda e, ob=ob, c=c, ci=ci, h=h: e.matmul(bank(ob), lhsT=qt[:, c, :], rhs=Sb[:, h, ci, :], start=False, stop=(ci == 1)),
                          reads=["qt", "Sb%d" % h], writes=obr)
                for ci in range(2):
                    c = 2 * h + ci
                    sb_ = 2 * (h % 2) + ci
                    sbr = ["pb%d" % sb_]
                    P.add("pe", lambda e, sb_=sb_, c=c, vh=vh: e.matmul(bank(sb_), lhsT=ks[:, c * 128:(c + 1) * 128], rhs=vh, start=True, stop=True),
                          reads=["ks", vres], writes=sbr)
                    P.add("dve", lambda e, sb_=sb_, c=c, ci=ci, h=h: e.scalar_tensor_tensor(
                        out=S[:, h, ci, :], in0=S[:, h, ci, :], scalar=eG[:, c, 127:128], in1=bank(sb_), op0=ALU.mult, op1=ALU.add),
                        reads=["S%d%d" % (h, ci), "eG"] + sbr, writes=["S%d%d" % (h, ci)])
            for h in range(4):
                P.add("act", lambda e, h=h: e.activation(out=sq4[h], in_=bank(4 + h), func=AF.Square, accum_out=nrm[:, h:h + 1]),
                      reads=["pb%d" % (4 + h)], writes=["sq%d" % h, "nrm"])
            P.add("dve", lambda e: e.tensor_scalar(out=nrm[:, 4:8], in0=nrm[:, 0:4], scalar1=1.0 / 512, scalar2=float(EPS), op0=ALU.mult, op1=ALU.add),
                  reads=["nrm"], writes=["nrm"])
            P.add("act", lambda e: e.activation(out=nrm[:, 4:8], in_=nrm[:, 4:8], func=AF.Sqrt), reads=["nrm"], writes=["nrm"])
            P.add("dve", lambda e: e.reciprocal(out=nrm[:, 4:8], in_=nrm[:, 4:8]), reads=["nrm"], writes=["nrm"])
            for h in range(4):
                P.add("dve", lambda e, h=h, u=u: e.scalar_tensor_tensor(
                    out=og[u][:, h * 512:(h + 1) * 512], in0=bank(4 + h), scalar=nrm[:, 4 + h:5 + h], in1=zg[:, h * 512:(h + 1) * 512],
                    op0=ALU.mult, op1=ALU.mult), reads=["pb%d" % (4 + h), "nrm", "zg%d" % (h // 2)], writes=["og" + sfx])
            for h in range(4):
                for ci in range(2):
                    P.add("pool", lambda e, h=h, ci=ci: e.tensor_copy(out=Sb[:, h, ci, :], in_=S[:, h, ci, :]),
                          reads=["S%d%d" % (h, ci)], writes=["Sb%d" % h])
            pstate["ptr"] = 0
            P.add("poolq", lambda e, u=u, r0=r0: e.dma_start(out=ogs[r0:r0 + 128, :], in_=og[u]),
                  reads=["og" + sfx], writes=["ogs"], chan="ogst" + sfx)

    def conv_a(x_src):
        P.barrier()
        A.release(persist_mark)
        TG = 256
        NG = L // TG
        w_in = conv_w_in[0]
        wc = A.alloc(8 * 6144, BF16).rearrange("p (k n) -> p k n", k=8)
        load_w(wc, w_in, 6144, 8, "wc", 0)
        prm_tm = A.alloc(2048, F32)
        P.add("sp", lambda e: e.dma_start(out=prm_tm[0:31, :], in_=conv_dw_w[0]), writes=["prm_tm"], chan="pr0")
        P.add("sp", lambda e: e.dma_start(out=prm_tm[31:32, :], in_=conv_dw_b[0].rearrange("(o n) -> o n", o=1)), writes=["prm_tm"], chan="pr1")
        P.add("sp", lambda e: e.dma_start(out=prm_tm[32:33, :], in_=conv_ln_g[0].rearrange("(o n) -> o n", o=1)), writes=["prm_tm"], chan="pr2")
        P.add("sp", lambda e: e.dma_start(out=prm_tm[33:34, :], in_=conv_ln_b[0].rearrange("(o n) -> o n", o=1)), writes=["prm_tm"], chan="pr3")
        prm = A.alloc(16 * 34, F32).rearrange("p (c k) -> p c k", c=16)
        for c in range(16):
            b, br = palloc(1)
            P.add("pe", lambda e, c=c, b=b: e.transpose(out=bank(b)[:, 0:34], in_=prm_tm[0:34, c * 128:(c + 1) * 128], identity=ident_f[0:34, 0:34]),
                  reads=["prm_tm", "ident_f"], writes=br)
            P.add("dve", lambda e, c=c, b=b: e.tensor_copy(out=prm[:, c, :], in_=bank(b)[:, 0:34]), reads=br, writes=["prm"])
        ones_f = A.alloc(128, F32)
        P.add("dve", lambda e: e.memset(ones_f, 1.0 / 2048.0), writes=["ones_f"])
        xt = [A.alloc(1024, F32) for _ in range(2)]
        xb = [A.alloc(1024, BF16) for _ in range(2)]
        xTg = A.alloc(8 * TG, BF16).rearrange("p (k t) -> p k t", k=8)
        uT = A.alloc(16 * (TG + 30), BF16).rearrange("p (c t) -> p c t", c=16)
        dg = [A.alloc(128, BF16) for _ in range(9)]
        szT = A.alloc(16 * TG, BF16).rearrange("p (c t) -> p c t", c=16)
        cT = A.alloc(16 * TG, F32).rearrange("p (c t) -> p c t", c=16)
        sgt = [A.alloc(TG, F32) for _ in range(2)]
        sgz = [A.alloc(TG, F32) for _ in range(2)]
        sqt = [A.alloc(TG, F32) for _ in range(2)]
        mean_t = A.alloc(TG, F32)
        rstd_t = A.alloc(TG, F32)
        t1 = [A.alloc(TG, F32) for _ in range(2)]
        t2 = [A.alloc(TG, F32) for _ in range(2)]
        ogTt = A.alloc(16 * TG, BF16).rearrange("p (c t) -> p c t", c=16)
        P.add("dve", lambda e: e.memset(uT.rearrange("p c t -> p (c t)"), 0.0), writes=["uT%d" % c for c in range(16)])
        for g in range(NG):
            for i in range(TG // 128):
                t = g * (TG // 128) + i
                u = t % 2
                sfx = "%d" % u
                r0 = t * 128
                P.add("sp", lambda e, u=u, r0=r0: e.dma_start(out=xt[u], in_=x_src[r0:r0 + 128, :]), writes=["xt" + sfx], chan="xt" + sfx)
                P.add("act", lambda e, u=u: e.activation(out=xb[u], in_=xt[u], func=AF.Copy), reads=["xt" + sfx], writes=["xb" + sfx])
                transpose_to(xTg[:, :, i * 128:(i + 1) * 128], xb[u], 8, "xb" + sfx, "xTg", evac="dve")
            for c in range(16):
                v = c % 2
                pa, par = palloc(1)
                pg, pgr = palloc(1)
                pz, pzr = palloc(1)
                for (pb_, pbr_, off) in ((pa, par, 0), (pg, pgr, 2048), (pz, pzr, 4096)):
                    for k in range(8):
                        P.add("pe", lambda e, pb_=pb_, off=off, c=c, k=k: e.matmul(
                            bank(pb_)[:, 0:TG], lhsT=wc[:, k, off + c * 128:off + (c + 1) * 128], rhs=xTg[:, k, :],
                            start=(k == 0), stop=(k == 7)), reads=["xTg", "wc"], writes=pbr_)
                P.add("act", lambda e, pg=pg, v=v: e.activation(out=sgt[v], in_=bank(pg)[:, 0:TG], func=AF.Sigmoid), reads=pgr, writes=["sgt%d" % v])
                if g > 0:
                    P.add("pool", lambda e, c=c: e.tensor_copy(out=uT[:, c, 0:30], in_=uT[:, c, TG:TG + 30]), reads=["uT%d" % c], writes=["uT%d" % c])
                P.add("dve", lambda e, pa=pa, v=v, c=c: e.tensor_tensor(out=uT[:, c, 30:30 + TG], in0=bank(pa)[:, 0:TG], in1=sgt[v], op=ALU.mult),
                      reads=par + ["sgt%d" % v, "uT%d" % c], writes=["uT%d" % c])
                P.add("act", lambda e, pz=pz, v=v: e.activation(out=sgz[v], in_=bank(pz)[:, 0:TG], func=AF.Sigmoid), reads=pzr, writes=["sgz%d" % v])
                P.add("dve", lambda e, pz=pz, v=v, c=c: e.tensor_tensor(out=szT[:, c, :], in0=bank(pz)[:, 0:TG], in1=sgz[v], op=ALU.mult),
                      reads=pzr + ["sgz%d" % v], writes=["szT%d" % c])
            for c in range(16):
                cb, cbr = palloc(1)
                for k in range(CONV_W):
                    n_ = c * CONV_W + k
                    di = n_ % len(dg)
                    eng = ("dve", "pool")[n_ % 2]
                    if eng == "dve":
                        P.add("dve", lambda e, c=c, k=k, di=di: e.tensor_scalar(out=dg[di], in0=ident_b, scalar1=prm[:, c, k:k + 1], scalar2=None, op0=ALU.mult),
                              reads=["ident_b", "prm"], writes=["dg%d" % di])
                    elif eng == "pool":
                        P.add("pool", lambda e, c=c, k=k, di=di: e.tensor_scalar(out=dg[di], in0=ident_b, scalar1=prm[:, c, k:k + 1], scalar2=1.0, op0=ALU.mult, op1=ALU.mult),
                              reads=["ident_b", "prm"], writes=["dg%d" % di])
                    else:
                        P.add("act", lambda e, c=c, k=k, di=di: e.activation(out=dg[di], in_=ident_b, func=AF.Copy, scale=prm[:, c, k:k + 1]),
                              reads=["ident_b", "prm"], writes=["dg%d" % di])
                    P.add("pe", lambda e, c=c, k=k, di=di, cb=cb: e.matmul(bank(cb)[:, 0:TG], lhsT=dg[di], rhs=uT[:, c, k:k + TG], start=(k == 0), stop=(k == CONV_W - 1)),
                          reads=["dg%d" % di, "uT%d" % c], writes=cbr)
                P.add("act", lambda e, c=c, cb=cb: e.activation(out=cT[:, c, :], in_=bank(cb)[:, 0:TG], func=AF.Identity, bias=prm[:, c, 31:32]),
                      reads=cbr + ["prm"], writes=["cT%d" % c])
            ps, psr = palloc(1)
            pq, pqr = palloc(1)
            for c in range(16):
                v = c % 2
                P.add("pe", lambda e, ps=ps, c=c: e.matmul(bank(ps)[:, 0:TG], lhsT=ones_f, rhs=cT[:, c, :], start=(c == 0), stop=(c == 15)),
                      reads=["ones_f", "cT%d" % c], writes=psr)
                P.add("act", lambda e, c=c, v=v: e.activation(out=sqt[v], in_=cT[:, c, :], func=AF.Square), reads=["cT%d" % c], writes=["sqt%d" % v])
                P.add("pe", lambda e, pq=pq, c=c, v=v: e.matmul(bank(pq)[:, 0:TG], lhsT=ones_f, rhs=sqt[v], start=(c == 0), stop=(c == 15)),
                      reads=["ones_f", "sqt%d" % v], writes=pqr)
            P.add("act", lambda e, ps=ps: e.activation(out=mean_t, in_=bank(ps)[:, 0:TG], func=AF.Copy), reads=psr, writes=["mean_t"])
            P.add("dve", lambda e: e.tensor_tensor(out=rstd_t, in0=mean_t, in1=mean_t, op=ALU.mult), reads=["mean_t"], writes=["rstd_t"])
            P.add("dve", lambda e, pq=pq: e.tensor_tensor(out=rstd_t, in0=bank(pq)[:, 0:TG], in1=rstd_t, op=ALU.subtract), reads=pqr + ["rstd_t"], writes=["rstd_t"])
            P.add("dve", lambda e: e.tensor_scalar(out=rstd_t, in0=rstd_t, scalar1=float(EPS), scalar2=None, op0=ALU.add), reads=["rstd_t"], writes=["rstd_t"])
            P.add("act", lambda e: e.activation(out=rstd_t, in_=rstd_t, func=AF.Sqrt), reads=["rstd_t"], writes=["rstd_t"])
            P.add("dve", lambda e: e.reciprocal(out=rstd_t, in_=rstd_t), reads=["rstd_t"], writes=["rstd_t"])
            for c in range(16):
                v = c % 2
                P.add("dve", lambda e, c=c, v=v: e.tensor_tensor(out=t1[v], in0=cT[:, c, :], in1=mean_t, op=ALU.subtract), reads=["cT%d" % c, "mean_t"], writes=["t1%d" % v])
                P.add("dve", lambda e, c=c, v=v: e.tensor_tensor(out=t1[v], in0=t1[v], in1=rstd_t, op=ALU.mult), reads=["t1%d" % v, "rstd_t"], writes=["t1%d" % v])
                P.add("act", lambda e, c=c, v=v: e.activation(out=t2[v], in_=t1[v], func=AF.Silu, scale=prm[:, c, 32:33], bias=prm[:, c, 33:34]),
                      reads=["t1%d" % v, "prm"], writes=["t2%d" % v])
                P.add("pool", lambda e, c=c, v=v: e.tensor_tensor(out=ogTt[:, c, :], in0=t2[v], in1=szT[:, c, :], op=ALU.mult),
                      reads=["t2%d" % v, "szT%d" % c], writes=["ogTt"])
            P.add("poolq", lambda e, g=g: e.dma_start(out=ogTs[:, g * TG:(g + 1) * TG].rearrange("(c f) t -> f c t", f=128), in_=ogTt),
                  reads=["ogTt"], writes=["ogTs"], chan="ogTst")

    def dsa_a(x_src):
        P.barrier()
        A.release(persist_mark)
        NIT = 18
        NOSAME = True
        PI = math.pi
        w_in = dsa_w_in[0]
        WC = 2888
        W = A.alloc(8 * WC, BF16).rearrange("p (k n) -> p k n", k=8)
        Wr = A.alloc(8 * 688, BF16).rearrange("p (k n) -> p k n", k=8)
        load_w(W[:, :, 0:2304], w_in, 2304, 8, "W", 0)
        load_w(W[:, :, 2304:WC], w_in, 584, 8, "W", 4352)
        load_w(Wr, dsa_w_rot, 688, 8, "Wr", 0)
        ropec = A.alloc(8, F32)
        negm = A.alloc(128, F32)
        pw2 = A.alloc(NIT, F32)
        sel = A.alloc(512, F32).rearrange("p (g m) -> p g m", g=4)
        ones_b = A.alloc(32, BF16)
        P.add("sp", lambda e: e.dma_start(out=ropec[0:32, :], in_=c_rope), writes=["ropec"], chan="cr0")
        P.add("sp", lambda e: e.dma_start(out=negm, in_=c_neg), writes=["negm"], chan="cr1")
        P.add("sp", lambda e: e.dma_start(out=pw2, in_=c_pw2), writes=["pw2"], chan="cr2")
        P.add("sp", lambda e: e.dma_start(out=sel.rearrange("p g m -> p (g m)"), in_=c_sel), writes=["sel"], chan="cr3")
        P.add("dve", lambda e: e.memset(ones_b, 1.0), writes=["ones_b"])
        kT = A.alloc(L, BF16)
        vtok = A.alloc(NT * 128, BF16).rearrange("p (b d) -> p b d", b=NT)
        kiT = A.alloc(L, BF16)
        xt = A.alloc(1024, F32)
        xb = A.alloc(1024, BF16)
        xT = [A.alloc(1024, BF16).rearrange("p (k t) -> p k t", k=8) for _ in range(2)]
        posi = A.alloc(128, I32)
        posf = A.alloc(128, F32)
        ang = A.alloc(128, F32)
        arg = A.alloc(128, F32)
        argi = A.alloc(128, I32)
        argk = A.alloc(128, F32)
        C32 = A.alloc(128, F32)
        S32 = A.alloc(128, F32)
        C16 = A.alloc(128, F32)
        S16 = A.alloc(128, F32)
        tA = A.alloc(1024, F32)
        tB = A.alloc(1024, F32)
        qT = [A.alloc(2048, BF16).rearrange("p (h t) -> p h t", h=16) for _ in range(2)]
        qiT = A.alloc(1024, BF16).rearrange("p (h t) -> p h t", h=8)
        wi = A.alloc(8, F32)
        acc = A.alloc(L, F32)
        maskb = A.alloc(L, BF16)
        maskT = [A.alloc(NT * 128, BF16).rearrange("p (b t) -> p b t", b=NT) for _ in range(2)]
        rbuf = [A.alloc(512, F32) for _ in range(2)]
        probs = [A.alloc(2048, BF16) for _ in range(2)]
        den_sb = A.alloc(512, F32)
        oS = A.alloc(2048, F32)
        ogTt = A.alloc(2048, BF16).rearrange("p (c t) -> p c t", c=16)
        sm = A.alloc(8 + 2 * NIT, F32)
        lo, mid, cnt, tmp, W0 = (sm[:, i:i + 1] for i in range(5))
        wcols = sm[:, 8:8 + NIT]
        nhw = sm[:, 8 + NIT:8 + 2 * NIT]

        def bc(ap32, n):
            return ap32.unsqueeze(1).to_broadcast([ap32.shape[0], n, 128])

        def proj_fm(dst_bank_ap, col0, ncol, u, wres, Wt, brs):
            for k in range(8):
                P.add("pe", lambda e, k=k: e.matmul(dst_bank_ap, lhsT=Wt[:, k, col0:col0 + ncol], rhs=xT[u][:, k, :],
                                                     start=(k == 0), stop=(k == 7)), reads=["xT%d" % u, wres], writes=brs)

        def stP(t):
            u = t % 2
            sfx = "%d" % u
            r0 = t * 128
            qTu = qT[u]
            qres = "qT%d" % u
            P.add("sp", lambda e: e.dma_start(out=xt, in_=x_src[r0:r0 + 128, :]), writes=["xt"], chan="xt")
            P.add("sp", lambda e: e.dma_start(out=posi[0:32, :], in_=pos_in[r0:r0 + 128].partition_broadcast(32)), writes=["posi"], chan="posi")
            P.add("act", lambda e: e.activation(out=xb, in_=xt, func=AF.Copy), reads=["xt"], writes=["xb"])
            transpose_to(xT[u], xb, 8, "xb", "xT" + sfx, evac="dve")
            P.add("dve", lambda e: e.tensor_copy(out=posf[0:32, :], in_=posi[0:32, :]), reads=["posi"], writes=["posf"])
            for (n, ci, Ct, St_, nm) in ((32, 0, C32, S32, "h"), (16, 3, C16, S16, "i")):
                P.add("dve", lambda e, n=n, ci=ci: e.tensor_scalar(out=ang[0:n, :], in0=posf[0:n, :], scalar1=ropec[0:n, ci:ci + 1], scalar2=None, op0=ALU.mult),
                      reads=["posf", "ropec"], writes=["ang"])
                for (off, dst, scale_, wn) in ((0.0, St_, ropec[0:n, ci + 1:ci + 2], "S" + nm), (0.25, Ct, 2 * PI, "C" + nm)):
                    P.add("dve", lambda e, n=n, off=off: e.tensor_scalar(out=arg[0:n, :], in0=ang[0:n, :], scalar1=float(off), scalar2=None, op0=ALU.add),
                          reads=["ang"], writes=["arg"])
                    P.add("dve", lambda e, n=n: e.tensor_copy(out=argi[0:n, :], in_=arg[0:n, :]), reads=["arg"], writes=["argi"])
                    P.add("dve", lambda e, n=n: e.tensor_copy(out=argk[0:n, :], in_=argi[0:n, :]), reads=["argi"], writes=["argk"])
                    P.add("dve", lambda e, n=n: e.tensor_tensor(out=arg[0:n, :], in0=arg[0:n, :], in1=argk[0:n, :], op=ALU.subtract), reads=["arg", "argk"], writes=["arg"])
                    P.add("dve", lambda e, n=n: e.tensor_scalar(out=argk[0:n, :], in0=arg[0:n, :], scalar1=0.5, scalar2=None, op0=ALU.is_gt), reads=["arg"], writes=["argk"])
                    P.add("dve", lambda e, n=n: e.tensor_tensor(out=arg[0:n, :], in0=arg[0:n, :], in1=argk[0:n, :], op=ALU.subtract), reads=["arg", "argk"], writes=["arg"])
                    P.add("dve", lambda e, n=n: e.tensor_scalar(out=argk[0:n, :], in0=arg[0:n, :], scalar1=-0.5, scalar2=None, op0=ALU.is_lt), reads=["arg"], writes=["argk"])
                    P.add("dve", lambda e, n=n: e.tensor_tensor(out=arg[0:n, :], in0=arg[0:n, :], in1=argk[0:n, :], op=ALU.add), reads=["arg", "argk"], writes=["arg"])
                    P.add("act", lambda e, n=n, dst=dst, scale_=scale_: e.activation(out=dst[0:n, :], in_=arg[0:n, :], func=AF.Sin, scale=scale_),
                          reads=["arg", "ropec"], writes=[wn])
            for g8 in range(2):
                qb, qbr = palloc(2)
                for hh in range(8):
                    h = g8 * 8 + hh
                    proj_fm(bank(qb, 2)[:, hh * 128:(hh + 1) * 128], h * 128, 128, u, "W", W, [qbr[hh // 4]])
                rb, rbr = palloc(2)
                for hh in range(8):
                    h = g8 * 8 + hh
                    proj_fm(bank(rb, 2)[0:32, hh * 128:(hh + 1) * 128], h * 32, 32, u, "Wr", Wr, [rbr[hh // 4]])
                P.add("act", lambda e, g8=g8, qb=qb: e.activation(out=qTu[:, g8 * 8:(g8 + 1) * 8, :].rearrange("p h t -> p (h t)"), in_=bank(qb, 2), func=AF.Copy),
                      reads=qbr, writes=[qres])
                P.add("dve", lambda e, qb=qb: e.tensor_tensor(out=tA[0:32, :].rearrange("p (h t) -> p h t", h=8), in0=bank(qb, 2)[0:32, :].rearrange("p (h t) -> p h t", h=8),
                                                             in1=bc(C32[0:32, :], 8), op=ALU.mult), reads=qbr + ["Ch", qres], writes=["tA"])
                P.add("dve", lambda e, rb=rb: e.tensor_tensor(out=tB[0:32, :].rearrange("p (h t) -> p h t", h=8), in0=bank(rb, 2)[0:32, :].rearrange("p (h t) -> p h t", h=8),
                                                             in1=bc(S32[0:32, :], 8), op=ALU.mult), reads=rbr + ["Sh"], writes=["tB"])
                P.add("dve", lambda e, g8=g8: e.tensor_tensor(out=qTu[0:32, g8 * 8:(g8 + 1) * 8, :].rearrange("p h t -> p (h t)"), in0=tA[0:32, :], in1=tB[0:32, :], op=ALU.add),
                      reads=["tA", "tB"], writes=[qres])
            kb_, kbr_ = palloc(1)
            proj_fm(bank(kb_)[:, 0:128], 2048, 128, u, "W", W, kbr_)
            kr, krr = palloc(1)
            proj_fm(bank(kr)[0:32, 0:128], 512, 32, u, "Wr", Wr, krr)
            P.add("act", lambda e: e.activation(out=kT[:, r0:r0 + 128], in_=bank(kb_)[:, 0:128], func=AF.Copy), reads=kbr_, writes=["kT"])
            P.add("dve", lambda e: e.tensor_tensor(out=tA[0:32, 0:128], in0=bank(kb_)[0:32, 0:128], in1=C32[0:32, :], op=ALU.mult), reads=kbr_ + ["Ch"], writes=["tA"])
            P.add("dve", lambda e: e.tensor_tensor(out=tB[0:32, 0:128], in0=bank(kr)[0:32, 0:128], in1=S32[0:32, :], op=ALU.mult), reads=krr + ["Sh"], writes=["tB"])
            P.add("dve", lambda e: e.tensor_tensor(out=kT[0:32, r0:r0 + 128], in0=tA[0:32, 0:128], in1=tB[0:32, 0:128], op=ALU.add), reads=["tA", "tB"], writes=["kT"])
            vb_, vbr_ = palloc(1)
            for k in range(8):
                P.add("pe", lambda e, k=k: e.matmul(bank(vb_)[:, 0:128], lhsT=xT[u][:, k, :], rhs=W[:, k, 2176:2304], start=(k == 0), stop=(k == 7)),
                      reads=["xT" + sfx, "W"], writes=vbr_)
            P.add("act", lambda e: e.activation(out=vtok[:, t, :], in_=bank(vb_)[:, 0:128], func=AF.Copy), reads=vbr_, writes=["vtok"])
            wb_, wbr_ = palloc(1)
            for k in range(8):
                P.add("pe", lambda e, k=k: e.matmul(bank(wb_)[:, 0:8], lhsT=xT[u][:, k, :], rhs=W[:, k, 2880:2888], start=(k == 0), stop=(k == 7)),
                      reads=["xT" + sfx, "W"], writes=wbr_)
            P.add("dve", lambda e: e.tensor_copy(out=wi, in_=bank(wb_)[:, 0:8]), reads=wbr_, writes=["wi"])
            ib, ibr = palloc(2)
            for h in range(8):
                proj_fm(bank(ib, 2)[0:64, h * 128:(h + 1) * 128], 2304 + h * 64, 64, u, "W", W, [ibr[h // 4]])
            jb, jbr = palloc(2)
            for h in range(8):
                proj_fm(bank(jb, 2)[0:16, h * 128:(h + 1) * 128], 544 + h * 16, 16, u, "Wr", Wr, [jbr[h // 4]])
            P.add("act", lambda e: e.activation(out=qiT[0:64, :, :].rearrange("p h t -> p (h t)"), in_=bank(ib, 2)[0:64, :], func=AF.Copy), reads=ibr, writes=["qiT"])
            P.add("dve", lambda e: e.tensor_tensor(out=tA[0:16, :].rearrange("p (h t) -> p h t", h=8), in0=bank(ib, 2)[0:16, :].rearrange("p (h t) -> p h t", h=8),
                                                  in1=bc(C16[0:16, :], 8), op=ALU.mult), reads=ibr + ["Ci"], writes=["tA"])
            P.add("dve", lambda e: e.tensor_tensor(out=tB[0:16, :].rearrange("p (h t) -> p h t", h=8), in0=bank(jb, 2)[0:16, :].rearrange("p (h t) -> p h t", h=8),
                                                  in1=bc(S16[0:16, :], 8), op=ALU.mult), reads=jbr + ["Si"], writes=["tB"])
            P.add("dve", lambda e: e.tensor_tensor(out=qiT[0:16, :, :].rearrange("p h t -> p (h t)"), in0=tA[0:16, :], in1=tB[0:16, :], op=ALU.add),
                  reads=["tA", "tB"], writes=["qiT"])
            cb, cbr = palloc(1)
            proj_fm(bank(cb)[0:64, 0:128], 2816, 64, u, "W", W, cbr)
            db, dbr = palloc(1)
            proj_fm(bank(db)[0:16, 0:128], 672, 16, u, "Wr", Wr, dbr)
            P.add("act", lambda e: e.activation(out=kiT[0:64, r0:r0 + 128], in_=bank(cb)[0:64, 0:128], func=AF.Copy), reads=cbr, writes=["kiT"])
            P.add("dve", lambda e: e.tensor_tensor(out=tA[0:16, 0:128], in0=bank(cb)[0:16, 0:128], in1=C16[0:16, :], op=ALU.mult), reads=cbr + ["Ci"], writes=["tA"])
            P.add("dve", lambda e: e.tensor_tensor(out=tB[0:16, 0:128], in0=bank(db)[0:16, 0:128], in1=S16[0:16, :], op=ALU.mult), reads=dbr + ["Si"], writes=["tB"])
            P.add("dve", lambda e: e.tensor_tensor(out=kiT[0:16, r0:r0 + 128], in0=tA[0:16, 0:128], in1=tB[0:16, 0:128], op=ALU.add), reads=["tA", "tB"], writes=["kiT"])

        def stI(t):
            r0 = t * 128
            St = (t + 1) * 128
            for s0 in range(0, St, 512):
                n = min(512, St - s0)
                for h in range(8):
                    v = h % 2
                    sb_, sbr = palloc(1)
                    P.add("pe", lambda e, sb_=sb_, h=h, s0=s0, n=n: e.matmul(bank(sb_)[:, 0:n], lhsT=qiT[0:64, h, :], rhs=kiT[0:64, s0:s0 + n], start=True, stop=True),
                          reads=["qiT", "kiT"], writes=sbr)
                    P.add("act", lambda e, sb_=sb_, v=v, n=n: e.activation(out=rbuf[v][:, 0:n], in_=bank(sb_)[:, 0:n], func=AF.Relu), reads=sbr, writes=["rbuf%d" % v])
                    if h == 0:
                        P.add("dve", lambda e, v=v, s0=s0, n=n: e.tensor_scalar(out=acc[:, s0:s0 + n], in0=rbuf[v][:, 0:n], scalar1=wi[:, 0:1], scalar2=None, op0=ALU.mult),
                              reads=["rbuf%d" % v, "wi"], writes=["acc"])
                    else:
                        P.add("dve", lambda e, v=v, s0=s0, n=n, h=h: e.scalar_tensor_tensor(out=acc[:, s0:s0 + n], in0=rbuf[v][:, 0:n], scalar=wi[:, h:h + 1], in1=acc[:, s0:s0 + n],
                                                                                      op0=ALU.mult, op1=ALU.add), reads=["rbuf%d" % v, "wi", "acc"], writes=["acc"])
            if t >= 2:
                P.add("dve", lambda e: e.tensor_reduce(out=W0, in_=acc[:, 0:St], axis=mybir.AxisListType.X, op=ALU.max, apply_absolute_value=True),
                      reads=["acc"], writes=["sm"])
            P.add("pool", lambda e: e.tensor_tensor(out=acc[:, r0:r0 + 128], in0=acc[:, r0:r0 + 128], in1=negm, op=ALU.add), reads=["acc", "negm"], writes=["acc"])

        def stT(t):
            St = (t + 1) * 128
            u = t % 2
            if t >= 2:
                P.add("dve", lambda e: e.memset(mid, 0.0), writes=["sm"])
                P.add("dve", lambda e: e.tensor_scalar(out=wcols, in0=pw2, scalar1=W0, scalar2=2.0, op0=ALU.mult, op1=ALU.mult), reads=["sm", "pw2"], writes=["sm"])
                P.add("dve", lambda e: e.tensor_scalar(out=nhw, in0=wcols, scalar1=-0.5, scalar2=None, op0=ALU.mult), reads=["sm"], writes=["sm"])
                for it in range(NIT):
                    P.add("dve", lambda e: e.tensor_scalar(out=maskb[:, 0:St], in0=acc[:, 0:St], scalar1=mid, scalar2=None, op0=ALU.is_ge, op1=ALU.add, accum_out=cnt),
                          reads=["acc", "sm"], writes=["maskb", "sm"])
                    P.add("dve", lambda e, it=it: e.tensor_scalar(out=tmp, in0=cnt, scalar1=256.0, scalar2=wcols[:, it:it + 1], op0=ALU.is_ge, op1=ALU.mult), reads=["sm"], writes=["sm"], nosame=NOSAME)
                    P.add("dve", lambda e, it=it: e.scalar_tensor_tensor(out=mid, in0=tmp, scalar=nhw[:, it:it + 1], in1=mid, op0=ALU.add, op1=ALU.add), reads=["sm"], writes=["sm"], nosame=NOSAME)
                P.add("dve", lambda e: e.tensor_scalar(out=lo, in0=W0, scalar1=-(2.0 ** -NIT), scalar2=mid, op0=ALU.mult, op1=ALU.add), reads=["sm"], writes=["sm"])
                P.add("dve", lambda e: e.tensor_scalar(out=maskb[:, 0:St], in0=acc[:, 0:St], scalar1=lo, scalar2=None, op0=ALU.is_ge), reads=["acc", "sm"], writes=["maskb"])
            else:
                P.add("dve", lambda e: e.tensor_scalar(out=maskb[:, 0:St], in0=acc[:, 0:St], scalar1=-1e29, scalar2=None, op0=ALU.is_ge), reads=["acc"], writes=["maskb"])
            transpose_to(maskT[u], maskb, t + 1, "maskb", "maskT%d" % u, evac="dve")

        ring = [5, 6, 7]

        def stAtt(t):
            u = t % 2
            qTu = qT[u]

            def logits(kb):
                pv = kb % 2
                for hg in range(4):
                    lb = ring[(kb * 4 + hg) % 3]
                    P.add("pe", lambda e, lb=lb, hg=hg: e.matmul(bank(lb), lhsT=kT[:, kb * 128:(kb + 1) * 128],
                                                               rhs=qTu[:, hg * 4:(hg + 1) * 4, :].rearrange("p h t -> p (h t)"), start=True, stop=True),
                          reads=["kT", "qT%d" % u], writes=["pb%d" % lb])
                    P.add("act", lambda e, lb=lb, hg=hg: e.activation(out=probs[pv][:, hg * 512:(hg + 1) * 512], in_=bank(lb), func=AF.Exp, scale=float(ATTN_SCALE)),
                          reads=["pb%d" % lb], writes=["probs%d_%d" % (pv, hg)])
                    P.add("pool", lambda e, hg=hg: e.tensor_tensor(out=probs[pv][:, hg * 512:(hg + 1) * 512].rearrange("p (h t) -> p h t", h=4),
                                                                 in0=probs[pv][:, hg * 512:(hg + 1) * 512].rearrange("p (h t) -> p h t", h=4),
                                                                 in1=bc(maskT[u][:, kb, :], 4), op=ALU.mult),
                          reads=["probs%d_%d" % (pv, hg), "maskT%d" % u], writes=["probs%d_%d" % (pv, hg)])

            def pvmm(kb):
                pv = kb % 2
                for hg in range(4):
                    P.add("pe", lambda e, hg=hg: e.matmul(bank(hg), lhsT=vtok[:, kb, :], rhs=probs[pv][:, hg * 512:(hg + 1) * 512],
                                                        start=(kb == 0), stop=(kb == t)), reads=["vtok", "probs%d_%d" % (pv, hg)], writes=["pb%d" % hg])
                    P.add("pe", lambda e, hg=hg: e.matmul(bank(4)[32 * hg:32 * (hg + 1), :], lhsT=ones_b[:, 0:32], rhs=probs[pv][:, hg * 512:(hg + 1) * 512],
                                                        start=(kb == 0), stop=(kb == t), tile_position=(0, 32 * hg)), reads=["ones_b", "probs%d_%d" % (pv, hg)], writes=["pb4"])

            logits(0)
            for kb in range(t + 1):
                if kb + 1 <= t:
                    logits(kb + 1)
                pvmm(kb)

        def stFe(t):
            P.add("act", lambda e: e.activation(out=den_sb, in_=bank(4), func=AF.Copy), reads=["pb4"], writes=["den_sb"])
            for hp in range(2):
                P.add("act", lambda e, hp=hp: e.activation(out=oS[:, hp * 1024:(hp + 1) * 1024], in_=bank(2 * hp, 2), func=AF.Copy),
                      reads=["pb%d" % (2 * hp), "pb%d" % (2 * hp + 1)], writes=["oS%d" % hp])

        def stFm(t):
            r0 = t * 128
            P.add("dve", lambda e: e.reciprocal(out=den_sb, in_=den_sb), reads=["den_sb"], writes=["den_sb"])
            for hg in range(4):
                lb, lbr = palloc(1)
                P.add("pe", lambda e, lb=lb, hg=hg: e.matmul(bank(lb), lhsT=sel[:, hg, :], rhs=den_sb, start=True, stop=True), reads=["sel", "den_sb"], writes=lbr)
                P.add("dve", lambda e, lb=lb, hg=hg: e.tensor_tensor(out=ogTt[:, hg * 4:(hg + 1) * 4, :].rearrange("p c t -> p (c t)"), in0=bank(lb), in1=oS[:, hg * 512:(hg + 1) * 512], op=ALU.mult),
                      reads=lbr + ["oS%d" % (hg // 2)], writes=["ogTt"])
            P.add("poolq", lambda e: e.dma_start(out=ogTs[:, r0:r0 + 128].rearrange("(c f) t -> f c t", f=128), in_=ogTt),
                  reads=["ogTt"], writes=["ogTs"], chan="ogTst")

        if os.environ.get("DSA_NOPIPE"):
            for t in range(NT):
                stP(t)
                stI(t)
                stT(t)
                stAtt(t)
                stFe(t)
                stFm(t)
            return
        VAR = os.environ.get("DSA_VAR", "b")
        stP(0)
        stI(0)
        stT(0)
        for t in range(NT):
            if t + 1 < NT:
                stP(t + 1)
                stI(t + 1)
                if VAR == "a":
                    stT(t + 1)
            stAtt(t)
            stFe(t)
            if t + 1 < NT and VAR != "a":
                stT(t + 1)
            stFm(t)

    cur = x_in
    for li in layers:
        kind = li % 3
        j = li // 3
        dst = out if li == layers[-1] else xs
        if kind == 0:
            gla_a(j, cur)
            phase_b(li, cur, dst, gla_w_out[j], False)
        elif kind == 2:
            conv_a(cur)
            phase_b(li, cur, dst, conv_w_out[0], True)
        else:
            dsa_a(cur)
            phase_b(li, cur, dst, dsa_w_out[0], True, zgate_w=dsa_w_in[0])
        cur = dst
    for u in range(2):
        P.final_waits.append("xo%d" % u)
    P.emit(st)
    st.close()
    return nc


def _consts():
    j = np.arange(128)[:, None]
    i = np.arange(128)[None, :]
    return {
        "c_ident": np.eye(128, dtype=np.float32),
        "c_umat": np.where(j <= i, -1.0 / 16.0, 0.0).astype(np.float32),
        "c_lmat": np.where(j > i, -1.0 / 16.0, 0.0).astype(np.float32),
        "c_tri": np.where(j <= i, 1.0, 0.0).astype(np.float32),
        "c_rope": _rope_consts(),
        "c_neg": np.where(i <= j, 0.0, -1e30).astype(np.float32),
        "c_pw2": np.tile((2.0 ** -(np.arange(18) + 1.0))[None, :], (128, 1)).astype(np.float32),
        "c_sel": _sel_consts(),
    }


def _rope_consts():
    c = np.zeros((32, 8), np.float32)
    pp = np.arange(32)
    c[:, 0] = (np.float32(ROPE_THETA) ** (-(np.arange(0, 32, 2, dtype=np.float32)) / np.float32(32)))[pp % 16] / (2 * math.pi)
    c[:, 1] = np.where(pp < 16, -1.0, 1.0) * 2 * math.pi
    c[:, 3] = (np.float32(ROPE_THETA) ** (-(np.arange(0, 16, 2, dtype=np.float32)) / np.float32(16)))[pp % 8] / (2 * math.pi)
    c[:, 4] = np.where(pp % 16 < 8, -1.0, 1.0) * 2 * math.pi
    return c


def _sel_consts():
    s = np.zeros((128, 4, 128), np.float32)
    for g in range(4):
        s[32 * g, g, :] = 1.0
    return s.reshape(128, 512)


def _rot_perm():
    idx = []
    for h in range(16):
        idx += list(range(h * 128 + 16, h * 128 + 32)) + list(range(h * 128, h * 128 + 16))
    idx += list(range(2048 + 16, 2048 + 32)) + list(range(2048, 2048 + 16))
    for h in range(8):
        b = 4352 + h * 64
        idx += list(range(b + 8, b + 16)) + list(range(b, b + 8))
    idx += list(range(4864 + 8, 4864 + 16)) + list(range(4864, 4864 + 8))
    return np.asarray(idx)


WEIGHT_KEYS = ["gla_w_in", "gla_w_a2", "gla_b_a", "gla_gn_g", "gla_w_out", "dsa_w_in", "dsa_w_out",
               "conv_w_in", "conv_dw_w", "conv_dw_b", "conv_ln_g", "conv_ln_b", "conv_w_out",
               "ln_g", "ln_b", "ple_w", "ple_gate_w"]


def run(inputs, NT=32, layers=(0, 1, 2, 3), cores=None):
    B = inputs["x"].shape[0]
    cores = list(range(B)) if cores is None else cores
    L = NT * 128
    nc = build_program(NT, list(layers))
    consts = _consts()
    in_maps = []
    for b in cores:
        m = {"x": np.ascontiguousarray(inputs["x"][b, :L]),
             "p": np.ascontiguousarray(inputs["p"][:, b, :L]),
             "pos": np.ascontiguousarray(inputs["positions"][b, :L]).astype(np.int32)}
        for k in WEIGHT_KEYS:
            m[k] = np.ascontiguousarray(inputs[k])
        m["dsa_w_rot"] = np.ascontiguousarray(inputs["dsa_w_in"][0][:, _rot_perm()])
        m.update(consts)
        in_maps.append(m)
    res = run_bass_kernel_spmd(nc, in_maps, core_ids=list(range(len(cores))))
    return np.stack([r["out"] for r in res.results], axis=0)


def kernel(**inputs):
    inputs = {k: np.asarray(v) for k, v in inputs.items()}
    return run(inputs).astype(np.float32)
```
